# Optimizing a Trainium2 kernel written in Bass

```python
import math, functools
import jax, jax.numpy as jnp
from jax import lax
import numpy as np

D_MODEL = 2048
BATCH = 8
SEQ = 2048
DEPTH = 1
DEC_BATCH = 128
DEC_SEQ = 4
PAST_LEN = 8192
PAGE_SIZE = 128

N_HEADS = 16
N_KV_HEADS = 4
GROUP = N_HEADS // N_KV_HEADS
HEAD_DIM = 64
WINDOW = 128
ROT_DIM = HEAD_DIM // 4
ROPE_THETA = 500000.0
SWA_Q = N_HEADS * HEAD_DIM
SWA_KV = N_KV_HEADS * HEAD_DIM
GLA_HEADS = 4
GLA_DK_TOTAL = D_MODEL // 2
GLA_DV_TOTAL = D_MODEL
GLA_DK = GLA_DK_TOTAL // GLA_HEADS
GLA_DV = GLA_DV_TOTAL // GLA_HEADS
GLA_GATE_RANK = 16
GLA_GATE_NORM = 16.0
GLA_CHUNK = 16
D_FF = 4 * D_MODEL
EPS = 1e-6
SPLITS = (SWA_Q, SWA_KV, SWA_KV, GLA_DK_TOTAL, GLA_DK_TOTAL, GLA_DV_TOTAL, GLA_DV_TOTAL,
          GLA_GATE_RANK, D_MODEL, D_MODEL)
D_IN = sum(SPLITS)

kernel_name = "hybrid_swa_sink_gla_gated_merge_step"


def rms_norm(x, w):
    xf = x.astype(jnp.float32)
    y = xf * lax.rsqrt(jnp.mean(xf * xf, axis=-1, keepdims=True) + EPS)
    return (y * w.astype(jnp.float32)).astype(x.dtype)


def partial_rope(x, pos):
    half = ROT_DIM // 2
    inv = ROPE_THETA ** (-jnp.arange(half, dtype=jnp.float32) * 2.0 / ROT_DIM)
    ang = pos.astype(jnp.float32)[:, None] * inv[None, :]
    cos = jnp.cos(ang)[:, None, :]
    sin = jnp.sin(ang)[:, None, :]
    xf = x.astype(jnp.float32)
    x1 = xf[..., :half]
    x2 = xf[..., half:ROT_DIM]
    out = jnp.concatenate([x1 * cos - x2 * sin, x2 * cos + x1 * sin, xf[..., ROT_DIM:]], axis=-1)
    return out.astype(x.dtype)


def sink_attend(q, k, v, mask, sink):
    s = jnp.einsum('...qkgd,...skd->...kgqs', q, k, preferred_element_type=jnp.float32) * (HEAD_DIM ** -0.5)
    s = jnp.where(mask, s, -jnp.inf)
    sk = sink.astype(jnp.float32)[:, :, None, None]
    m = jnp.maximum(jnp.max(s, axis=-1, keepdims=True), sk)
    p = jnp.exp(s - m)
    denom = jnp.sum(p, axis=-1, keepdims=True) + jnp.exp(sk - m)
    return jnp.einsum('...kgqs,...skd->...qkgd', p / denom, v.astype(jnp.float32))


def swa_prompt(q, k, v, sink):
    B, T = q.shape[:2]
    nb = T // WINDOW
    qb = q.reshape(B, nb, WINDOW, N_KV_HEADS, GROUP, HEAD_DIM)
    kb = k.reshape(B, nb, WINDOW, N_KV_HEADS, HEAD_DIM)
    vb = v.reshape(B, nb, WINDOW, N_KV_HEADS, HEAD_DIM)
    prev = lambda a: jnp.concatenate([jnp.zeros_like(a[:, :1]), a[:, :-1]], axis=1)
    kc = jnp.concatenate([prev(kb), kb], axis=2)
    vc = jnp.concatenate([prev(vb), vb], axis=2)
    qpos = jnp.arange(T).reshape(nb, WINDOW)
    kpos = jnp.concatenate([qpos - WINDOW, qpos], axis=1)
    diff = qpos[:, :, None] - kpos[:, None, :]
    mask = (diff >= 0) & (diff < WINDOW) & (kpos[:, None, :] >= 0)
    o = sink_attend(qb, kc, vc, mask[None, :, None, None], sink)
    w_keep = min(WINDOW, T)
    return o.reshape(B, T, SWA_Q), (k[:, T - w_keep:], v[:, T - w_keep:])


def swa_sample(ck, cv, q, k, v, sink):
    Bd, S = q.shape[:2]
    Wc = ck.shape[1]
    kall = jnp.concatenate([ck, k.astype(ck.dtype)], axis=1)
    vall = jnp.concatenate([cv, v.astype(cv.dtype)], axis=1)
    qpos = PAST_LEN + jnp.arange(S)
    kpos = PAST_LEN - Wc + jnp.arange(Wc + S)
    diff = qpos[:, None] - kpos[None, :]
    mask = (diff >= 0) & (diff < WINDOW)
    o = sink_attend(q.reshape(Bd, S, N_KV_HEADS, GROUP, HEAD_DIM), kall, vall, mask, sink)
    return o.reshape(Bd, S, SWA_Q), (kall[:, S:], vall[:, S:])


def gla_recurrent(q, k, v, log_a, S0):
    B, T = q.shape[:2]
    L = math.gcd(T, GLA_CHUNK)
    N = T // L
    r = lambda a: a.reshape(B, N, L, *a.shape[2:])
    q, k, v, log_a = r(q), r(k), r(v), r(log_a)
    b = jnp.cumsum(log_a, axis=2)
    b_last = b[:, :, -1:]
    q_i = q * jnp.exp(b)
    k_i = k * jnp.exp(-b)
    k_d = k * jnp.exp(b_last - b)
    causal = jnp.tril(jnp.ones((L, L), dtype=bool))
    A = jnp.where(causal, jnp.einsum('bnlhk,bnmhk->bnhlm', q_i, k_i), 0.0)
    o_intra = jnp.einsum('bnhlm,bnmhv->bnlhv', A, v)
    decay = jnp.exp(b_last[:, :, 0])

    def step(S, xs):
        qc, kc, vc, dc = xs
        o = jnp.einsum('blhk,bhkv->blhv', qc, S)
        S = dc[..., None] * S + jnp.einsum('blhk,blhv->bhkv', kc, vc)
        return S, o

    xs = tuple(jnp.moveaxis(a, 1, 0) for a in (q_i, k_d, v, decay))
    S_fin, o_inter = lax.scan(step, S0.astype(jnp.float32), xs)
    o = o_intra + jnp.moveaxis(o_inter, 0, 1)
    return o.reshape(B, T, GLA_HEADS, GLA_DV), S_fin


def decoder_layer(x, pos, swa_fn, S0, norm1, w_in, w_a2, b_a, sink, gla_norm,
                  p_swa, p_gla, w_o, norm2, w_up, w_down):
    B, T, _ = x.shape
    f32 = jnp.float32
    h = rms_norm(x, norm1)
    z = h @ w_in
    qs, ks, vs, qg, kg, vg, rg, ag, gs, gg = jnp.split(z, np.cumsum(SPLITS)[:-1].tolist(), axis=-1)
    q = partial_rope(qs.reshape(B, T, N_HEADS, HEAD_DIM), pos)
    k = partial_rope(ks.reshape(B, T, N_KV_HEADS, HEAD_DIM), pos)
    v = vs.reshape(B, T, N_KV_HEADS, HEAD_DIM)
    o_swa, swa_state = swa_fn(q, k, v, sink.reshape(N_KV_HEADS, GROUP))
    log_a = jax.nn.log_sigmoid((ag @ w_a2 + b_a).astype(f32)) / GLA_GATE_NORM
    hd = lambda a, d: a.reshape(B, T, GLA_HEADS, d).astype(f32)
    o_g, S_new = gla_recurrent(hd(qg, GLA_DK) * (GLA_DK ** -0.5), hd(kg, GLA_DK), hd(vg, GLA_DV),
                               hd(log_a, GLA_DK), S0)
    o_g = o_g * lax.rsqrt(jnp.mean(o_g * o_g, axis=-1, keepdims=True) + EPS) * gla_norm.astype(f32)
    o_g = (o_g.reshape(B, T, GLA_DV_TOTAL) * jax.nn.silu(rg.astype(f32))).astype(x.dtype)
    y = jax.nn.sigmoid(gs) * (o_swa.astype(x.dtype) @ p_swa) + jax.nn.sigmoid(gg) * (o_g @ p_gla)
    x = x + y @ w_o
    h2 = rms_norm(x, norm2)
    x = x + jnp.square(jax.nn.relu(h2 @ w_up)) @ w_down
    return x, swa_state, S_new


def setup_inputs(seed: int = 0) -> dict:
    key = jax.random.key(seed)
    ks = jax.random.split(key, 20)
    nrm = lambda k, shape, scale: jax.random.normal(k, shape, jnp.float32) * scale
    w_buf = min(WINDOW, PAST_LEN)
    return {
        "x_prompt": nrm(ks[0], (BATCH, SEQ, D_MODEL), 1.0),
        "x_sample": nrm(ks[1], (DEC_BATCH, DEC_SEQ, D_MODEL), 1.0),
        "cache_swa_k": nrm(ks[2], (DEPTH, DEC_BATCH, w_buf, N_KV_HEADS, HEAD_DIM), 1.0),
        "cache_swa_v": nrm(ks[3], (DEPTH, DEC_BATCH, w_buf, N_KV_HEADS, HEAD_DIM), 1.0),
        "state_gla": nrm(ks[4], (DEPTH, DEC_BATCH, GLA_HEADS, GLA_DK, GLA_DV), 1.0),
        "norm1": 1.0 + nrm(ks[5], (DEPTH, D_MODEL), 0.02),
        "w_in": nrm(ks[6], (DEPTH, D_MODEL, D_IN), D_MODEL ** -0.5),
        "w_a2": nrm(ks[7], (DEPTH, GLA_GATE_RANK, GLA_DK_TOTAL), GLA_GATE_RANK ** -0.5),
        "b_a": nrm(ks[8], (DEPTH, GLA_DK_TOTAL), 0.1),
        "sink": nrm(ks[9], (DEPTH, N_HEADS), 0.5),
        "gla_norm": 1.0 + nrm(ks[10], (DEPTH, GLA_DV), 0.02),
        "p_swa": nrm(ks[11], (DEPTH, SWA_Q, D_MODEL), SWA_Q ** -0.5),
        "p_gla": nrm(ks[12], (DEPTH, GLA_DV_TOTAL, D_MODEL), GLA_DV_TOTAL ** -0.5),
        "w_o": nrm(ks[13], (DEPTH, D_MODEL, D_MODEL), D_MODEL ** -0.5),
        "norm2": 1.0 + nrm(ks[14], (DEPTH, D_MODEL), 0.02),
        "w_up": nrm(ks[15], (DEPTH, D_MODEL, D_FF), D_MODEL ** -0.5),
        "w_down": nrm(ks[16], (DEPTH, D_FF, D_MODEL), D_FF ** -0.5),
        "final_norm": 1.0 + nrm(ks[17], (D_MODEL,), 0.02),
    }


def reference(x_prompt, x_sample, cache_swa_k, cache_swa_v, state_gla, norm1, w_in, w_a2, b_a,
              sink, gla_norm, p_swa, p_gla, w_o, norm2, w_up, w_down, final_norm):
    xp, xs = x_prompt, x_sample
    pos_p = jnp.arange(xp.shape[1])
    pos_s = PAST_LEN + jnp.arange(xs.shape[1])
    S0_p = jnp.zeros((xp.shape[0], GLA_HEADS, GLA_DK, GLA_DV), jnp.float32)
    kp, vp, sp, ksn, vsn, ssn = [], [], [], [], [], []
    for l in range(DEPTH):
        lw = (norm1[l], w_in[l], w_a2[l], b_a[l], sink[l], gla_norm[l],
              p_swa[l], p_gla[l], w_o[l], norm2[l], w_up[l], w_down[l])
        xp, (k1, v1), S1 = decoder_layer(xp, pos_p, swa_prompt, S0_p, *lw)
        xs, (k2, v2), S2 = decoder_layer(
            xs, pos_s, functools.partial(swa_sample, cache_swa_k[l], cache_swa_v[l]), state_gla[l], *lw)
        kp.append(k1.astype(cache_swa_k.dtype)); vp.append(v1.astype(cache_swa_v.dtype))
        sp.append(S1.astype(state_gla.dtype))
        ksn.append(k2); vsn.append(v2); ssn.append(S2.astype(state_gla.dtype))
    y_prompt = rms_norm(xp, final_norm)
    y_sample = rms_norm(xs, final_norm)
    return (y_prompt, y_sample, jnp.stack(kp), jnp.stack(vp), jnp.stack(sp),
            jnp.stack(ksn), jnp.stack(vsn), jnp.stack(ssn))
```

```python
import numpy as np
import concourse.bass as bass
import concourse.mybir as mybir
from concourse.bass_utils import run_bass_kernel_spmd

F32 = mybir.dt.float32
BF16 = mybir.dt.bfloat16
ALU = mybir.AluOpType
ACT = mybir.ActivationFunctionType
AX = mybir.AxisListType

ENGS = ("pe", "act", "dve", "pool", "sp")


class Buf:
    def __init__(self, name, t, off=0, nbytes=0):
        self.name = name
        self.t = t
        self.off = off
        self.nbytes = nbytes
        self.inherit = []
        self.tokens = set()

    def tok(self, *key):
        k = (self.name,) + key
        self.tokens.add(k)
        return k

    def __getitem__(self, key):
        return self.t[key]


class _Rec:
    def __init__(self):
        self.call = None

    def __getattr__(self, name):
        def f(*a, **k):
            self.call = (name, a, k)
        return f


class Prog:
    RING = {"sp": 8, "pool": 8, "act": 4}

    def __init__(self, nc):
        self.nc = nc
        self.ops = {e: [] for e in ENGS}
        self.last_w = {}
        self.readers = {}
        self.bufs = {}
        self.ndma = {e: 0 for e in self.RING}
        self.live = []
        self.dead = []
        self.sb_lo = None
        self.psum_names = set()
        self.nop = 0

    def sb_init(self, lo, hi):
        self.sb_lo, self.sb_hi = (lo + 63) // 64 * 64, hi

    def sb_alloc(self, name, shape, dtype, off):
        nb = int(np.prod(shape[1:])) * mybir.dt.size(dtype)
        assert off % 32 == 0, (name, off)
        assert self.sb_lo + off + nb <= self.sb_hi, (name, off, nb, self.sb_hi - self.sb_lo)
        for (o, e, b) in self.live:
            assert off + nb <= o or off >= e, f"{name} overlaps live {b.name}"
        t = self.nc.alloc_sbuf_tensor_at(name, list(shape), dtype, offset=self.sb_lo + off)
        b = Buf(name, t, off, nb)
        inh = []
        for (o, e, d) in self.dead:
            if not (off + nb <= o or off >= e):
                for tk in d.tokens:
                    if tk in self.last_w:
                        inh.append(self.last_w[tk])
                    inh.extend(self.readers.get(tk, []))
                inh.extend(d.inherit)
        b.inherit = list(set(inh))
        self.live.append((off, off + nb, b))
        self.bufs[name] = b
        return b

    def sb_free(self, *bufs):
        for b in bufs:
            ent = [x for x in self.live if x[2] is b]
            assert ent, b.name
            self.live.remove(ent[0])
            self.dead.append(ent[0])

    def psum(self, name, shape, dtype=F32):
        t = self.nc.alloc_psum_tensor(name, list(shape), dtype)
        b = Buf(name, t)
        self.bufs[name] = b
        self.psum_names.add(name)
        return b

    def _deps(self, eng, r, w):
        deps = set()
        for tk in r:
            self._touch(tk)
            if tk in self.last_w:
                deps.add(self.last_w[tk] + ("raw",))
        for tk in w:
            self._touch(tk)
            port = tk == ("PSUMPORT",)
            if tk in self.last_w:
                deps.add(self.last_w[tk] + ("port" if port else "waw",))
            for rd in self.readers.get(tk, []):
                deps.add(rd + ("war",))
        return deps

    def _touch(self, tk):
        if tk not in self.last_w and tk not in self.readers:
            b = self.bufs.get(tk[0])
            if b is not None and b.inherit:
                self.readers[tk] = list(b.inherit)

    def op(self, eng, fn, r=(), w=(), dma=False):
        r = [x for x in r if x is not None]
        w = [x for x in w if x is not None]
        if eng in ("act", "dve") and any(tk[0] in self.psum_names for tk in r):
            w = w + [("PSUMPORT",)]
        deps = self._deps(eng, r, w)
        idx = len(self.ops[eng])
        ref = (eng, idx)
        dmaj = None
        if dma:
            dmaj = self.ndma[eng]
            self.ndma[eng] += 1
        rec = _Rec()
        fn(rec)
        assert rec.call is not None
        self.ops[eng].append(dict(fn=rec.call, deps=deps, dma=dmaj, sig=False))
        for tk in w:
            self.last_w[tk] = ref
            self.readers[tk] = []
        for tk in r:
            self.readers.setdefault(tk, []).append(ref)
        return ref

    def pe(self, fn, r=(), w=()):
        return self.op("pe", fn, r, w)

    def act(self, fn, r=(), w=()):
        return self.op("act", fn, r, w)

    def dve(self, fn, r=(), w=()):
        return self.op("dve", fn, r, w)

    def pool(self, fn, r=(), w=()):
        return self.op("pool", fn, r, w)

    def dma(self, out, in_, r=(), w=(), q="sp", **kw):
        return self.op(q, lambda e: e.dma_start(out=out, in_=in_, **kw), r, w, dma=True)

    def _needs_wait(self, eng, idx, dep):
        deng, didx, kind = dep
        dop = self.ops[deng][didx]
        if dop["dma"] is not None:
            return True
        if deng != eng:
            return True
        if eng == "pe":
            return False
        if kind in ("war", "port"):
            return False
        return (idx - didx) <= 3

    def emit(self):
        nc = self.nc
        sem_eng = {e: nc.alloc_semaphore(f"s_{e}") for e in ("pe", "act", "dve", "pool")}
        ring = {q: [nc.alloc_semaphore(f"r_{q}{i}") for i in range(n)] for q, n in self.RING.items()}
        for eng in ENGS:
            for idx, o in enumerate(self.ops[eng]):
                keep = set()
                for dep in o["deps"]:
                    if self._needs_wait(eng, idx, dep):
                        keep.add(dep[:2])
                        d = self.ops[dep[0]][dep[1]]
                        if d["dma"] is None:
                            d["sig"] = True
                o["waits"] = keep
        for eng in ENGS:
            cnt = 0
            for o in self.ops[eng]:
                if o["dma"] is not None:
                    n = self.RING[eng]
                    j = o["dma"]
                    o["done"] = (ring[eng][j % n], 16 * (j // n + 1))
                    o["pre"] = (ring[eng][j % n], 16 * (j // n)) if j >= n else None
                elif o["sig"]:
                    cnt += 1
                    o["done"] = (sem_eng[eng], cnt)
        engobj = {"pe": "tensor", "act": "scalar", "dve": "vector", "pool": "gpsimd", "sp": "sync"}
        self.final_waits = {}

        def run(eng):
            def body(e):
                waited = {}
                ops = self.ops[eng]
                for o in ops:
                    ws = {}
                    for (deng, didx) in o["waits"]:
                        s, v = self.ops[deng][didx]["done"]
                        ws[s] = max(ws.get(s, 0), v)
                    if o["dma"] is not None and o["pre"] is not None:
                        s, v = o["pre"]
                        ws[s] = max(ws.get(s, 0), v)
                    for s, v in ws.items():
                        if waited.get(s, 0) < v:
                            e.wait_ge(s, v)
                            waited[s] = v
                    ins = getattr(e, o["fn"][0])(*o["fn"][1], **o["fn"][2])
                    if o["dma"] is not None:
                        s, v = o["done"]
                        ins.then_inc(s, 16)
                    elif o["sig"]:
                        ins.then_inc(sem_eng[eng], 1)
                if eng in self.RING:
                    last = {}
                    for o in ops:
                        if o["dma"] is not None:
                            s, v = o["done"]
                            last[s] = max(last.get(s, 0), v)
                    for s, v in last.items():
                        if waited.get(s, 0) < v:
                            e.wait_ge(s, v)
            return body

        with nc.Block() as block:
            for eng in ENGS:
                if not self.ops[eng]:
                    continue
                getattr(block, engobj[eng])(run(eng))

    def stats(self):
        return {e: len(self.ops[e]) for e in ENGS}

D = 2048
DIN = 11792
DFF = 8192
PAST = 8192
QS, KS, VS, QG, KG, VG, RG, AG, GS, GG = 0, 1024, 1280, 1536, 2560, 3584, 5632, 7680, 7696, 9744
EPS = 1e-6
NEG = -30000.0
NS = 3


class Group:
    cnt = [0]

    def __init__(s, P, off):
        s.P, s.off, s.bufs = P, (off + 63) // 64 * 64, []

    def a(s, name, shape, dt):
        Group.cnt[0] += 1
        b = s.P.sb_alloc(f"{name}_{Group.cnt[0]}", shape, dt, s.off)
        s.off += (b.nbytes + 63) // 64 * 64
        s.bufs.append(b)
        return b

    def free(s):
        for b in s.bufs:
            s.P.sb_free(b)
        s.bufs = []


def build_program(dbg=False):
    import os
    STOP = os.environ.get('KSTOP', '')
    BLOCKS = [int(c) for c in os.environ.get('KBLOCKS', '01234')]
    KSUB = int(os.environ.get('KSUB', '0'))
    ck_n = [0]

    class StopBuild(Exception):
        pass

    def ckpt(tag=''):
        ck_n[0] += 1
        if KSUB and ck_n[0] == KSUB:
            print('STOP at ckpt', ck_n[0], tag)
            raise StopBuild()
    nc = bass.Bass("TRN2", target_bir_lowering=False)
    P = Prog(nc)
    P.sb_init(nc.sbuf_base, nc.sbuf_top)
    P.marks = []

    def mark(lab):
        P.marks.append((lab, sum(1 for o in P.ops['pe'] if o['fn'][0] == 'matmul')))
    al = Group(P, 0)

    def din(name, shape, dt=F32):
        return nc.dram_tensor(name, list(shape), dt, kind="ExternalInput").ap()

    def dout(name, shape):
        return nc.dram_tensor(name, list(shape), F32, kind="ExternalOutput").ap()

    xp = din("xp", [2048, D]); xs = din("xs", [64, D])
    ck = din("ck", [16, 128, 256]); cv = din("cv", [16, 128, 256])
    st = din("st", [16, 4, 256, 512])
    w_in = din("w_in", [D, DIN]); w_a2 = din("w_a2", [16, 1024]); b_a = din("b_a", [1, 1024])
    sinkl = din("sinkl", [128, 8]); gnorm = din("gnorm", [1, 512])
    p_swa = din("p_swa", [1024, D]); p_gla = din("p_gla", [D, D]); w_o = din("w_o", [D, D])
    n1 = din("n1", [128, 16]); n2 = din("n2", [128, 16]); fnw = din("fnw", [1, D])
    w_up = din("w_up", [D, DFF]); w_down = din("w_down", [DFF, D])
    c_ident = din("c_ident", [128, 128], BF16)
    c_mprev = din("c_mprev", [128, 512], BF16); c_mdiag = din("c_mdiag", [128, 512], BF16)
    c_msc = din("c_msc", [128, 256], BF16); c_msn = din("c_msn", [128, 256], BF16)
    c_tri = din("c_tri", [2, 3, 128, 128])
    c_ropep = din("c_ropep", [128, 2, 16, 8]); c_ropes = din("c_ropes", [128, 2, 8])
    c_cm = din("c_cm", [128, 16, 64], BF16); c_rm = din("c_rm", [128, 16])
    c_ones = din("c_ones", [128, 2, 128], BF16)

    yp = dout("yp", [2048, D]); ys = dout("ys", [64, D])
    kpo = dout("kpo", [128, 256]); vpo = dout("vpo", [128, 256]); spo = dout("spo", [4, 256, 512])
    kso = dout("kso", [16, 128, 256]); vso = dout("vso", [16, 128, 256]); sso = dout("sso", [16, 4, 256, 512])

    pools = {
        "mm": [P.psum(f"mm{i}", [128, 512], F32) for i in range(4)],
        "tp": [P.psum(f"tp{i}", [128, 1024], BF16) for i in range(2)],
        "ax": [P.psum(f"ax{i}", [128, 512], F32) for i in range(2)],
    }
    pcnt = {k: 0 for k in pools}

    def nxt(pool):
        b = pools[pool][pcnt[pool] % len(pools[pool])]
        pcnt[pool] += 1
        return b

    ident = al.a("ident", [128, 128], BF16)
    mprev = al.a("mprev", [128, 512], BF16); mdiag = al.a("mdiag", [128, 512], BF16)
    msc = al.a("msc", [128, 256], BF16); msn = al.a("msn", [128, 256], BF16)
    tri = al.a("tri", [128, 6, 128], F32)
    ropep = al.a("ropep", [128, 2, 16, 8], F32); ropes = al.a("ropes", [128, 2, 8], F32)
    cm = al.a("cm", [128, 16, 64], BF16); rm = al.a("rm", [128, 16], F32)
    ones = al.a("ones", [128, 2, 128], BF16)
    esink = al.a("esink", [128, 8], F32)
    gnb = al.a("gnb", [128, 512], F32)
    n1t = al.a("n1t", [128, 16], F32); n2t = al.a("n2t", [128, 16], F32)
    wa2 = al.a("wa2", [32, 1024], BF16)
    ssq = al.a("ssq", [128, 4], F32); rstd = al.a("rstd", [128, 4], F32)
    S = al.a("S", [128, 8, 512], F32)
    hT = al.a("hT", [128, 16, 512], BF16)
    wslots = [al.a(f"ws{i}", [128, 16, 512], BF16) for i in range(NS)]
    kT = [al.a("kT", [64, 4, 128], BF16) for _ in range(2)]
    vAB = [al.a("vAB", [128, 4, 2, 128], BF16) for _ in range(2)]
    kTn, vnAB = kT[0], vAB[0]
    B0 = al.off
    print('B0', B0, 'arena', P.sb_hi - P.sb_lo)
    T0 = Group(P, B0)
    wa2f = T0.a("wa2f", [32, 1024], F32)
    CT = lambda b: [b.tok()]

    for dst, src in ((ident, c_ident), (mprev, c_mprev), (mdiag, c_mdiag), (msc, c_msc), (msn, c_msn),
                     (ropep, c_ropep), (ropes, c_ropes), (cm, c_cm), (rm, c_rm), (ones, c_ones),
                     (n1t, n1), (n2t, n2)):
        P.dma(dst[:], src, w=CT(dst))
    P.dma(tri[:], c_tri.rearrange("a b p l -> p (a b) l"), w=CT(tri))
    P.dma(esink[:], sinkl, w=CT(esink))
    P.act(lambda e: e.activation(out=esink[:], in_=esink[:], func=ACT.Exp), r=CT(esink), w=CT(esink))
    P.dma(gnb[:], gnorm.broadcast_to([128, 512]), w=CT(gnb))
    P.dve(lambda e: e.memset(wa2f[:], 0.0), w=CT(wa2f))
    P.dma(wa2f[0:16, :], w_a2, w=CT(wa2f))
    P.dma(wa2f[16:17, :], b_a, w=CT(wa2f))
    P.dve(lambda e: e.tensor_copy(out=wa2[:], in_=wa2f[:]), r=CT(wa2f), w=CT(wa2))
    T0.free()
    for v_ in vAB:
        P.dve(lambda e, v_=v_: e.memset(v_[:], 0.0), w=CT(v_))

    wuse = [0]
    ring = list(wslots)
    wblk = [0]
    wvis = {}
    wscr = {}

    def wload(parts):
        ws = ring[wuse[0] % len(ring)]
        wuse[0] += 1
        k = wblk[0]
        wblk[0] += 1
        visit = wvis.get(k, 0)
        wvis[k] = visit + 1
        store_visit = 0 if k % 3 == 0 else 1
        if k not in wscr:
            wscr[k] = nc.dram_tensor(f"wscr{k}", [128, 16 * 512], BF16).ap()
        scr = wscr[k]
        nk = parts[0][1]
        ctot = sum(p_[3] for p_ in parts)
        assert all(p_[1] == nk for p_ in parts) and parts[0][2] == 0
        toks = [ws.tok(j) for j in range(len(parts))]
        flat = ws[:].rearrange("p k c -> p (k c)")[:, 0:nk * ctot]
        view = flat.rearrange("p (k c) -> p k c", c=ctot)
        if visit <= store_visit:
            for j, (src, nk_, c0, C) in enumerate(parts):
                P.dma(view[:, :, c0:c0 + C], src.rearrange("(kc p) c -> p kc c", p=128), w=[ws.tok(j)], q="pool")
            if visit == store_visit:
                P.dma(scr[:, 0:nk * ctot], flat, r=toks, w=[("wscr", k)])
        else:
            P.dma(flat, scr[:, 0:nk * ctot], r=[("wscr", k)], w=toks, q="pool")
        return view, toks

    def rms_rstd(src_ap, src_tok, PT, col, scale, junk, junk_tok):
        P.act(lambda e: e.activation(out=junk, in_=src_ap, func=ACT.Square, accum_out=ssq[:PT, col:col + 1]),
              r=src_tok, w=[ssq.tok(col)] + junk_tok)
        P.act(lambda e: e.activation(out=rstd[:PT, col:col + 1], in_=ssq[:PT, col:col + 1], func=ACT.Ln, scale=scale, bias=EPS),
              r=[ssq.tok(col)], w=[rstd.tok(col)])
        P.act(lambda e: e.activation(out=rstd[:PT, col:col + 1], in_=rstd[:PT, col:col + 1], func=ACT.Exp, scale=-0.5),
              r=[rstd.tok(col)], w=[rstd.tok(col)])

    def norm_T(x_ap, x_tok, PT, nwt, dstT, c0, hb, junk):
        rms_rstd(x_ap, x_tok, PT, 0, 1.0 / D, junk[:PT, :], CT(junk))
        P.dve(lambda e: e.tensor_scalar(out=hb[:PT, :], in0=x_ap, scalar1=rstd[:PT, 0:1], scalar2=None, op0=ALU.mult),
              r=x_tok + [rstd.tok(0)], w=CT(hb))
        for half in range(2):
            bank = nxt("tp")
            for j in range(8):
                kc = half * 8 + j
                P.pe(lambda e, kc=kc, j=j, bank=bank: e.transpose(out=bank[:, j * PT:(j + 1) * PT], in_=hb[:PT, kc * 128:(kc + 1) * 128], identity=ident[:PT, :PT]),
                     r=CT(hb) + CT(ident), w=CT(bank))
            for j in range(8):
                kc = half * 8 + j
                if half == 0:
                    P.act(lambda e, kc=kc, j=j, bank=bank: e.activation(out=dstT[:, kc, c0:c0 + PT], in_=bank[:, j * PT:(j + 1) * PT], func=ACT.Copy, scale=nwt[:, kc:kc + 1]),
                          r=CT(bank) + CT(nwt), w=[dstT.tok(kc)])
                else:
                    P.dve(lambda e, kc=kc, j=j, bank=bank: e.tensor_scalar(out=dstT[:, kc, c0:c0 + PT], in0=bank[:, j * PT:(j + 1) * PT], scalar1=nwt[:, kc:kc + 1], scalar2=None, op0=ALU.mult),
                          r=CT(bank) + CT(nwt), w=[dstT.tok(kc)])

    hT_all = [hT.tok(kc) for kc in range(16)]

    def proj_tok(ws, wt, nk, wc0, wC, actT, act_tok, tiles, sink):
        for t, (c0, PT) in enumerate(tiles):
            bank = nxt("mm")
            for kc in range(nk):
                P.pe(lambda e, kc=kc, bank=bank, c0=c0, PT=PT: e.matmul(bank[:PT, 0:wC], lhsT=actT[:, kc, c0:c0 + PT], rhs=ws[:, kc, wc0:wc0 + wC], start=(kc == 0), stop=(kc == nk - 1)),
                     r=wt + act_tok, w=CT(bank))
            sink(t, bank)

    def proj_feat(ws, wt, nk, wc0, nchunk, actT, act_tok, NT, sink, pool="mm", M=128):
        for cc in range(nchunk):
            bank = nxt(pool)
            for kc in range(nk):
                P.pe(lambda e, kc=kc, bank=bank, cc=cc: e.matmul(bank[:M, 0:NT], lhsT=ws[:, kc, wc0 + cc * 128:wc0 + cc * 128 + M], rhs=actT[:, kc, 0:NT], start=(kc == 0), stop=(kc == nk - 1)),
                     r=wt + act_tok, w=CT(bank))
            sink(cc, bank)

    def phase1(bi):
        sample = bi == 4
        NT = 64 if sample else 512
        tiles = [(0, 64)] if sample else [(i * 128, 128) for i in range(4)]
        xsrc = xs if sample else xp[bi * 512:(bi + 1) * 512, :]
        r1off = B0 + (48 * NT + 63) // 64 * 64
        T = Group(P, r1off)
        xt = [T.a("xt", [128, D], F32) for _ in range(2)]
        hb = T.a("hb", [128, D], BF16)
        junk = T.a("junk", [128, D], F32)
        for t, (c0, PT) in enumerate(tiles):
            x_ = xt[t % 2]
            P.dma(x_[:PT, :], xsrc[c0:c0 + PT, :], w=CT(x_), q="pool")
            norm_T(x_[:PT, :], CT(x_), PT, n1t, hT, c0, hb, junk)
        T.free()

    def block(bi):
        sample = bi == 4
        wblk[0] = 0
        NT = 64 if sample else 512
        tiles = [(0, 64)] if sample else [(i * 128, 128) for i in range(4)]
        PTm = tiles[0][1]
        xsrc = xs if sample else xp[bi * 512:(bi + 1) * 512, :]
        ydst = ys if sample else yp[bi * 512:(bi + 1) * 512, :]
        ci = 1 if sample else 0
        trin, uu, caus = tri[:PTm, 3 * ci, :PTm], tri[:PTm, 3 * ci + 1, :PTm], tri[:PTm, 3 * ci + 2, :PTm]
        R1 = Group(P, B0)
        oTs = R1.a("oTs", [128, 8, NT], BF16)
        ogT = R1.a("ogT", [128, 16, NT], BF16)

        mark(f'b{bi} ph0 end')
        if bi == BLOCKS[0]:
            phase1(bi)

        mark(f'b{bi} ph1 end')
        T = Group(P, R1.off)
        z32 = [T.a("z32", [128, 512], F32) for _ in range(2)]
        rt = T.a("rt", [128, 4, 64], F32)
        qr = T.a("qr", [128, len(tiles), 1024], BF16)
        kr = T.a("kr", [128, len(tiles), 256], BF16)
        qT = [T.a("qT", [64, 16, 128], BF16) for _ in range(2)]
        PTb = [T.a("PTb", [128, 2, 512], BF16) for _ in range(2)]
        dtmp = T.a("dtmp", [128, 2, 128], F32)
        vst = T.a("vst", [128, len(tiles), 256], BF16)
        zc = [0]

        def rope_inplace(zb, PT, nh, t):
            v = zb[:PT, 0:nh * 64].rearrange("p (h d) -> p h d", d=64)
            x1, x2 = v[:, :, 0:8], v[:, :, 8:16]
            if sample:
                cos, sin = ropes[:PT, 0, :], ropes[:PT, 1, :]
            else:
                cos, sin = ropep[:PT, 0, bi * 4 + t, :], ropep[:PT, 1, bi * 4 + t, :]
            cb = cos.unsqueeze(1).to_broadcast([PT, nh, 8]); sb = sin.unsqueeze(1).to_broadcast([PT, nh, 8])
            T = [rt[:PT, i, 0:nh * 8].rearrange("p (h d) -> p h d", d=8) for i in range(4)]
            zt_, rtt = CT(zb), CT(rt)
            rr = zt_ + [ropes.tok() if sample else ropep.tok()]
            P.dve(lambda e: e.tensor_tensor(out=T[0], in0=x1, in1=cb, op=ALU.mult), r=rr, w=rtt)
            P.dve(lambda e: e.tensor_tensor(out=T[1], in0=x2, in1=sb, op=ALU.mult), r=rr, w=rtt)
            P.dve(lambda e: e.tensor_tensor(out=T[2], in0=x2, in1=cb, op=ALU.mult), r=rr, w=rtt)
            P.dve(lambda e: e.tensor_tensor(out=T[3], in0=x1, in1=sb, op=ALU.mult), r=rr, w=rtt)
            P.dve(lambda e: e.tensor_tensor(out=x1, in0=T[0], in1=T[1], op=ALU.subtract), r=rtt, w=zt_)
            P.dve(lambda e: e.tensor_tensor(out=x2, in0=T[2], in1=T[3], op=ALU.add), r=rtt, w=zt_)

        def q_sink(slot):
            def f(t, bank):
                PT = tiles[t][1]
                zb = z32[zc[0] % 2]; zc[0] += 1
                P.act(lambda e: e.activation(out=zb[:PT, :], in_=bank[:PT, :], func=ACT.Copy), r=CT(bank), w=CT(zb))
                rope_inplace(zb, PT, 8, t)
                P.dve(lambda e: e.tensor_copy(out=qr[:PT, t, slot * 512:(slot + 1) * 512], in_=zb[:PT, :]), r=CT(zb), w=[qr.tok(t, slot)])
            return f

        def kv_sink(t, bank):
            PT = tiles[t][1]
            gt = bi * 4 + t
            zb = z32[zc[0] % 2]; zc[0] += 1
            P.act(lambda e: e.activation(out=zb[:PT, :], in_=bank[:PT, :], func=ACT.Copy), r=CT(bank), w=CT(zb))
            rope_inplace(zb, PT, 4, t)
            P.dve(lambda e: e.tensor_copy(out=kr[:PT, t, :], in_=zb[:PT, 0:256]), r=CT(zb), w=[kr.tok(t)])
            P.dve(lambda e: e.tensor_copy(out=vst[:PT, t, :], in_=zb[:PT, 256:512]), r=CT(zb), w=[vst.tok(t)])
            if sample:
                for b in range(16):
                    P.dma(kso[b, 124:128, :], zb[4 * b:4 * b + 4, 0:256], r=CT(zb))
                    P.dma(vso[b, 124:128, :], zb[4 * b:4 * b + 4, 256:512], r=CT(zb))
            elif gt == 15:
                P.dma(kpo, zb[:, 0:256], r=CT(zb))
                P.dma(vpo, zb[:, 256:512], r=CT(zb))

        for slot in range(2):
            ws, wt = wload([(w_in[:, QS + slot * 512:QS + (slot + 1) * 512], 16, 0, 512)])
            proj_tok(ws, wt, 16, 0, 512, hT, hT_all, tiles, q_sink(slot))
        ckpt('q proj')
        ws, wt = wload([(w_in[:, KS:KS + 512], 16, 0, 512)])
        proj_tok(ws, wt, 16, 0, 512, hT, hT_all, tiles, kv_sink)
        ckpt('kv proj')

        if sample:
            ckb = T.a("ckb", [128, 16, 256], BF16); cvb = T.a("cvb", [128, 16, 256], BF16)
            kTc = T.a("kTc", [64, 16, 4, 128], BF16)
            vcAB = T.a("vcAB", [128, 64, 2, 128], BF16)
            P.dma(ckb[:], ck.rearrange("b c f -> c b f"), w=CT(ckb), q="pool")
            P.dma(cvb[:], cv.rearrange("b c f -> c b f"), w=CT(cvb), q="pool")
            for b in range(16):
                P.dma(kso[b, 0:124, :], ck[b, 4:128, :])
                P.dma(vso[b, 0:124, :], cv[b, 4:128, :])
            P.dve(lambda e: e.memset(vcAB[:], 0.0), w=CT(vcAB))
            cvv = cvb[:].rearrange("p b (g d) -> p (b g) d", d=64)
            P.dve(lambda e: e.tensor_copy(out=vcAB[:, :, 0, 0:64], in_=cvv), r=CT(cvb), w=CT(vcAB))
            P.dve(lambda e: e.tensor_copy(out=vcAB[:, :, 1, 64:128], in_=cvv), r=CT(cvb), w=CT(vcAB))
            for b2 in range(8):
                bank = nxt("tp")
                for j in range(8):
                    b, g = (b2 * 8 + j) // 4, (b2 * 8 + j) % 4
                    P.pe(lambda e, b=b, g=g, j=j, bank=bank: e.transpose(out=bank[:64, j * 128:(j + 1) * 128], in_=ckb[:, b, g * 64:(g + 1) * 64], identity=ident[:]),
                         r=CT(ckb) + CT(ident), w=CT(bank))
                src = bank[:64, :].rearrange("p (b g c) -> p b g c", g=4, c=128)
                if b2 % 2 == 0:
                    P.act(lambda e, b2=b2, src=src: e.activation(out=kTc[:, 2 * b2:2 * b2 + 2, :, :], in_=src, func=ACT.Copy), r=CT(bank), w=CT(kTc))
                else:
                    P.dve(lambda e, b2=b2, src=src: e.tensor_copy(out=kTc[:, 2 * b2:2 * b2 + 2, :, :], in_=src), r=CT(bank), w=CT(kTc))

        for t, (c0, PT) in enumerate(tiles):
            gt = bi * 4 + t
            qT_ = qT[t % 2]
            kT_ = kT[gt % 2] if not sample else kTn
            vs_ = vAB[gt % 2] if not sample else vnAB
            vv = vst[:PT, t, :].rearrange("p (g d) -> p g d", d=64)
            P.dve(lambda e: e.tensor_copy(out=vs_[:PT, :, 0, 0:64], in_=vv), r=[vst.tok(t)], w=CT(vs_))
            P.dve(lambda e: e.tensor_copy(out=vs_[:PT, :, 1, 64:128], in_=vv), r=[vst.tok(t)], w=CT(vs_))
            for half in range(2):
                bank = nxt("tp")
                for j in range(8):
                    hh = half * 8 + j
                    P.pe(lambda e, hh=hh, j=j, bank=bank: e.transpose(out=bank[:64, j * PT:(j + 1) * PT], in_=qr[:PT, t, hh * 64:(hh + 1) * 64], identity=ident[:PT, :PT]),
                         r=[qr.tok(t, hh // 8)] + CT(ident), w=CT(bank))
                src = bank[:64, 0:8 * PT].rearrange("p (h q) -> p h q", q=PT)
                if half == 0:
                    P.act(lambda e, src=src, half=half: e.activation(out=qT_[:, 0:8, :PT], in_=src, func=ACT.Copy), r=CT(bank), w=CT(qT_))
                else:
                    P.dve(lambda e, src=src, half=half: e.tensor_copy(out=qT_[:, 8:16, :PT], in_=src), r=CT(bank), w=CT(qT_))
            bank = nxt("tp")
            for g in range(4):
                P.pe(lambda e, g=g, bank=bank: e.transpose(out=bank[:64, g * PT:(g + 1) * PT], in_=kr[:PT, t, g * 64:(g + 1) * 64], identity=ident[:PT, :PT]),
                     r=[kr.tok(t)] + CT(ident), w=CT(bank))
            P.dve(lambda e, bank=bank: e.tensor_copy(out=kT_[:, :, :PT], in_=bank[:64, 0:4 * PT].rearrange("p (g s) -> p g s", s=PT)), r=CT(bank), w=CT(kT_))

            mmb2, axb2 = pools["mm"], pools["ax"]

            def S_(g):
                PT_ = PTb[g % 2]
                rq = qT_[:, 4 * g:4 * g + 4, :PT]
                sb = (mmb2[0], mmb2[1]) if g % 2 == 0 else (mmb2[2], mmb2[3])
                if not sample:
                    kinds = ([(0, kT[(gt - 1) % 2], mprev)] if gt > 0 else []) + [(1, kT_, mdiag)]
                    for (kd_, ksrc, msk) in kinds:
                        bank = sb[kd_]
                        P.pe(lambda e: e.matmul(bank[:, :], lhsT=ident[:], rhs=msk[:], start=True, stop=False), r=CT(ident) + CT(msk), w=CT(bank))
                        P.pe(lambda e: e.matmul(bank[:, :], lhsT=ksrc[:, g, :], rhs=rq, start=False, stop=True), r=CT(ksrc) + CT(qT_), w=CT(bank))
                        P.act(lambda e: e.activation(out=PT_[:, kd_, :], in_=bank[:, :], func=ACT.Exp, scale=0.125), r=CT(bank), w=[PT_.tok(kd_)])
                else:
                    bank = sb[0]
                    P.pe(lambda e: e.matmul(bank[:, 0:256], lhsT=ident[:], rhs=msc[:], start=True, stop=False), r=CT(ident) + CT(msc), w=CT(bank))
                    for b in range(16):
                        P.pe(lambda e, b=b: e.matmul(bank[:, 0:256].rearrange("p (h q) -> p h q", q=64)[:, :, 4 * b:4 * b + 4], lhsT=kTc[:, b, g, :], rhs=qT_[:, 4 * g:4 * g + 4, 4 * b:4 * b + 4], start=False, stop=(b == 15)),
                             r=CT(kTc) + CT(qT_), w=CT(bank))
                    P.act(lambda e: e.activation(out=PT_[:, 0, 0:256], in_=bank[:, 0:256], func=ACT.Exp, scale=0.125), r=CT(bank), w=[PT_.tok(0)])
                    bank2 = sb[1]
                    P.pe(lambda e: e.matmul(bank2[:64, 0:256], lhsT=ident[:64, :64], rhs=msn[:64, :], start=True, stop=False), r=CT(ident) + CT(msn), w=CT(bank2))
                    P.pe(lambda e: e.matmul(bank2[:64, 0:256], lhsT=kT_[:, g, :64], rhs=rq, start=False, stop=True), r=CT(kT_) + CT(qT_), w=CT(bank2))
                    P.act(lambda e: e.activation(out=PT_[:64, 1, 0:256], in_=bank2[:64, 0:256], func=ACT.Exp, scale=0.125), r=CT(bank2), w=[PT_.tok(1)])

            def PV_(g):
                PT_ = PTb[g % 2]
                pv = axb2[g % 2]
                pvv = pv[:, :].rearrange("p (r o q) -> p r o q", r=2, o=2)
                ptoks = [PT_.tok(0), PT_.tok(1)]
                for pr in range(2):
                    for od in range(2):
                        mms = []
                        if not sample:
                            for (kd_, vsrc) in ([(0, vAB[(gt - 1) % 2])] if gt > 0 else []) + [(1, vAB[gt % 2])]:
                                for ab in range(2):
                                    lh = vsrc[:, g, ab, :] if od == 0 else ones[:, ab, :]
                                    mms.append((pvv[:, pr, od, :], lh, PT_[:, kd_, (2 * pr + ab) * 128:(2 * pr + ab + 1) * 128], CT(vsrc)))
                        else:
                            for ab in range(2):
                                lh = vnAB[:64, g, ab, :] if od == 0 else ones[:64, ab, :]
                                mms.append((pvv[:, pr, od, 0:64], lh, PT_[:64, 1, (2 * pr + ab) * 64:(2 * pr + ab + 1) * 64], CT(vnAB)))
                            for b in range(16):
                                for ab in range(2):
                                    lh = vcAB[:, b * 4 + g, ab, :] if od == 0 else ones[:, ab, :]
                                    c_ = (2 * pr + ab) * 64 + 4 * b
                                    mms.append((pvv[:, pr, od, 4 * b:4 * b + 4], lh, PT_[:, 0, c_:c_ + 4], CT(vcAB)))
                        for i, (o_, l_, r_, tk) in enumerate(mms):
                            P.pe(lambda e, o_=o_, l_=l_, r_=r_, i=i, n=len(mms): e.matmul(o_, lhsT=l_, rhs=r_, start=(i == 0), stop=(i == n - 1)),
                                 r=ptoks + tk + CT(ones), w=CT(pv))

            def EV_(g):
                pv = axb2[g % 2]
                pvv = pv[:, :].rearrange("p (r o q) -> p r o q", r=2, o=2)
                for pr in range(2):
                    P.dve(lambda e, pr=pr: e.tensor_scalar(out=dtmp[:, pr, :PT], in0=pvv[:, pr, 1, :PT], scalar1=esink[:, 2 * g + pr:2 * g + pr + 1], scalar2=None, op0=ALU.add),
                          r=CT(pv) + CT(esink), w=CT(dtmp))
                P.dve(lambda e: e.reciprocal(out=dtmp[:, :, :PT], in_=dtmp[:, :, :PT]), r=CT(dtmp), w=CT(dtmp))
                P.dve(lambda e: e.tensor_tensor(out=oTs[:, 2 * g:2 * g + 2, c0:c0 + PT], in0=pvv[:, :, 0, :PT], in1=dtmp[:, :, :PT], op=ALU.mult),
                      r=CT(pv) + CT(dtmp), w=[oTs.tok(g)])

            S_(0)
            for g in range(4):
                if g < 3:
                    S_(g + 1)
                PV_(g)
                EV_(g)
        T.free()

        mark(f'b{bi} ph2 end')
        T = Group(P, R1.off)
        agT = T.a("agT", [32, NT], BF16)
        qgT = [T.a("qgT", [128, 2, NT], BF16) for _ in range(2)]; kgT = [T.a("kgT", [128, 2, NT], BF16) for _ in range(2)]
        kg = [T.a("kg", [128, len(tiles), 256], BF16) for _ in range(2)]
        vg = [T.a("vg", [128, len(tiles), 512], BF16) for _ in range(2)]
        srg = [T.a("srg", [128, len(tiles), 512], BF16) for _ in range(2)]
        e1 = T.a("e1", [128, 256], F32); sp = T.a("sp", [128, 256], F32)
        epos = [T.a("epos", [128, 2, 128], F32) for _ in range(2)]; eneg = T.a("eneg", [128, 2, 128], F32); ed = T.a("ed", [128, 256], F32)
        qiT = [T.a("qiT", [128, 2, 128], BF16) for _ in range(2)]; kiT = [T.a("kiT", [128, 2, 128], BF16) for _ in range(2)]; kd = [T.a("kd", [128, 256], BF16) for _ in range(2)]
        ATm = [T.a("ATm", [128, 128], BF16) for _ in range(2)]
        Sbf = [T.a("Sbf", [128, 2, 512], BF16) for _ in range(2)]
        ogf = T.a("ogf", [128, 512], F32); ogb = T.a("ogb", [128, 512], BF16)
        gjunk = T.a("gjunk", [128, 512], F32)
        if sample:
            qm = T.a("qm", [128, 2, 64], BF16); kdm = T.a("kdm", [64, 256], BF16)
            S0 = [T.a("S0", [128, 2, 512], F32) for _ in range(3)]
            S0b = [T.a("S0b", [128, 2, 512], BF16) for _ in range(2)]
            Sn = [T.a("Sn", [128, 2, 512], F32) for _ in range(2)]

        ws, wt = wload([(w_in[:, AG:AG + 16], 16, 0, 16)])
        P.dve(lambda e: e.memset(agT[:], 1.0), w=CT(agT))

        def ag_sink(cc, bank):
            P.act(lambda e: e.activation(out=agT[0:16, :], in_=bank[0:16, 0:NT], func=ACT.Copy), r=CT(bank), w=CT(agT))
        proj_feat(ws, wt, 16, 0, 1, hT, hT_all, NT, ag_sink, pool="ax", M=16)

        mmb = pools["mm"]
        pj_cnt = [0]

        def make_proj_items(h):
            hp = h % 2
            items = []
            ws1, wt1 = wload([(w_in[:, QG + h * 256:QG + (h + 1) * 256], 16, 0, 256), (w_in[:, KG + h * 256:KG + (h + 1) * 256], 16, 256, 256)])
            ws2, wt2 = wload([(w_in[:, VG + h * 512:VG + (h + 1) * 512], 16, 0, 512)])
            ws3, wt3 = wload([(w_in[:, RG + h * 512:RG + (h + 1) * 512], 16, 0, 512)])

            def pjbank():
                bk = mmb[2 + pj_cnt[0] % 2]
                pj_cnt[0] += 1
                return bk

            def feat_item(cc):
                def f():
                    bank = pjbank()
                    for kc in range(16):
                        P.pe(lambda e, kc=kc: e.matmul(bank[:, 0:NT], lhsT=ws1[:, kc, cc * 128:(cc + 1) * 128], rhs=hT[:, kc, 0:NT], start=(kc == 0), stop=(kc == 15)), r=wt1 + hT_all, w=CT(bank))
                    dst = qgT[hp] if cc < 2 else kgT[hp]
                    P.act(lambda e: e.activation(out=dst[:, cc % 2, :], in_=bank[:, 0:NT], func=ACT.Copy), r=CT(bank), w=CT(dst))
                return f

            def tok_item(which, t):
                def f():
                    c0, PT = tiles[t]
                    bank = pjbank()
                    ws_, wt_, wc0, wC = {"k": (ws1, wt1, 256, 256), "v": (ws2, wt2, 0, 512), "r": (ws3, wt3, 0, 512)}[which]
                    for kc in range(16):
                        P.pe(lambda e, kc=kc: e.matmul(bank[:PT, 0:wC], lhsT=hT[:, kc, c0:c0 + PT], rhs=ws_[:, kc, wc0:wc0 + wC], start=(kc == 0), stop=(kc == 15)), r=wt_ + hT_all, w=CT(bank))
                    if which == "k":
                        P.act(lambda e: e.activation(out=kg[hp][:PT, t, :], in_=bank[:PT, 0:256], func=ACT.Copy), r=CT(bank), w=CT(kg[hp]))
                    elif which == "v":
                        P.act(lambda e: e.activation(out=vg[hp][:PT, t, :], in_=bank[:PT, :], func=ACT.Copy), r=CT(bank), w=CT(vg[hp]))
                    else:
                        P.act(lambda e: e.activation(out=srg[hp][:PT, t, :], in_=bank[:PT, :], func=ACT.Silu), r=CT(bank), w=CT(srg[hp]))
                return f
            for cc in range(4):
                items.append(feat_item(cc))
            for which in ("k", "v", "r"):
                for t in range(len(tiles)):
                    items.append(tok_item(which, t))
            return items

        pending = make_proj_items(0)
        for h in range(4):
            hp = h % 2
            for it in pending:
                it()
            pending = make_proj_items(h + 1) if h < 3 else []

            def PJ():
                if pending:
                    pending.pop(0)()
            nt_ = len(tiles)
            st_ = {}

            def A1(i):
                c0, PT = tiles[i]
                p = i % 2
                bx = nxt("ax")
                P.pe(lambda e: e.matmul(bx[:PT, 0:256], lhsT=agT[:, c0:c0 + PT], rhs=wa2[:, h * 256:(h + 1) * 256], start=True, stop=True), r=CT(agT) + CT(wa2), w=CT(bx))
                P.act(lambda e: e.activation(out=e1[:PT, :], in_=bx[:PT, 0:256], func=ACT.Exp, scale=-1.0), r=CT(bx), w=CT(e1))
                P.act(lambda e: e.activation(out=sp[:PT, :], in_=e1[:PT, :], func=ACT.Ln, bias=1.0), r=CT(e1), w=CT(sp))

            def A2(i):
                c0, PT = tiles[i]
                p = i % 2
                bb = nxt("ax")
                for kc in range(2):
                    P.pe(lambda e, kc=kc: e.matmul(bb[:, kc * PT:(kc + 1) * PT], lhsT=sp[:PT, kc * 128:(kc + 1) * 128], rhs=trin, start=True, stop=True), r=CT(sp) + CT(tri), w=CT(bb))
                P.pe(lambda e: e.matmul(bb[:PT, 256:512], lhsT=uu, rhs=sp[:PT, :], start=True, stop=True), r=CT(sp) + CT(tri), w=CT(bb))
                bbT = bb[:, 0:2 * PT].rearrange("p (k l) -> p k l", l=PT)
                ep = epos[p]
                P.act(lambda e: e.activation(out=ep[:, :, :PT], in_=bbT, func=ACT.Exp), r=CT(bb), w=CT(ep))
                P.act(lambda e: e.activation(out=eneg[:, :, :PT], in_=bbT, func=ACT.Exp, scale=-1.0), r=CT(bb), w=CT(eneg))
                P.act(lambda e: e.activation(out=ed[:PT, :], in_=bb[:PT, 256:512], func=ACT.Exp), r=CT(bb), w=CT(ed))

            def A3(i):
                c0, PT = tiles[i]
                p = i % 2
                ep, qi, ki, kd_, AT_ = epos[p], qiT[p], kiT[p], kd[p], ATm[p]
                P.dve(lambda e: e.scalar_tensor_tensor(out=qi[:, :, :PT], in0=qgT[hp][:, :, c0:c0 + PT], scalar=1.0 / 16.0, in1=ep[:, :, :PT], op0=ALU.mult, op1=ALU.mult), r=CT(qgT[hp]) + CT(ep), w=CT(qi))
                P.dve(lambda e: e.tensor_tensor(out=ki[:, :, :PT], in0=kgT[hp][:, :, c0:c0 + PT], in1=eneg[:, :, :PT], op=ALU.mult), r=CT(kgT[hp]) + CT(eneg), w=CT(ki))
                P.dve(lambda e: e.tensor_tensor(out=kd_[:PT, :], in0=kg[hp][:PT, i, :], in1=ed[:PT, :], op=ALU.mult), r=CT(kg[hp]) + CT(ed), w=CT(kd_))
                ba = nxt("ax")
                for kc in range(2):
                    P.pe(lambda e, kc=kc: e.matmul(ba[:PT, 0:PT], lhsT=ki[:, kc, :PT], rhs=qi[:, kc, :PT], start=(kc == 0), stop=(kc == 1)), r=CT(ki) + CT(qi), w=CT(ba))
                P.dve(lambda e: e.tensor_tensor(out=AT_[:PT, :PT], in0=ba[:PT, 0:PT], in1=caus, op=ALU.mult), r=CT(ba) + CT(tri), w=CT(AT_))

            def B1(t):
                c0, PT = tiles[t]
                gt = bi * 4 + t
                p = t % 2
                ep, qi, kd_, AT_ = epos[p], qiT[p], kd[p], ATm[p]
                po = mmb[0]
                st_[t] = po
                if not sample:
                    has_state = gt > 0
                    Sb = Sbf[t % 2]
                    if has_state:
                        P.act(lambda e: e.activation(out=Sb[:, 0, :], in_=S[:, 2 * h, :], func=ACT.Copy), r=[S.tok(2 * h)], w=[Sb.tok(0)])
                        P.dve(lambda e: e.tensor_copy(out=Sb[:, 1, :], in_=S[:, 2 * h + 1, :]), r=[S.tok(2 * h + 1)], w=[Sb.tok(1)])
                    P.pe(lambda e: e.matmul(po[:PT, :], lhsT=AT_[:PT, :PT], rhs=vg[hp][:PT, t, :], start=True, stop=not has_state), r=CT(AT_) + CT(vg[hp]), w=CT(po))
                    if has_state:
                        for kc in range(2):
                            P.pe(lambda e, kc=kc: e.matmul(po[:PT, :], lhsT=qi[:, kc, :PT], rhs=Sb[:, kc, :], start=False, stop=(kc == 1)), r=CT(qi) + [Sb.tok(kc)], w=CT(po))
                else:
                    P.pe(lambda e: e.matmul(po[:PT, :], lhsT=AT_[:PT, :PT], rhs=vg[hp][:PT, t, :], start=True, stop=False), r=CT(AT_) + CT(vg[hp]), w=CT(po))
                    for b in range(16):
                        P.dve(lambda e, b=b: e.tensor_tensor(out=qm[:, :, :], in0=qi[:, :, :64], in1=cm[:, b, :].unsqueeze(1).to_broadcast([128, 2, 64]), op=ALU.mult), r=CT(qi) + CT(cm), w=CT(qm))
                        P.dve(lambda e, b=b: e.tensor_scalar(out=kdm[:, :], in0=kd_[:64, :], scalar1=rm[:64, b:b + 1], scalar2=None, op0=ALU.mult), r=CT(kd_) + CT(rm), w=CT(kdm))
                        s0, s0b, sn = S0[b % 3], S0b[b % 2], Sn[b % 2]
                        P.dma(s0[:], st[b, h].rearrange("(kc p) v -> p kc v", p=128), w=CT(s0), q="pool")
                        P.act(lambda e, s0=s0, s0b=s0b: e.activation(out=s0b[:, 0, :], in_=s0[:, 0, :], func=ACT.Copy), r=CT(s0), w=[s0b.tok(0)])
                        P.dve(lambda e, s0=s0, s0b=s0b: e.tensor_copy(out=s0b[:, 1, :], in_=s0[:, 1, :]), r=CT(s0), w=[s0b.tok(1)])
                        for kc in range(2):
                            i = b * 2 + kc
                            P.pe(lambda e, kc=kc, s0b=s0b, i=i: e.matmul(po[:64, :], lhsT=qm[:, kc, :], rhs=s0b[:, kc, :], start=False, stop=(i == 31)), r=CT(qm) + [s0b.tok(kc)], w=CT(po))
                            pd_ = nxt("ax")
                            P.pe(lambda e, pd_=pd_, kc=kc: e.matmul(pd_[:, :], lhsT=kdm[:, kc * 128:(kc + 1) * 128], rhs=vg[hp][:64, t, :], start=True, stop=True), r=CT(kdm) + CT(vg[hp]), w=CT(pd_))
                            P.dve(lambda e, pd_=pd_, kc=kc, s0=s0, sn=sn, b=b: e.scalar_tensor_tensor(out=sn[:, kc, :], in0=s0[:, kc, :], scalar=ep[:, kc, 4 * b + 3:4 * b + 4], in1=pd_[:, :], op0=ALU.mult, op1=ALU.add),
                                  r=CT(pd_) + CT(ep) + CT(s0), w=[sn.tok(kc)])
                        P.dma(sso[b, h].rearrange("(kc p) v -> p kc v", p=128), sn[:], r=[sn.tok(0), sn.tok(1)])

            def B2(t, kcs=(0, 1), norm=True):
                c0, PT = tiles[t]
                gt = bi * 4 + t
                p = t % 2
                ep, kd_ = epos[p], kd[p]
                po = st_[t]
                if not sample:
                    has_state = gt > 0
                    for kc in kcs:
                        pd_ = mmb[1]
                        P.pe(lambda e, pd_=pd_, kc=kc: e.matmul(pd_[:, :], lhsT=kd_[:PT, kc * 128:(kc + 1) * 128], rhs=vg[hp][:PT, t, :], start=True, stop=True), r=CT(kd_) + CT(vg[hp]), w=CT(pd_))
                        if has_state:
                            P.dve(lambda e, pd_=pd_, kc=kc: e.scalar_tensor_tensor(out=S[:, 2 * h + kc, :], in0=S[:, 2 * h + kc, :], scalar=ep[:, kc, PT - 1:PT], in1=pd_[:, :], op0=ALU.mult, op1=ALU.add),
                                  r=CT(pd_) + CT(ep) + [S.tok(2 * h + kc)], w=[S.tok(2 * h + kc)])
                        else:
                            P.dve(lambda e, pd_=pd_, kc=kc: e.tensor_copy(out=S[:, 2 * h + kc, :], in_=pd_[:, :]), r=CT(pd_), w=[S.tok(2 * h + kc)])
                        if gt == 15:
                            P.dma(spo[h, kc * 128:(kc + 1) * 128, :], S[:, 2 * h + kc, :], r=[S.tok(2 * h + kc)])
                if norm:
                    rms_rstd(po[:PT, :], CT(po), PT, 1, 1.0 / 512.0, gjunk[:PT, :], CT(gjunk))

            def B3(t):
                c0, PT = tiles[t]
                po = st_[t]
                P.act(lambda e: e.activation(out=ogf[:PT, :], in_=po[:PT, :], func=ACT.Copy, scale=rstd[:PT, 1:2]), r=CT(po) + [rstd.tok(1)], w=CT(ogf))
                P.dve(lambda e: e.tensor_tensor(out=ogf[:PT, :], in0=ogf[:PT, :], in1=gnb[:PT, :], op=ALU.mult), r=CT(ogf) + CT(gnb), w=CT(ogf))
                P.dve(lambda e: e.tensor_tensor(out=ogb[:PT, :], in0=ogf[:PT, :], in1=srg[hp][:PT, t, :], op=ALU.mult), r=CT(ogf) + CT(srg[hp]), w=CT(ogb))
                bank = nxt("tp")
                for j in range(4):
                    P.pe(lambda e, j=j: e.transpose(out=bank[:, j * PT:(j + 1) * PT], in_=ogb[:PT, j * 128:(j + 1) * 128], identity=ident[:PT, :PT]), r=CT(ogb) + CT(ident), w=CT(bank))
                P.act(lambda e: e.activation(out=ogT[:, 4 * h:4 * h + 4, c0:c0 + PT], in_=bank[:, 0:4 * PT].rearrange("p (j q) -> p j q", q=PT), func=ACT.Copy), r=CT(bank), w=[ogT.tok(h)])

            for step in range(nt_ + 1):
                i, j = step, step - 1
                if i < nt_:
                    A1(i)
                PJ()
                if j >= 0:
                    B1(j)
                if i < nt_:
                    A2(i)
                PJ()
                if j >= 0:
                    B2(j, kcs=(0,), norm=True)
                if i < nt_:
                    A3(i)
                if j >= 0:
                    B2(j, kcs=(1,), norm=False)
                PJ()
                if j >= 0:
                    B3(j)
                PJ()
        T.free()

        mark(f'b{bi} ph3 end')
        XS = None
        if sample:
            XS = Group(P, B0 + 56 * 1024)
            ring.extend(XS.a("wsx", [128, 16, 512], BF16) for _ in range(3))
        G4 = Group(P, R1.off)
        yT = G4.a("yT", [128, 16, NT], BF16)
        sg = G4.a("sg", [128, 4, 2, NT], F32)
        y1 = G4.a("y1", [128, NT], F32)
        oTs_all = [oTs.tok(g) for g in range(4)]; ogT_all = [ogT.tok(h) for h in range(4)]
        for c4 in range(4):
            for c2 in range(2):
                fc = c4 * 4 + c2 * 2
                wsG, wtG = wload([(w_in[:, GS + fc * 128:GS + fc * 128 + 256], 16, 0, 256), (w_in[:, GG + fc * 128:GG + fc * 128 + 256], 16, 256, 256)])
                for j in range(2):
                    for gi in range(2):
                        bg = nxt("ax")
                        cc = c2 * 2 + j
                        for kc in range(16):
                            P.pe(lambda e, kc=kc, bg=bg, gi=gi, wsG=wsG, j=j: e.matmul(bg[:, 0:NT], lhsT=wsG[:, kc, gi * 256 + j * 128:gi * 256 + j * 128 + 128], rhs=hT[:, kc, 0:NT], start=(kc == 0), stop=(kc == 15)),
                                 r=wtG + hT_all, w=CT(bg))
                        P.act(lambda e, bg=bg, gi=gi, cc=cc: e.activation(out=sg[:, cc, gi, :], in_=bg[:, 0:NT], func=ACT.Sigmoid), r=CT(bg), w=[sg.tok(cc, gi)])
            wsA, wtA = wload([(p_swa[:, c4 * 512:(c4 + 1) * 512], 8, 0, 512)])
            wsB, wtB = wload([(p_gla[:, c4 * 512:(c4 + 1) * 512], 16, 0, 512)])
            for cc in range(4):
                fc = c4 * 4 + cc
                pa = nxt("mm")
                for kc in range(8):
                    P.pe(lambda e, kc=kc, pa=pa, wsA=wsA, cc=cc: e.matmul(pa[:, 0:NT], lhsT=wsA[:, kc, cc * 128:(cc + 1) * 128], rhs=oTs[:, kc, :], start=(kc == 0), stop=(kc == 7)), r=wtA + oTs_all, w=CT(pa))
                pb = nxt("mm")
                for kc in range(16):
                    P.pe(lambda e, kc=kc, pb=pb, wsB=wsB, cc=cc: e.matmul(pb[:, 0:NT], lhsT=wsB[:, kc, cc * 128:(cc + 1) * 128], rhs=ogT[:, kc, :], start=(kc == 0), stop=(kc == 15)), r=wtB + ogT_all, w=CT(pb))
                P.dve(lambda e, pa=pa, cc=cc: e.tensor_tensor(out=y1[:, :], in0=pa[:, 0:NT], in1=sg[:, cc, 0, :], op=ALU.mult), r=CT(pa) + [sg.tok(cc, 0)], w=CT(y1))
                P.dve(lambda e, pb=pb, cc=cc: e.tensor_tensor(out=sg[:, cc, 1, :], in0=pb[:, 0:NT], in1=sg[:, cc, 1, :], op=ALU.mult), r=CT(pb) + [sg.tok(cc, 1)], w=[sg.tok(cc, 1)])
                P.dve(lambda e, fc=fc, cc=cc: e.tensor_tensor(out=yT[:, fc, :], in0=y1[:, :], in1=sg[:, cc, 1, :], op=ALU.add), r=CT(y1) + [sg.tok(cc, 1)], w=[yT.tok(fc)])
        yT_all = [yT.tok(fc) for fc in range(16)]
        R1.free()

        mark(f'b{bi} ph4 end')
        X = Group(P, B0 + (40 if sample else 72) * 1024)
        x2 = X.a("x2", [128, len(tiles), D], F32)
        G5 = Group(P, G4.off)
        hb = G5.a("hb2", [128, D], BF16); junk = G5.a("junk2", [128, D], F32)
        for t, (c0, PT) in enumerate(tiles):
            P.dma(x2[:PT, t, :], xsrc[c0:c0 + PT, :], w=[x2.tok(t)])
        for c4 in range(4):
            ws, wt = wload([(w_o[:, c4 * 512:(c4 + 1) * 512], 16, 0, 512)])

            def wo_sink(t, bank, c4=c4):
                PT = tiles[t][1]
                P.dve(lambda e: e.tensor_tensor(out=x2[:PT, t, c4 * 512:(c4 + 1) * 512], in0=bank[:PT, :], in1=x2[:PT, t, c4 * 512:(c4 + 1) * 512], op=ALU.add), r=CT(bank) + [x2.tok(t)], w=[x2.tok(t)])
            proj_tok(ws, wt, 16, 0, 512, yT, yT_all, tiles, wo_sink)
        for t, (c0, PT) in enumerate(tiles):
            norm_T(x2[:PT, t, :], [x2.tok(t)], PT, n2t, hT, c0, hb, junk)
        G4.free(); G5.free()

        mark(f'b{bi} ph5 end')
        G6 = Group(P, B0)
        aT = G6.a("aT", [128, 64, NT], BF16)
        rl = [G6.a("rl", [128, NT], F32) for _ in range(2)]
        for g16 in range(16):
            ws, wt = wload([(w_up[:, g16 * 512:(g16 + 1) * 512], 16, 0, 512)])

            def up_sink(cc, bank, g16=g16):
                fcc = g16 * 4 + cc
                r_ = rl[fcc % 2]
                P.act(lambda e: e.activation(out=r_[:, :], in_=bank[:, 0:NT], func=ACT.Relu), r=CT(bank), w=CT(r_))
                P.dve(lambda e: e.tensor_tensor(out=aT[:, fcc, :], in0=r_[:, :], in1=r_[:, :], op=ALU.mult), r=CT(r_), w=[aT.tok(fcc)])
            proj_feat(ws, wt, 16, 0, 4, hT, hT_all, NT, up_sink)
        for c4 in range(4):
            banks = [nxt("mm") for _ in tiles]
            for g4 in range(4):
                ws, wt = wload([(w_down[g4 * 2048:(g4 + 1) * 2048, c4 * 512:(c4 + 1) * 512], 16, 0, 512)])
                for t, (c0, PT) in enumerate(tiles):
                    for fc in range(16):
                        P.pe(lambda e, t=t, fc=fc, c0=c0, PT=PT, ws=ws, g4=g4: e.matmul(banks[t][:PT, :], lhsT=aT[:, g4 * 16 + fc, c0:c0 + PT], rhs=ws[:, fc, :], start=(g4 == 0 and fc == 0), stop=(g4 == 3 and fc == 15)),
                             r=wt + [aT.tok(g4 * 16 + fc)], w=CT(banks[t]))
            for t, (c0, PT) in enumerate(tiles):
                P.dve(lambda e, t=t, PT=PT, c4=c4: e.tensor_tensor(out=x2[:PT, t, c4 * 512:(c4 + 1) * 512], in0=banks[t][:PT, :], in1=x2[:PT, t, c4 * 512:(c4 + 1) * 512], op=ALU.add), r=CT(banks[t]) + [x2.tok(t)], w=[x2.tok(t)])
        G6.free()

        mark(f'b{bi} ph6 end')
        nxt_bi = BLOCKS[BLOCKS.index(bi) + 1] if BLOCKS.index(bi) + 1 < len(BLOCKS) else None
        if nxt_bi is not None:
            phase1(nxt_bi)
        G7 = Group(P, B0 + (24 if sample else 56) * 1024)
        fnb = G7.a("fnb", [128, D], F32)
        yo_ = G7.a("yo", [128, D], F32)
        P.dma(fnb[:], fnw.broadcast_to([128, D]), w=CT(fnb))
        for t, (c0, PT) in enumerate(tiles):
            rms_rstd(x2[:PT, t, :], [x2.tok(t)], PT, 2, 1.0 / D, yo_[:PT, :], CT(yo_))
            P.dve(lambda e, t=t, PT=PT: e.scalar_tensor_tensor(out=yo_[:PT, :], in0=x2[:PT, t, :], scalar=rstd[:PT, 2:3], in1=fnb[:PT, :], op0=ALU.mult, op1=ALU.mult), r=[x2.tok(t), rstd.tok(2)] + CT(fnb), w=CT(yo_))
            P.dma(ydst[c0:c0 + PT, :], yo_[:PT, :], r=CT(yo_))
        G7.free(); X.free()
        if XS is not None:
            del ring[NS:]
            XS.free()
        mark(f'b{bi} ph7 end')

    try:
        for bi in BLOCKS:
            if block(bi):
                break
    except StopBuild:
        pass
    P.emit()
    return nc, P


def _consts():
    import ml_dtypes
    bf = ml_dtypes.bfloat16
    c = {}
    c["c_ident"] = np.eye(128, dtype=np.float32).astype(bf)
    s_ = np.arange(128)[:, None]; q_ = np.arange(128)[None, :]
    mprev = np.where(s_ > q_, 0.0, NEG).astype(np.float32)
    mdiag = np.where(s_ <= q_, 0.0, NEG).astype(np.float32)
    c["c_mprev"] = np.tile(mprev, (1, 4)).astype(bf)
    c["c_mdiag"] = np.tile(mdiag, (1, 4)).astype(bf)
    tok = np.arange(64)
    msc = np.where(np.arange(128)[:, None] > (tok % 4)[None, :], 0.0, NEG).astype(np.float32)
    c["c_msc"] = np.tile(msc, (1, 4)).astype(bf)
    j_ = np.arange(64)[:, None]; i_ = tok[None, :]
    msn = np.where((j_ // 4 == i_ // 4) & (j_ % 4 <= i_ % 4), 0.0, NEG).astype(np.float32)
    msn_full = np.full((128, 256), NEG, np.float32); msn_full[:64] = np.tile(msn, (1, 4))
    c["c_msn"] = msn_full.astype(bf)
    tri = np.zeros((2, 3, 128, 128), np.float32)
    m_ = np.arange(128)[:, None]; l_ = np.arange(128)[None, :]
    tri[0, 0] = np.where(m_ <= l_, -1.0 / 16.0, 0.0); tri[0, 1] = np.where(m_ > l_, -1.0 / 16.0, 0.0); tri[0, 2] = np.where(m_ <= l_, 1.0, 0.0)
    same = (m_ // 4 == l_ // 4)
    tri[1, 0] = np.where(same & (m_ <= l_), -1.0 / 16.0, 0.0); tri[1, 1] = np.where(same & (m_ > l_), -1.0 / 16.0, 0.0); tri[1, 2] = np.where(same & (m_ <= l_), 1.0, 0.0)
    c["c_tri"] = tri
    inv = (np.float32(500000.0) ** (-np.arange(8, dtype=np.float32) * np.float32(2.0) / np.float32(16.0))).astype(np.float32)
    pos = (np.arange(16)[None, :] * 128 + np.arange(128)[:, None]).astype(np.float32)
    ang = (pos[:, :, None] * inv[None, None, :]).astype(np.float32)
    c["c_ropep"] = np.stack([np.cos(ang), np.sin(ang)], axis=1).astype(np.float32)
    poss = (PAST + (np.arange(128) % 4)).astype(np.float32)
    angs = (poss[:, None] * inv[None, :]).astype(np.float32)
    c["c_ropes"] = np.stack([np.cos(angs), np.sin(angs)], axis=1).astype(np.float32)
    cmk = (np.arange(64)[None, :] // 4 == np.arange(16)[:, None]).astype(np.float32)
    c["c_cm"] = np.broadcast_to(cmk[None], (128, 16, 64)).astype(bf)
    rmk = np.zeros((128, 16), np.float32); rmk[:64] = cmk.T
    c["c_rm"] = rmk
    ones = np.zeros((128, 2, 128), np.float32); ones[:, 0, :64] = 1.0; ones[:, 1, 64:] = 1.0
    c["c_ones"] = ones.astype(bf)
    return {k: np.ascontiguousarray(v) for k, v in c.items()}


_CACHE = {}


def kernel(x_prompt, x_sample, cache_swa_k, cache_swa_v, state_gla, norm1, w_in, w_a2, b_a,
           sink, gla_norm, p_swa, p_gla, w_o, norm2, w_up, w_down, final_norm):
    f = lambda a: np.ascontiguousarray(np.asarray(a, dtype=np.float32))
    x_prompt, x_sample = f(x_prompt), f(x_sample)
    ck, cv, stt = f(cache_swa_k)[0], f(cache_swa_v)[0], f(state_gla)[0]
    if "nc" not in _CACHE:
        _CACHE["nc"] = build_program()[0]
    nc = _CACHE["nc"]
    sk = f(sink)[0]
    sinkl = np.empty((128, 8), np.float32)
    sinkl[:64, :] = sk[0::2][None, :]; sinkl[64:, :] = sk[1::2][None, :]
    shared = dict(
        w_in=f(w_in)[0], w_a2=f(w_a2)[0], b_a=f(b_a), sinkl=sinkl, gnorm=f(gla_norm),
        p_swa=f(p_swa)[0], p_gla=f(p_gla)[0], w_o=f(w_o)[0],
        n1=np.ascontiguousarray(f(norm1)[0].reshape(16, 128).T), n2=np.ascontiguousarray(f(norm2)[0].reshape(16, 128).T),
        fnw=f(final_norm).reshape(1, D), w_up=f(w_up)[0], w_down=f(w_down)[0],
    )
    shared.update(_consts())
    in_maps = []
    for c in range(8):
        m = dict(shared)
        m["xp"] = x_prompt[c]
        m["xs"] = np.ascontiguousarray(x_sample[c * 16:(c + 1) * 16].reshape(64, D))
        m["ck"] = np.ascontiguousarray(ck[c * 16:(c + 1) * 16].reshape(16, 128, 256))
        m["cv"] = np.ascontiguousarray(cv[c * 16:(c + 1) * 16].reshape(16, 128, 256))
        m["st"] = np.ascontiguousarray(stt[c * 16:(c + 1) * 16])
        in_maps.append(m)
    res = run_bass_kernel_spmd(nc, in_maps, core_ids=list(range(8)))
    R = res.results
    y_prompt = np.stack([R[c]["yp"] for c in range(8)]).astype(np.float32)
    y_sample = np.concatenate([R[c]["ys"].reshape(16, 4, D) for c in range(8)]).astype(np.float32)
    kp = np.stack([R[c]["kpo"].reshape(128, 4, 64) for c in range(8)])[None].astype(np.float32)
    vp = np.stack([R[c]["vpo"].reshape(128, 4, 64) for c in range(8)])[None].astype(np.float32)
    sp_ = np.stack([R[c]["spo"] for c in range(8)])[None].astype(np.float32)
    ks = np.concatenate([R[c]["kso"].reshape(16, 128, 4, 64) for c in range(8)])[None].astype(np.float32)
    vs = np.concatenate([R[c]["vso"].reshape(16, 128, 4, 64) for c in range(8)])[None].astype(np.float32)
    ss = np.concatenate([R[c]["sso"] for c in range(8)])[None].astype(np.float32)
    return (y_prompt, y_sample, kp, vp, sp_, ks, vs, ss)
```

```python
import numpy as np
import concourse.bass as bass
import concourse.mybir as mybir
from concourse.bass_utils import run_bass_kernel_spmd

F32 = mybir.dt.float32
BF16 = mybir.dt.bfloat16
ALU = mybir.AluOpType
ACT = mybir.ActivationFunctionType
AX = mybir.AxisListType

ENGS = ("pe", "act", "dve", "pool", "sp")


class Buf:
    def __init__(self, name, t, off=0, nbytes=0):
        self.name = name
        self.t = t
        self.off = off
        self.nbytes = nbytes
        self.inherit = []
        self.tokens = set()

    def tok(self, *key):
        k = (self.name,) + key
        self.tokens.add(k)
        return k

    def __getitem__(self, key):
        return self.t[key]


class _Rec:
    def __init__(self):
        self.call = None

    def __getattr__(self, name):
        def f(*a, **k):
            self.call = (name, a, k)
        return f


class Prog:
    RING = {"sp": 8, "pool": 8, "act": 4}

    def __init__(self, nc):
        self.nc = nc
        self.ops = {e: [] for e in ENGS}
        self.last_w = {}
        self.readers = {}
        self.bufs = {}
        self.ndma = {e: 0 for e in self.RING}
        self.live = []
        self.dead = []
        self.sb_lo = None
        self.psum_names = set()
        self.nop = 0

    def sb_init(self, lo, hi):
        self.sb_lo, self.sb_hi = (lo + 63) // 64 * 64, hi

    def sb_alloc(self, name, shape, dtype, off):
        nb = int(np.prod(shape[1:])) * mybir.dt.size(dtype)
        assert off % 32 == 0, (name, off)
        assert self.sb_lo + off + nb <= self.sb_hi, (name, off, nb, self.sb_hi - self.sb_lo)
        for (o, e, b) in self.live:
            assert off + nb <= o or off >= e, f"{name} overlaps live {b.name}"
        t = self.nc.alloc_sbuf_tensor_at(name, list(shape), dtype, offset=self.sb_lo + off)
        b = Buf(name, t, off, nb)
        inh = []
        for (o, e, d) in self.dead:
            if not (off + nb <= o or off >= e):
                for tk in d.tokens:
                    if tk in self.last_w:
                        inh.append(self.last_w[tk])
                    inh.extend(self.readers.get(tk, []))
                inh.extend(d.inherit)
        b.inherit = list(set(inh))
        self.live.append((off, off + nb, b))
        self.bufs[name] = b
        return b

    def sb_free(self, *bufs):
        for b in bufs:
            ent = [x for x in self.live if x[2] is b]
            assert ent, b.name
            self.live.remove(ent[0])
            self.dead.append(ent[0])

    def psum(self, name, shape, dtype=F32):
        t = self.nc.alloc_psum_tensor(name, list(shape), dtype)
        b = Buf(name, t)
        self.bufs[name] = b
        self.psum_names.add(name)
        return b

    def _deps(self, eng, r, w):
        deps = set()
        for tk in r:
            self._touch(tk)
            if tk in self.last_w:
                deps.add(self.last_w[tk] + ("raw",))
        for tk in w:
            self._touch(tk)
            port = tk == ("PSUMPORT",)
            if tk in self.last_w:
                deps.add(self.last_w[tk] + ("port" if port else "waw",))
            for rd in self.readers.get(tk, []):
                deps.add(rd + ("war",))
        return deps

    def _touch(self, tk):
        if tk not in self.last_w and tk not in self.readers:
            b = self.bufs.get(tk[0])
            if b is not None and b.inherit:
                self.readers[tk] = list(b.inherit)

    def op(self, eng, fn, r=(), w=(), dma=False):
        r = [x for x in r if x is not None]
        w = [x for x in w if x is not None]
        if eng in ("act", "dve") and any(tk[0] in self.psum_names for tk in r):
            w = w + [("PSUMPORT",)]
        deps = self._deps(eng, r, w)
        idx = len(self.ops[eng])
        ref = (eng, idx)
        dmaj = None
        if dma:
            dmaj = self.ndma[eng]
            self.ndma[eng] += 1
        rec = _Rec()
        fn(rec)
        assert rec.call is not None
        self.ops[eng].append(dict(fn=rec.call, deps=deps, dma=dmaj, sig=False))
        for tk in w:
            self.last_w[tk] = ref
            self.readers[tk] = []
        for tk in r:
            self.readers.setdefault(tk, []).append(ref)
        return ref

    def pe(self, fn, r=(), w=()):
        return self.op("pe", fn, r, w)

    def act(self, fn, r=(), w=()):
        return self.op("act", fn, r, w)

    def dve(self, fn, r=(), w=()):
        return self.op("dve", fn, r, w)

    def pool(self, fn, r=(), w=()):
        return self.op("pool", fn, r, w)

    def dma(self, out, in_, r=(), w=(), q="sp", **kw):
        return self.op(q, lambda e: e.dma_start(out=out, in_=in_, **kw), r, w, dma=True)

    def _needs_wait(self, eng, idx, dep):
        deng, didx, kind = dep
        dop = self.ops[deng][didx]
        if dop["dma"] is not None:
            return True
        if deng != eng:
            return True
        if eng == "pe":
            return False
        if kind in ("war", "port"):
            return False
        return (idx - didx) <= 3

    def emit(self):
        nc = self.nc
        sem_eng = {e: nc.alloc_semaphore(f"s_{e}") for e in ("pe", "act", "dve", "pool")}
        ring = {q: [nc.alloc_semaphore(f"r_{q}{i}") for i in range(n)] for q, n in self.RING.items()}
        for eng in ENGS:
            for idx, o in enumerate(self.ops[eng]):
                keep = set()
                for dep in o["deps"]:
                    if self._needs_wait(eng, idx, dep):
                        keep.add(dep[:2])
                        d = self.ops[dep[0]][dep[1]]
                        if d["dma"] is None:
                            d["sig"] = True
                o["waits"] = keep
        for eng in ENGS:
            cnt = 0
            for o in self.ops[eng]:
                if o["dma"] is not None:
                    n = self.RING[eng]
                    j = o["dma"]
                    o["done"] = (ring[eng][j % n], 16 * (j // n + 1))
                    o["pre"] = (ring[eng][j % n], 16 * (j // n)) if j >= n else None
                elif o["sig"]:
                    cnt += 1
                    o["done"] = (sem_eng[eng], cnt)
        engobj = {"pe": "tensor", "act": "scalar", "dve": "vector", "pool": "gpsimd", "sp": "sync"}
        self.final_waits = {}

        def run(eng):
            def body(e):
                waited = {}
                ops = self.ops[eng]
                for o in ops:
                    ws = {}
                    for (deng, didx) in o["waits"]:
                        s, v = self.ops[deng][didx]["done"]
                        ws[s] = max(ws.get(s, 0), v)
                    if o["dma"] is not None and o["pre"] is not None:
                        s, v = o["pre"]
                        ws[s] = max(ws.get(s, 0), v)
                    for s, v in ws.items():
                        if waited.get(s, 0) < v:
                            e.wait_ge(s, v)
                            waited[s] = v
                    ins = getattr(e, o["fn"][0])(*o["fn"][1], **o["fn"][2])
                    if o["dma"] is not None:
                        s, v = o["done"]
                        ins.then_inc(s, 16)
                    elif o["sig"]:
                        ins.then_inc(sem_eng[eng], 1)
                if eng in self.RING:
                    last = {}
                    for o in ops:
                        if o["dma"] is not None:
                            s, v = o["done"]
                            last[s] = max(last.get(s, 0), v)
                    for s, v in last.items():
                        if waited.get(s, 0) < v:
                            e.wait_ge(s, v)
            return body

        with nc.Block() as block:
            for eng in ENGS:
                if not self.ops[eng]:
                    continue
                getattr(block, engobj[eng])(run(eng))

    def stats(self):
        return {e: len(self.ops[e]) for e in ENGS}

D = 2048
DIN = 11792
DFF = 8192
PAST = 8192
QS, KS, VS, QG, KG, VG, RG, AG, GS, GG = 0, 1024, 1280, 1536, 2560, 3584, 5632, 7680, 7696, 9744
EPS = 1e-6
NEG = -30000.0
NS = 3


class Group:
    cnt = [0]

    def __init__(s, P, off):
        s.P, s.off, s.bufs = P, (off + 63) // 64 * 64, []

    def a(s, name, shape, dt):
        Group.cnt[0] += 1
        b = s.P.sb_alloc(f"{name}_{Group.cnt[0]}", shape, dt, s.off)
        s.off += (b.nbytes + 63) // 64 * 64
        s.bufs.append(b)
        return b

    def free(s):
        for b in s.bufs:
            s.P.sb_free(b)
        s.bufs = []


def build_program(dbg=False):
    import os
    STOP = os.environ.get('KSTOP', '')
    BLOCKS = [int(c) for c in os.environ.get('KBLOCKS', '01234')]
    KSUB = int(os.environ.get('KSUB', '0'))
    ck_n = [0]

    class StopBuild(Exception):
        pass

    def ckpt(tag=''):
        ck_n[0] += 1
        if KSUB and ck_n[0] == KSUB:
            print('STOP at ckpt', ck_n[0], tag)
            raise StopBuild()
    nc = bass.Bass("TRN2", target_bir_lowering=False)
    P = Prog(nc)
    P.sb_init(nc.sbuf_base, nc.sbuf_top)
    P.marks = []

    def mark(lab):
        P.marks.append((lab, sum(1 for o in P.ops['pe'] if o['fn'][0] == 'matmul')))
    al = Group(P, 0)

    def din(name, shape, dt=F32):
        return nc.dram_tensor(name, list(shape), dt, kind="ExternalInput").ap()

    def dout(name, shape):
        return nc.dram_tensor(name, list(shape), F32, kind="ExternalOutput").ap()

    xp = din("xp", [2048, D]); xs = din("xs", [64, D])
    ck = din("ck", [16, 128, 256]); cv = din("cv", [16, 128, 256])
    st = din("st", [16, 4, 256, 512])
    w_in = din("w_in", [D, DIN]); w_a2 = din("w_a2", [16, 1024]); b_a = din("b_a", [1, 1024])
    sinkl = din("sinkl", [128, 8]); gnorm = din("gnorm", [1, 512])
    p_swa = din("p_swa", [1024, D]); p_gla = din("p_gla", [D, D]); w_o = din("w_o", [D, D])
    n1 = din("n1", [128, 16]); n2 = din("n2", [128, 16]); fnw = din("fnw", [1, D])
    w_up = din("w_up", [D, DFF]); w_down = din("w_down", [DFF, D])
    c_ident = din("c_ident", [128, 128], BF16)
    c_mprev = din("c_mprev", [128, 512], BF16); c_mdiag = din("c_mdiag", [128, 512], BF16)
    c_msc = din("c_msc", [128, 256], BF16); c_msn = din("c_msn", [128, 256], BF16)
    c_tri = din("c_tri", [2, 3, 128, 128])
    c_ropep = din("c_ropep", [128, 2, 16, 8]); c_ropes = din("c_ropes", [128, 2, 8])
    c_cm = din("c_cm", [128, 16, 64], BF16); c_rm = din("c_rm", [128, 16])
    c_ones = din("c_ones", [128, 2, 128], BF16)

    yp = dout("yp", [2048, D]); ys = dout("ys", [64, D])
    kpo = dout("kpo", [128, 256]); vpo = dout("vpo", [128, 256]); spo = dout("spo", [4, 256, 512])
    kso = dout("kso", [16, 128, 256]); vso = dout("vso", [16, 128, 256]); sso = dout("sso", [16, 4, 256, 512])

    pools = {
        "mm": [P.psum(f"mm{i}", [128, 512], F32) for i in range(4)],
        "tp": [P.psum(f"tp{i}", [128, 1024], BF16) for i in range(2)],
        "ax": [P.psum(f"ax{i}", [128, 512], F32) for i in range(2)],
    }
    pcnt = {k: 0 for k in pools}

    def nxt(pool):
        b = pools[pool][pcnt[pool] % len(pools[pool])]
        pcnt[pool] += 1
        return b

    ident = al.a("ident", [128, 128], BF16)
    mprev = al.a("mprev", [128, 512], BF16); mdiag = al.a("mdiag", [128, 512], BF16)
    msc = al.a("msc", [128, 256], BF16); msn = al.a("msn", [128, 256], BF16)
    tri = al.a("tri", [128, 6, 128], F32)
    ropep = al.a("ropep", [128, 2, 16, 8], F32); ropes = al.a("ropes", [128, 2, 8], F32)
    cm = al.a("cm", [128, 16, 64], BF16); rm = al.a("rm", [128, 16], F32)
    ones = al.a("ones", [128, 2, 128], BF16)
    esink = al.a("esink", [128, 8], F32)
    gnb = al.a("gnb", [128, 512], F32)
    n1t = al.a("n1t", [128, 16], F32); n2t = al.a("n2t", [128, 16], F32)
    wa2 = al.a("wa2", [32, 1024], BF16)
    ssq = al.a("ssq", [128, 4], F32); rstd = al.a("rstd", [128, 4], F32)
    S = al.a("S", [128, 8, 512], F32)
    hT = al.a("hT", [128, 16, 512], BF16)
    wslots = [al.a(f"ws{i}", [128, 16, 512], BF16) for i in range(NS)]
    kT = [al.a("kT", [64, 4, 128], BF16) for _ in range(2)]
    vAB = [al.a("vAB", [128, 4, 2, 128], BF16) for _ in range(2)]
    kTn, vnAB = kT[0], vAB[0]
    B0 = al.off
    print('B0', B0, 'arena', P.sb_hi - P.sb_lo)
    T0 = Group(P, B0)
    wa2f = T0.a("wa2f", [32, 1024], F32)
    CT = lambda b: [b.tok()]

    for dst, src in ((ident, c_ident), (mprev, c_mprev), (mdiag, c_mdiag), (msc, c_msc), (msn, c_msn),
                     (ropep, c_ropep), (ropes, c_ropes), (cm, c_cm), (rm, c_rm), (ones, c_ones),
                     (n1t, n1), (n2t, n2)):
        P.dma(dst[:], src, w=CT(dst))
    P.dma(tri[:], c_tri.rearrange("a b p l -> p (a b) l"), w=CT(tri))
    P.dma(esink[:], sinkl, w=CT(esink))
    P.act(lambda e: e.activation(out=esink[:], in_=esink[:], func=ACT.Exp), r=CT(esink), w=CT(esink))
    P.dma(gnb[:], gnorm.broadcast_to([128, 512]), w=CT(gnb))
    P.dve(lambda e: e.memset(wa2f[:], 0.0), w=CT(wa2f))
    P.dma(wa2f[0:16, :], w_a2, w=CT(wa2f))
    P.dma(wa2f[16:17, :], b_a, w=CT(wa2f))
    P.dve(lambda e: e.tensor_copy(out=wa2[:], in_=wa2f[:]), r=CT(wa2f), w=CT(wa2))
    T0.free()
    for v_ in vAB:
        P.dve(lambda e, v_=v_: e.memset(v_[:], 0.0), w=CT(v_))

    wuse = [0]
    ring = list(wslots)
    wblk = [0]
    wvis = {}
    wscr = {}

    def wload(parts):
        ws = ring[wuse[0] % len(ring)]
        wuse[0] += 1
        k = wblk[0]
        wblk[0] += 1
        visit = wvis.get(k, 0)
        wvis[k] = visit + 1
        store_visit = 0 if k % 3 == 0 else 1
        if k not in wscr:
            wscr[k] = nc.dram_tensor(f"wscr{k}", [128, 16 * 512], BF16).ap()
        scr = wscr[k]
        nk = parts[0][1]
        ctot = sum(p_[3] for p_ in parts)
        assert all(p_[1] == nk for p_ in parts) and parts[0][2] == 0
        toks = [ws.tok(j) for j in range(len(parts))]
        flat = ws[:].rearrange("p k c -> p (k c)")[:, 0:nk * ctot]
        view = flat.rearrange("p (k c) -> p k c", c=ctot)
        if visit <= store_visit:
            for j, (src, nk_, c0, C) in enumerate(parts):
                P.dma(view[:, :, c0:c0 + C], src.rearrange("(kc p) c -> p kc c", p=128), w=[ws.tok(j)], q="pool")
            if visit == store_visit:
                P.dma(scr[:, 0:nk * ctot], flat, r=toks, w=[("wscr", k)])
        else:
            P.dma(flat, scr[:, 0:nk * ctot], r=[("wscr", k)], w=toks, q="pool")
        return view, toks

    def rms_rstd(src_ap, src_tok, PT, col, scale, junk, junk_tok):
        P.act(lambda e: e.activation(out=junk, in_=src_ap, func=ACT.Square, accum_out=ssq[:PT, col:col + 1]),
              r=src_tok, w=[ssq.tok(col)] + junk_tok)
        P.act(lambda e: e.activation(out=rstd[:PT, col:col + 1], in_=ssq[:PT, col:col + 1], func=ACT.Ln, scale=scale, bias=EPS),
              r=[ssq.tok(col)], w=[rstd.tok(col)])
        P.act(lambda e: e.activation(out=rstd[:PT, col:col + 1], in_=rstd[:PT, col:col + 1], func=ACT.Exp, scale=-0.5),
              r=[rstd.tok(col)], w=[rstd.tok(col)])

    def norm_T(x_ap, x_tok, PT, nwt, dstT, c0, hb, junk):
        rms_rstd(x_ap, x_tok, PT, 0, 1.0 / D, junk[:PT, :], CT(junk))
        P.dve(lambda e: e.tensor_scalar(out=hb[:PT, :], in0=x_ap, scalar1=rstd[:PT, 0:1], scalar2=None, op0=ALU.mult),
              r=x_tok + [rstd.tok(0)], w=CT(hb))
        for half in range(2):
            bank = nxt("tp")
            for j in range(8):
                kc = half * 8 + j
                P.pe(lambda e, kc=kc, j=j, bank=bank: e.transpose(out=bank[:, j * PT:(j + 1) * PT], in_=hb[:PT, kc * 128:(kc + 1) * 128], identity=ident[:PT, :PT]),
                     r=CT(hb) + CT(ident), w=CT(bank))
            for j in range(8):
                kc = half * 8 + j
                if half == 0:
                    P.act(lambda e, kc=kc, j=j, bank=bank: e.activation(out=dstT[:, kc, c0:c0 + PT], in_=bank[:, j * PT:(j + 1) * PT], func=ACT.Copy, scale=nwt[:, kc:kc + 1]),
                          r=CT(bank) + CT(nwt), w=[dstT.tok(kc)])
                else:
                    P.dve(lambda e, kc=kc, j=j, bank=bank: e.tensor_scalar(out=dstT[:, kc, c0:c0 + PT], in0=bank[:, j * PT:(j + 1) * PT], scalar1=nwt[:, kc:kc + 1], scalar2=None, op0=ALU.mult),
                          r=CT(bank) + CT(nwt), w=[dstT.tok(kc)])

    hT_all = [hT.tok(kc) for kc in range(16)]

    def proj_tok(ws, wt, nk, wc0, wC, actT, act_tok, tiles, sink):
        for t, (c0, PT) in enumerate(tiles):
            bank = nxt("mm")
            for kc in range(nk):
                P.pe(lambda e, kc=kc, bank=bank, c0=c0, PT=PT: e.matmul(bank[:PT, 0:wC], lhsT=actT[:, kc, c0:c0 + PT], rhs=ws[:, kc, wc0:wc0 + wC], start=(kc == 0), stop=(kc == nk - 1)),
                     r=wt + act_tok, w=CT(bank))
            sink(t, bank)

    def proj_feat(ws, wt, nk, wc0, nchunk, actT, act_tok, NT, sink, pool="mm", M=128):
        for cc in range(nchunk):
            bank = nxt(pool)
            for kc in range(nk):
                P.pe(lambda e, kc=kc, bank=bank, cc=cc: e.matmul(bank[:M, 0:NT], lhsT=ws[:, kc, wc0 + cc * 128:wc0 + cc * 128 + M], rhs=actT[:, kc, 0:NT], start=(kc == 0), stop=(kc == nk - 1)),
                     r=wt + act_tok, w=CT(bank))
            sink(cc, bank)

    def phase1(bi):
        sample = bi == 4
        NT = 64 if sample else 512
        tiles = [(0, 64)] if sample else [(i * 128, 128) for i in range(4)]
        xsrc = xs if sample else xp[bi * 512:(bi + 1) * 512, :]
        r1off = B0 + (48 * NT + 63) // 64 * 64
        T = Group(P, r1off)
        xt = [T.a("xt", [128, D], F32) for _ in range(2)]
        hb = T.a("hb", [128, D], BF16)
        junk = T.a("junk", [128, D], F32)
        for t, (c0, PT) in enumerate(tiles):
            x_ = xt[t % 2]
            P.dma(x_[:PT, :], xsrc[c0:c0 + PT, :], w=CT(x_), q="pool")
            norm_T(x_[:PT, :], CT(x_), PT, n1t, hT, c0, hb, junk)
        T.free()

    def block(bi):
        sample = bi == 4
        wblk[0] = 0
        NT = 64 if sample else 512
        tiles = [(0, 64)] if sample else [(i * 128, 128) for i in range(4)]
        PTm = tiles[0][1]
        xsrc = xs if sample else xp[bi * 512:(bi + 1) * 512, :]
        ydst = ys if sample else yp[bi * 512:(bi + 1) * 512, :]
        ci = 1 if sample else 0
        trin, uu, caus = tri[:PTm, 3 * ci, :PTm], tri[:PTm, 3 * ci + 1, :PTm], tri[:PTm, 3 * ci + 2, :PTm]
        R1 = Group(P, B0)
        oTs = R1.a("oTs", [128, 8, NT], BF16)
        ogT = R1.a("ogT", [128, 16, NT], BF16)

        mark(f'b{bi} ph0 end')
        if bi == BLOCKS[0]:
            phase1(bi)

        mark(f'b{bi} ph1 end')
        T = Group(P, R1.off)
        z32 = [T.a("z32", [128, 512], F32) for _ in range(2)]
        rt = T.a("rt", [128, 4, 64], F32)
        qr = T.a("qr", [128, len(tiles), 1024], BF16)
        kr = T.a("kr", [128, len(tiles), 256], BF16)
        qT = [T.a("qT", [64, 16, 128], BF16) for _ in range(2)]
        PTb = [T.a("PTb", [128, 2, 512], BF16) for _ in range(2)]
        dtmp = T.a("dtmp", [128, 2, 128], F32)
        vst = T.a("vst", [128, len(tiles), 256], BF16)
        zc = [0]

        def rope_inplace(zb, PT, nh, t):
            v = zb[:PT, 0:nh * 64].rearrange("p (h d) -> p h d", d=64)
            x1, x2 = v[:, :, 0:8], v[:, :, 8:16]
            if sample:
                cos, sin = ropes[:PT, 0, :], ropes[:PT, 1, :]
            else:
                cos, sin = ropep[:PT, 0, bi * 4 + t, :], ropep[:PT, 1, bi * 4 + t, :]
            cb = cos.unsqueeze(1).to_broadcast([PT, nh, 8]); sb = sin.unsqueeze(1).to_broadcast([PT, nh, 8])
            T = [rt[:PT, i, 0:nh * 8].rearrange("p (h d) -> p h d", d=8) for i in range(4)]
            zt_, rtt = CT(zb), CT(rt)
            rr = zt_ + [ropes.tok() if sample else ropep.tok()]
            P.dve(lambda e: e.tensor_tensor(out=T[0], in0=x1, in1=cb, op=ALU.mult), r=rr, w=rtt)
            P.dve(lambda e: e.tensor_tensor(out=T[1], in0=x2, in1=sb, op=ALU.mult), r=rr, w=rtt)
            P.dve(lambda e: e.tensor_tensor(out=T[2], in0=x2, in1=cb, op=ALU.mult), r=rr, w=rtt)
            P.dve(lambda e: e.tensor_tensor(out=T[3], in0=x1, in1=sb, op=ALU.mult), r=rr, w=rtt)
            P.dve(lambda e: e.tensor_tensor(out=x1, in0=T[0], in1=T[1], op=ALU.subtract), r=rtt, w=zt_)
            P.dve(lambda e: e.tensor_tensor(out=x2, in0=T[2], in1=T[3], op=ALU.add), r=rtt, w=zt_)

        def q_sink(slot):
            def f(t, bank):
                PT = tiles[t][1]
                zb = z32[zc[0] % 2]; zc[0] += 1
                P.act(lambda e: e.activation(out=zb[:PT, :], in_=bank[:PT, :], func=ACT.Copy), r=CT(bank), w=CT(zb))
                rope_inplace(zb, PT, 8, t)
                P.dve(lambda e: e.tensor_copy(out=qr[:PT, t, slot * 512:(slot + 1) * 512], in_=zb[:PT, :]), r=CT(zb), w=[qr.tok(t, slot)])
            return f

        def kv_sink(t, bank):
            PT = tiles[t][1]
            gt = bi * 4 + t
            zb = z32[zc[0] % 2]; zc[0] += 1
            P.act(lambda e: e.activation(out=zb[:PT, :], in_=bank[:PT, :], func=ACT.Copy), r=CT(bank), w=CT(zb))
            rope_inplace(zb, PT, 4, t)
            P.dve(lambda e: e.tensor_copy(out=kr[:PT, t, :], in_=zb[:PT, 0:256]), r=CT(zb), w=[kr.tok(t)])
            P.dve(lambda e: e.tensor_copy(out=vst[:PT, t, :], in_=zb[:PT, 256:512]), r=CT(zb), w=[vst.tok(t)])
            if sample:
                for b in range(16):
                    P.dma(kso[b, 124:128, :], zb[4 * b:4 * b + 4, 0:256], r=CT(zb))
                    P.dma(vso[b, 124:128, :], zb[4 * b:4 * b + 4, 256:512], r=CT(zb))
            elif gt == 15:
                P.dma(kpo, zb[:, 0:256], r=CT(zb))
                P.dma(vpo, zb[:, 256:512], r=CT(zb))

        if sample:
            ckb = T.a("ckb", [128, 16, 256], BF16); cvb = T.a("cvb", [128, 16, 256], BF16)
            kTc = T.a("kTc", [64, 16, 4, 128], BF16)
            vcAB = T.a("vcAB", [128, 64, 2, 128], BF16)
            P.dma(ckb[:], ck.rearrange("b c f -> c b f"), w=CT(ckb), q="pool")
            P.dma(cvb[:], cv.rearrange("b c f -> c b f"), w=CT(cvb), q="pool")
            for b in range(16):
                P.dma(kso[b, 0:124, :], ck[b, 4:128, :])
                P.dma(vso[b, 0:124, :], cv[b, 4:128, :])
            P.dve(lambda e: e.memset(vcAB[:], 0.0), w=CT(vcAB))
            cvv = cvb[:].rearrange("p b (g d) -> p (b g) d", d=64)
            P.dve(lambda e: e.tensor_copy(out=vcAB[:, :, 0, 0:64], in_=cvv), r=CT(cvb), w=CT(vcAB))
            P.dve(lambda e: e.tensor_copy(out=vcAB[:, :, 1, 64:128], in_=cvv), r=CT(cvb), w=CT(vcAB))
            for b2 in range(8):
                bank = nxt("tp")
                for j in range(8):
                    b, g = (b2 * 8 + j) // 4, (b2 * 8 + j) % 4
                    P.pe(lambda e, b=b, g=g, j=j, bank=bank: e.transpose(out=bank[:64, j * 128:(j + 1) * 128], in_=ckb[:, b, g * 64:(g + 1) * 64], identity=ident[:]),
                         r=CT(ckb) + CT(ident), w=CT(bank))
                src = bank[:64, :].rearrange("p (b g c) -> p b g c", g=4, c=128)
                if b2 % 2 == 0:
                    P.act(lambda e, b2=b2, src=src: e.activation(out=kTc[:, 2 * b2:2 * b2 + 2, :, :], in_=src, func=ACT.Copy), r=CT(bank), w=CT(kTc))
                else:
                    P.dve(lambda e, b2=b2, src=src: e.tensor_copy(out=kTc[:, 2 * b2:2 * b2 + 2, :, :], in_=src), r=CT(bank), w=CT(kTc))

        for slot in range(2):
            ws, wt = wload([(w_in[:, QS + slot * 512:QS + (slot + 1) * 512], 16, 0, 512)])
            proj_tok(ws, wt, 16, 0, 512, hT, hT_all, tiles, q_sink(slot))
        ckpt('q proj')
        ws, wt = wload([(w_in[:, KS:KS + 512], 16, 0, 512)])
        proj_tok(ws, wt, 16, 0, 512, hT, hT_all, tiles, kv_sink)
        ckpt('kv proj')

        for t, (c0, PT) in enumerate(tiles):
            gt = bi * 4 + t
            qT_ = qT[t % 2]
            kT_ = kT[gt % 2] if not sample else kTn
            vs_ = vAB[gt % 2] if not sample else vnAB
            vv = vst[:PT, t, :].rearrange("p (g d) -> p g d", d=64)
            P.dve(lambda e: e.tensor_copy(out=vs_[:PT, :, 0, 0:64], in_=vv), r=[vst.tok(t)], w=CT(vs_))
            P.dve(lambda e: e.tensor_copy(out=vs_[:PT, :, 1, 64:128], in_=vv), r=[vst.tok(t)], w=CT(vs_))
            for half in range(2):
                bank = nxt("tp")
                for j in range(8):
                    hh = half * 8 + j
                    P.pe(lambda e, hh=hh, j=j, bank=bank: e.transpose(out=bank[:64, j * PT:(j + 1) * PT], in_=qr[:PT, t, hh * 64:(hh + 1) * 64], identity=ident[:PT, :PT]),
                         r=[qr.tok(t, hh // 8)] + CT(ident), w=CT(bank))
                src = bank[:64, 0:8 * PT].rearrange("p (h q) -> p h q", q=PT)
                if half == 0:
                    P.act(lambda e, src=src, half=half: e.activation(out=qT_[:, 0:8, :PT], in_=src, func=ACT.Copy), r=CT(bank), w=CT(qT_))
                else:
                    P.dve(lambda e, src=src, half=half: e.tensor_copy(out=qT_[:, 8:16, :PT], in_=src), r=CT(bank), w=CT(qT_))
            bank = nxt("tp")
            for g in range(4):
                P.pe(lambda e, g=g, bank=bank: e.transpose(out=bank[:64, g * PT:(g + 1) * PT], in_=kr[:PT, t, g * 64:(g + 1) * 64], identity=ident[:PT, :PT]),
                     r=[kr.tok(t)] + CT(ident), w=CT(bank))
            P.dve(lambda e, bank=bank: e.tensor_copy(out=kT_[:, :, :PT], in_=bank[:64, 0:4 * PT].rearrange("p (g s) -> p g s", s=PT)), r=CT(bank), w=CT(kT_))

            mmb2, axb2 = pools["mm"], pools["ax"]

            def S_(g):
                PT_ = PTb[g % 2]
                rq = qT_[:, 4 * g:4 * g + 4, :PT]
                sb = (mmb2[0], mmb2[1]) if g % 2 == 0 else (mmb2[2], mmb2[3])
                if not sample:
                    kinds = ([(0, kT[(gt - 1) % 2], mprev)] if gt > 0 else []) + [(1, kT_, mdiag)]
                    for (kd_, ksrc, msk) in kinds:
                        bank = sb[kd_]
                        P.pe(lambda e: e.matmul(bank[:, :], lhsT=ident[:], rhs=msk[:], start=True, stop=False), r=CT(ident) + CT(msk), w=CT(bank))
                        P.pe(lambda e: e.matmul(bank[:, :], lhsT=ksrc[:, g, :], rhs=rq, start=False, stop=True), r=CT(ksrc) + CT(qT_), w=CT(bank))
                        P.act(lambda e: e.activation(out=PT_[:, kd_, :], in_=bank[:, :], func=ACT.Exp, scale=0.125), r=CT(bank), w=[PT_.tok(kd_)])
                else:
                    bank = sb[0]
                    P.pe(lambda e: e.matmul(bank[:, 0:256], lhsT=ident[:], rhs=msc[:], start=True, stop=False), r=CT(ident) + CT(msc), w=CT(bank))
                    for b in range(16):
                        P.pe(lambda e, b=b: e.matmul(bank[:, 0:256].rearrange("p (h q) -> p h q", q=64)[:, :, 4 * b:4 * b + 4], lhsT=kTc[:, b, g, :], rhs=qT_[:, 4 * g:4 * g + 4, 4 * b:4 * b + 4], start=False, stop=(b == 15)),
                             r=CT(kTc) + CT(qT_), w=CT(bank))
                    P.act(lambda e: e.activation(out=PT_[:, 0, 0:256], in_=bank[:, 0:256], func=ACT.Exp, scale=0.125), r=CT(bank), w=[PT_.tok(0)])
                    bank2 = sb[1]
                    P.pe(lambda e: e.matmul(bank2[:64, 0:256], lhsT=ident[:64, :64], rhs=msn[:64, :], start=True, stop=False), r=CT(ident) + CT(msn), w=CT(bank2))
                    P.pe(lambda e: e.matmul(bank2[:64, 0:256], lhsT=kT_[:, g, :64], rhs=rq, start=False, stop=True), r=CT(kT_) + CT(qT_), w=CT(bank2))
                    P.act(lambda e: e.activation(out=PT_[:64, 1, 0:256], in_=bank2[:64, 0:256], func=ACT.Exp, scale=0.125), r=CT(bank2), w=[PT_.tok(1)])

            def PV_(g):
                PT_ = PTb[g % 2]
                pv = axb2[g % 2]
                pvv = pv[:, :].rearrange("p (r o q) -> p r o q", r=2, o=2)
                ptoks = [PT_.tok(0), PT_.tok(1)]
                for pr in range(2):
                    for od in range(2):
                        mms = []
                        if not sample:
                            for (kd_, vsrc) in ([(0, vAB[(gt - 1) % 2])] if gt > 0 else []) + [(1, vAB[gt % 2])]:
                                for ab in range(2):
                                    lh = vsrc[:, g, ab, :] if od == 0 else ones[:, ab, :]
                                    mms.append((pvv[:, pr, od, :], lh, PT_[:, kd_, (2 * pr + ab) * 128:(2 * pr + ab + 1) * 128], CT(vsrc)))
                        else:
                            for ab in range(2):
                                lh = vnAB[:64, g, ab, :] if od == 0 else ones[:64, ab, :]
                                mms.append((pvv[:, pr, od, 0:64], lh, PT_[:64, 1, (2 * pr + ab) * 64:(2 * pr + ab + 1) * 64], CT(vnAB)))
                            for b in range(16):
                                for ab in range(2):
                                    lh = vcAB[:, b * 4 + g, ab, :] if od == 0 else ones[:, ab, :]
                                    c_ = (2 * pr + ab) * 64 + 4 * b
                                    mms.append((pvv[:, pr, od, 4 * b:4 * b + 4], lh, PT_[:, 0, c_:c_ + 4], CT(vcAB)))
                        for i, (o_, l_, r_, tk) in enumerate(mms):
                            P.pe(lambda e, o_=o_, l_=l_, r_=r_, i=i, n=len(mms): e.matmul(o_, lhsT=l_, rhs=r_, start=(i == 0), stop=(i == n - 1)),
                                 r=ptoks + tk + CT(ones), w=CT(pv))

            def EV_(g):
                pv = axb2[g % 2]
                pvv = pv[:, :].rearrange("p (r o q) -> p r o q", r=2, o=2)
                for pr in range(2):
                    P.dve(lambda e, pr=pr: e.tensor_scalar(out=dtmp[:, pr, :PT], in0=pvv[:, pr, 1, :PT], scalar1=esink[:, 2 * g + pr:2 * g + pr + 1], scalar2=None, op0=ALU.add),
                          r=CT(pv) + CT(esink), w=CT(dtmp))
                P.dve(lambda e: e.reciprocal(out=dtmp[:, :, :PT], in_=dtmp[:, :, :PT]), r=CT(dtmp), w=CT(dtmp))
                P.dve(lambda e: e.tensor_tensor(out=oTs[:, 2 * g:2 * g + 2, c0:c0 + PT], in0=pvv[:, :, 0, :PT], in1=dtmp[:, :, :PT], op=ALU.mult),
                      r=CT(pv) + CT(dtmp), w=[oTs.tok(g)])

            S_(0)
            for g in range(4):
                if g < 3:
                    S_(g + 1)
                PV_(g)
                EV_(g)
        T.free()

        mark(f'b{bi} ph2 end')
        T = Group(P, R1.off)
        agT = T.a("agT", [32, NT], BF16)
        qgT = [T.a("qgT", [128, 2, NT], BF16) for _ in range(2)]; kgT = [T.a("kgT", [128, 2, NT], BF16) for _ in range(2)]
        kg = [T.a("kg", [128, len(tiles), 256], BF16) for _ in range(2)]
        vg = [T.a("vg", [128, len(tiles), 512], BF16) for _ in range(2)]
        srg = [T.a("srg", [128, len(tiles), 512], BF16) for _ in range(2)]
        e1 = T.a("e1", [128, 256], F32); sp = T.a("sp", [128, 256], F32)
        epos = [T.a("epos", [128, 2, 128], F32) for _ in range(2)]; eneg = T.a("eneg", [128, 2, 128], F32); ed = T.a("ed", [128, 256], F32)
        qiT = [T.a("qiT", [128, 2, 128], BF16) for _ in range(2)]; kiT = [T.a("kiT", [128, 2, 128], BF16) for _ in range(2)]; kd = [T.a("kd", [128, 256], BF16) for _ in range(2)]
        ATm = [T.a("ATm", [128, 128], BF16) for _ in range(2)]
        Sbf = [T.a("Sbf", [128, 2, 512], BF16) for _ in range(2)]
        ogf = T.a("ogf", [128, 512], F32); ogb = T.a("ogb", [128, 512], BF16)
        gjunk = T.a("gjunk", [128, 512], F32)
        if sample:
            qm = T.a("qm", [128, 2, 64], BF16); kdm = T.a("kdm", [64, 256], BF16)
            S0 = [T.a("S0", [128, 2, 512], F32) for _ in range(3)]
            S0b = [T.a("S0b", [128, 2, 512], BF16) for _ in range(2)]
            Sn = [T.a("Sn", [128, 2, 512], F32) for _ in range(2)]

        ws, wt = wload([(w_in[:, AG:AG + 16], 16, 0, 16)])
        P.dve(lambda e: e.memset(agT[:], 1.0), w=CT(agT))

        def ag_sink(cc, bank):
            P.act(lambda e: e.activation(out=agT[0:16, :], in_=bank[0:16, 0:NT], func=ACT.Copy), r=CT(bank), w=CT(agT))
        proj_feat(ws, wt, 16, 0, 1, hT, hT_all, NT, ag_sink, pool="ax", M=16)

        mmb = pools["mm"]
        pj_cnt = [0]

        def make_proj_items(h):
            hp = h % 2
            items = []
            ws1, wt1 = wload([(w_in[:, QG + h * 256:QG + (h + 1) * 256], 16, 0, 256), (w_in[:, KG + h * 256:KG + (h + 1) * 256], 16, 256, 256)])
            ws2, wt2 = wload([(w_in[:, VG + h * 512:VG + (h + 1) * 512], 16, 0, 512)])
            ws3, wt3 = wload([(w_in[:, RG + h * 512:RG + (h + 1) * 512], 16, 0, 512)])

            def pjbank():
                bk = mmb[2 + pj_cnt[0] % 2]
                pj_cnt[0] += 1
                return bk

            def feat_item(cc):
                def f():
                    bank = pjbank()
                    for kc in range(16):
                        P.pe(lambda e, kc=kc: e.matmul(bank[:, 0:NT], lhsT=ws1[:, kc, cc * 128:(cc + 1) * 128], rhs=hT[:, kc, 0:NT], start=(kc == 0), stop=(kc == 15)), r=wt1 + hT_all, w=CT(bank))
                    dst = qgT[hp] if cc < 2 else kgT[hp]
                    if cc % 2 == 0:
                        P.act(lambda e: e.activation(out=dst[:, cc % 2, :], in_=bank[:, 0:NT], func=ACT.Copy), r=CT(bank), w=CT(dst))
                    else:
                        P.dve(lambda e: e.tensor_copy(out=dst[:, cc % 2, :], in_=bank[:, 0:NT]), r=CT(bank), w=CT(dst))
                return f

            def tok_item(which, t):
                def f():
                    c0, PT = tiles[t]
                    bank = pjbank()
                    ws_, wt_, wc0, wC = {"k": (ws1, wt1, 256, 256), "v": (ws2, wt2, 0, 512), "r": (ws3, wt3, 0, 512)}[which]
                    for kc in range(16):
                        P.pe(lambda e, kc=kc: e.matmul(bank[:PT, 0:wC], lhsT=hT[:, kc, c0:c0 + PT], rhs=ws_[:, kc, wc0:wc0 + wC], start=(kc == 0), stop=(kc == 15)), r=wt_ + hT_all, w=CT(bank))
                    if which == "k":
                        P.dve(lambda e: e.tensor_copy(out=kg[hp][:PT, t, :], in_=bank[:PT, 0:256]), r=CT(bank), w=CT(kg[hp]))
                    elif which == "v":
                        P.act(lambda e: e.activation(out=vg[hp][:PT, t, :], in_=bank[:PT, :], func=ACT.Copy), r=CT(bank), w=CT(vg[hp]))
                    else:
                        P.act(lambda e: e.activation(out=srg[hp][:PT, t, :], in_=bank[:PT, :], func=ACT.Silu), r=CT(bank), w=CT(srg[hp]))
                return f
            for cc in range(4):
                items.append(feat_item(cc))
            for which in ("k", "v", "r"):
                for t in range(len(tiles)):
                    items.append(tok_item(which, t))
            return items

        pending = make_proj_items(0)
        for h in range(4):
            hp = h % 2
            for it in pending:
                it()
            pending = make_proj_items(h + 1) if h < 3 else []

            def PJ():
                if pending:
                    pending.pop(0)()
            nt_ = len(tiles)
            st_ = {}

            def A1(i):
                c0, PT = tiles[i]
                p = i % 2
                bx = nxt("ax")
                P.pe(lambda e: e.matmul(bx[:PT, 0:256], lhsT=agT[:, c0:c0 + PT], rhs=wa2[:, h * 256:(h + 1) * 256], start=True, stop=True), r=CT(agT) + CT(wa2), w=CT(bx))
                P.act(lambda e: e.activation(out=e1[:PT, :], in_=bx[:PT, 0:256], func=ACT.Exp, scale=-1.0), r=CT(bx), w=CT(e1))
                P.act(lambda e: e.activation(out=sp[:PT, :], in_=e1[:PT, :], func=ACT.Ln, bias=1.0), r=CT(e1), w=CT(sp))

            def A2(i):
                c0, PT = tiles[i]
                p = i % 2
                bb = nxt("ax")
                for kc in range(2):
                    P.pe(lambda e, kc=kc: e.matmul(bb[:, kc * PT:(kc + 1) * PT], lhsT=sp[:PT, kc * 128:(kc + 1) * 128], rhs=trin, start=True, stop=True), r=CT(sp) + CT(tri), w=CT(bb))
                P.pe(lambda e: e.matmul(bb[:PT, 256:512], lhsT=uu, rhs=sp[:PT, :], start=True, stop=True), r=CT(sp) + CT(tri), w=CT(bb))
                bbT = bb[:, 0:2 * PT].rearrange("p (k l) -> p k l", l=PT)
                ep = epos[p]
                P.act(lambda e: e.activation(out=ep[:, :, :PT], in_=bbT, func=ACT.Exp), r=CT(bb), w=CT(ep))
                P.act(lambda e: e.activation(out=eneg[:, :, :PT], in_=bbT, func=ACT.Exp, scale=-1.0), r=CT(bb), w=CT(eneg))
                P.act(lambda e: e.activation(out=ed[:PT, :], in_=bb[:PT, 256:512], func=ACT.Exp), r=CT(bb), w=CT(ed))

            def A3(i):
                c0, PT = tiles[i]
                p = i % 2
                ep, qi, ki, kd_, AT_ = epos[p], qiT[p], kiT[p], kd[p], ATm[p]
                P.dve(lambda e: e.scalar_tensor_tensor(out=qi[:, :, :PT], in0=qgT[hp][:, :, c0:c0 + PT], scalar=1.0 / 16.0, in1=ep[:, :, :PT], op0=ALU.mult, op1=ALU.mult), r=CT(qgT[hp]) + CT(ep), w=CT(qi))
                P.dve(lambda e: e.tensor_tensor(out=ki[:, :, :PT], in0=kgT[hp][:, :, c0:c0 + PT], in1=eneg[:, :, :PT], op=ALU.mult), r=CT(kgT[hp]) + CT(eneg), w=CT(ki))
                P.dve(lambda e: e.tensor_tensor(out=kd_[:PT, :], in0=kg[hp][:PT, i, :], in1=ed[:PT, :], op=ALU.mult), r=CT(kg[hp]) + CT(ed), w=CT(kd_))
                ba = nxt("ax")
                for kc in range(2):
                    P.pe(lambda e, kc=kc: e.matmul(ba[:PT, 0:PT], lhsT=ki[:, kc, :PT], rhs=qi[:, kc, :PT], start=(kc == 0), stop=(kc == 1)), r=CT(ki) + CT(qi), w=CT(ba))
                P.dve(lambda e: e.tensor_tensor(out=AT_[:PT, :PT], in0=ba[:PT, 0:PT], in1=caus, op=ALU.mult), r=CT(ba) + CT(tri), w=CT(AT_))

            def B1(t):
                c0, PT = tiles[t]
                gt = bi * 4 + t
                p = t % 2
                ep, qi, kd_, AT_ = epos[p], qiT[p], kd[p], ATm[p]
                po = mmb[0]
                st_[t] = po
                if not sample:
                    has_state = gt > 0
                    Sb = Sbf[t % 2]
                    if has_state:
                        P.act(lambda e: e.activation(out=Sb[:, 0, :], in_=S[:, 2 * h, :], func=ACT.Copy), r=[S.tok(2 * h)], w=[Sb.tok(0)])
                        P.dve(lambda e: e.tensor_copy(out=Sb[:, 1, :], in_=S[:, 2 * h + 1, :]), r=[S.tok(2 * h + 1)], w=[Sb.tok(1)])
                    P.pe(lambda e: e.matmul(po[:PT, :], lhsT=AT_[:PT, :PT], rhs=vg[hp][:PT, t, :], start=True, stop=not has_state), r=CT(AT_) + CT(vg[hp]), w=CT(po))
                    if has_state:
                        for kc in range(2):
                            P.pe(lambda e, kc=kc: e.matmul(po[:PT, :], lhsT=qi[:, kc, :PT], rhs=Sb[:, kc, :], start=False, stop=(kc == 1)), r=CT(qi) + [Sb.tok(kc)], w=CT(po))
                else:
                    P.pe(lambda e: e.matmul(po[:PT, :], lhsT=AT_[:PT, :PT], rhs=vg[hp][:PT, t, :], start=True, stop=False), r=CT(AT_) + CT(vg[hp]), w=CT(po))
                    for b in range(16):
                        P.dve(lambda e, b=b: e.tensor_tensor(out=qm[:, :, :], in0=qi[:, :, :64], in1=cm[:, b, :].unsqueeze(1).to_broadcast([128, 2, 64]), op=ALU.mult), r=CT(qi) + CT(cm), w=CT(qm))
                        P.dve(lambda e, b=b: e.tensor_scalar(out=kdm[:, :], in0=kd_[:64, :], scalar1=rm[:64, b:b + 1], scalar2=None, op0=ALU.mult), r=CT(kd_) + CT(rm), w=CT(kdm))
                        s0, s0b, sn = S0[b % 3], S0b[b % 2], Sn[b % 2]
                        P.dma(s0[:], st[b, h].rearrange("(kc p) v -> p kc v", p=128), w=CT(s0), q="pool")
                        P.act(lambda e, s0=s0, s0b=s0b: e.activation(out=s0b[:, 0, :], in_=s0[:, 0, :], func=ACT.Copy), r=CT(s0), w=[s0b.tok(0)])
                        P.dve(lambda e, s0=s0, s0b=s0b: e.tensor_copy(out=s0b[:, 1, :], in_=s0[:, 1, :]), r=CT(s0), w=[s0b.tok(1)])
                        for kc in range(2):
                            i = b * 2 + kc
                            P.pe(lambda e, kc=kc, s0b=s0b, i=i: e.matmul(po[:64, :], lhsT=qm[:, kc, :], rhs=s0b[:, kc, :], start=False, stop=(i == 31)), r=CT(qm) + [s0b.tok(kc)], w=CT(po))
                            pd_ = nxt("ax")
                            P.pe(lambda e, pd_=pd_, kc=kc: e.matmul(pd_[:, :], lhsT=kdm[:, kc * 128:(kc + 1) * 128], rhs=vg[hp][:64, t, :], start=True, stop=True), r=CT(kdm) + CT(vg[hp]), w=CT(pd_))
                            P.dve(lambda e, pd_=pd_, kc=kc, s0=s0, sn=sn, b=b: e.scalar_tensor_tensor(out=sn[:, kc, :], in0=s0[:, kc, :], scalar=ep[:, kc, 4 * b + 3:4 * b + 4], in1=pd_[:, :], op0=ALU.mult, op1=ALU.add),
                                  r=CT(pd_) + CT(ep) + CT(s0), w=[sn.tok(kc)])
                        P.dma(sso[b, h].rearrange("(kc p) v -> p kc v", p=128), sn[:], r=[sn.tok(0), sn.tok(1)])

            def B2(t, kcs=(0, 1), norm=True):
                c0, PT = tiles[t]
                gt = bi * 4 + t
                p = t % 2
                ep, kd_ = epos[p], kd[p]
                po = st_[t]
                if not sample:
                    has_state = gt > 0
                    for kc in kcs:
                        pd_ = mmb[1]
                        P.pe(lambda e, pd_=pd_, kc=kc: e.matmul(pd_[:, :], lhsT=kd_[:PT, kc * 128:(kc + 1) * 128], rhs=vg[hp][:PT, t, :], start=True, stop=True), r=CT(kd_) + CT(vg[hp]), w=CT(pd_))
                        if has_state:
                            P.dve(lambda e, pd_=pd_, kc=kc: e.scalar_tensor_tensor(out=S[:, 2 * h + kc, :], in0=S[:, 2 * h + kc, :], scalar=ep[:, kc, PT - 1:PT], in1=pd_[:, :], op0=ALU.mult, op1=ALU.add),
                                  r=CT(pd_) + CT(ep) + [S.tok(2 * h + kc)], w=[S.tok(2 * h + kc)])
                        else:
                            P.dve(lambda e, pd_=pd_, kc=kc: e.tensor_copy(out=S[:, 2 * h + kc, :], in_=pd_[:, :]), r=CT(pd_), w=[S.tok(2 * h + kc)])
                        if gt == 15:
                            P.dma(spo[h, kc * 128:(kc + 1) * 128, :], S[:, 2 * h + kc, :], r=[S.tok(2 * h + kc)])
                if norm:
                    rms_rstd(po[:PT, :], CT(po), PT, 1, 1.0 / 512.0, gjunk[:PT, :], CT(gjunk))

            def B3(t):
                c0, PT = tiles[t]
                po = st_[t]
                P.dve(lambda e: e.scalar_tensor_tensor(out=ogf[:PT, :], in0=po[:PT, :], scalar=rstd[:PT, 1:2], in1=gnb[:PT, :], op0=ALU.mult, op1=ALU.mult), r=CT(po) + [rstd.tok(1)] + CT(gnb), w=CT(ogf))
                P.dve(lambda e: e.tensor_tensor(out=ogb[:PT, :], in0=ogf[:PT, :], in1=srg[hp][:PT, t, :], op=ALU.mult), r=CT(ogf) + CT(srg[hp]), w=CT(ogb))
                bank = nxt("tp")
                for j in range(4):
                    P.pe(lambda e, j=j: e.transpose(out=bank[:, j * PT:(j + 1) * PT], in_=ogb[:PT, j * 128:(j + 1) * 128], identity=ident[:PT, :PT]), r=CT(ogb) + CT(ident), w=CT(bank))
                P.act(lambda e: e.activation(out=ogT[:, 4 * h:4 * h + 4, c0:c0 + PT], in_=bank[:, 0:4 * PT].rearrange("p (j q) -> p j q", q=PT), func=ACT.Copy), r=CT(bank), w=[ogT.tok(h)])

            for step in range(nt_ + 1):
                i, j = step, step - 1
                if i < nt_:
                    A1(i)
                PJ()
                if j >= 0:
                    B1(j)
                if i < nt_:
                    A2(i)
                PJ()
                if j >= 0:
                    B2(j, kcs=(0,), norm=True)
                if i < nt_:
                    A3(i)
                if j >= 0:
                    B2(j, kcs=(1,), norm=False)
                PJ()
                if j >= 0:
                    B3(j)
                PJ()
        T.free()

        mark(f'b{bi} ph3 end')
        XS = None
        if sample:
            XS = Group(P, B0 + 56 * 1024)
            ring.extend(XS.a("wsx", [128, 16, 512], BF16) for _ in range(3))
        G4 = Group(P, R1.off)
        yT = G4.a("yT", [128, 16, NT], BF16)
        sg = G4.a("sg", [128, 4, 2, NT], F32)
        y1 = G4.a("y1", [128, NT], F32)
        oTs_all = [oTs.tok(g) for g in range(4)]; ogT_all = [ogT.tok(h) for h in range(4)]
        for c4 in range(4):
            for c2 in range(2):
                fc = c4 * 4 + c2 * 2
                wsG, wtG = wload([(w_in[:, GS + fc * 128:GS + fc * 128 + 256], 16, 0, 256), (w_in[:, GG + fc * 128:GG + fc * 128 + 256], 16, 256, 256)])
                for j in range(2):
                    for gi in range(2):
                        bg = nxt("ax")
                        cc = c2 * 2 + j
                        for kc in range(16):
                            P.pe(lambda e, kc=kc, bg=bg, gi=gi, wsG=wsG, j=j: e.matmul(bg[:, 0:NT], lhsT=wsG[:, kc, gi * 256 + j * 128:gi * 256 + j * 128 + 128], rhs=hT[:, kc, 0:NT], start=(kc == 0), stop=(kc == 15)),
                                 r=wtG + hT_all, w=CT(bg))
                        P.act(lambda e, bg=bg, gi=gi, cc=cc: e.activation(out=sg[:, cc, gi, :], in_=bg[:, 0:NT], func=ACT.Sigmoid), r=CT(bg), w=[sg.tok(cc, gi)])
            wsA, wtA = wload([(p_swa[:, c4 * 512:(c4 + 1) * 512], 8, 0, 512)])
            wsB, wtB = wload([(p_gla[:, c4 * 512:(c4 + 1) * 512], 16, 0, 512)])
            for cc in range(4):
                fc = c4 * 4 + cc
                pa = nxt("mm")
                for kc in range(8):
                    P.pe(lambda e, kc=kc, pa=pa, wsA=wsA, cc=cc: e.matmul(pa[:, 0:NT], lhsT=wsA[:, kc, cc * 128:(cc + 1) * 128], rhs=oTs[:, kc, :], start=(kc == 0), stop=(kc == 7)), r=wtA + oTs_all, w=CT(pa))
                pb = nxt("mm")
                for kc in range(16):
                    P.pe(lambda e, kc=kc, pb=pb, wsB=wsB, cc=cc: e.matmul(pb[:, 0:NT], lhsT=wsB[:, kc, cc * 128:(cc + 1) * 128], rhs=ogT[:, kc, :], start=(kc == 0), stop=(kc == 15)), r=wtB + ogT_all, w=CT(pb))
                P.dve(lambda e, pa=pa, cc=cc: e.tensor_tensor(out=y1[:, :], in0=pa[:, 0:NT], in1=sg[:, cc, 0, :], op=ALU.mult), r=CT(pa) + [sg.tok(cc, 0)], w=CT(y1))
                P.dve(lambda e, pb=pb, cc=cc: e.tensor_tensor(out=sg[:, cc, 1, :], in0=pb[:, 0:NT], in1=sg[:, cc, 1, :], op=ALU.mult), r=CT(pb) + [sg.tok(cc, 1)], w=[sg.tok(cc, 1)])
                P.dve(lambda e, fc=fc, cc=cc: e.tensor_tensor(out=yT[:, fc, :], in0=y1[:, :], in1=sg[:, cc, 1, :], op=ALU.add), r=CT(y1) + [sg.tok(cc, 1)], w=[yT.tok(fc)])
        yT_all = [yT.tok(fc) for fc in range(16)]
        R1.free()

        mark(f'b{bi} ph4 end')
        X = Group(P, B0 + (40 if sample else 72) * 1024)
        x2 = X.a("x2", [128, len(tiles), D], F32)
        G5 = Group(P, G4.off)
        hb = G5.a("hb2", [128, D], BF16); junk = G5.a("junk2", [128, D], F32)
        for t, (c0, PT) in enumerate(tiles):
            P.dma(x2[:PT, t, :], xsrc[c0:c0 + PT, :], w=[x2.tok(t)])
        for c4 in range(4):
            ws, wt = wload([(w_o[:, c4 * 512:(c4 + 1) * 512], 16, 0, 512)])

            def wo_sink(t, bank, c4=c4):
                PT = tiles[t][1]
                P.dve(lambda e: e.tensor_tensor(out=x2[:PT, t, c4 * 512:(c4 + 1) * 512], in0=bank[:PT, :], in1=x2[:PT, t, c4 * 512:(c4 + 1) * 512], op=ALU.add), r=CT(bank) + [x2.tok(t)], w=[x2.tok(t)])
            proj_tok(ws, wt, 16, 0, 512, yT, yT_all, tiles, wo_sink)
        for t, (c0, PT) in enumerate(tiles):
            norm_T(x2[:PT, t, :], [x2.tok(t)], PT, n2t, hT, c0, hb, junk)
        G4.free(); G5.free()

        mark(f'b{bi} ph5 end')
        G6 = Group(P, B0)
        aT = G6.a("aT", [128, 64, NT], BF16)
        rl = [G6.a("rl", [128, NT], F32) for _ in range(2)]
        for g16 in range(16):
            ws, wt = wload([(w_up[:, g16 * 512:(g16 + 1) * 512], 16, 0, 512)])

            def up_sink(cc, bank, g16=g16):
                fcc = g16 * 4 + cc
                r_ = rl[fcc % 2]
                P.act(lambda e: e.activation(out=r_[:, :], in_=bank[:, 0:NT], func=ACT.Relu), r=CT(bank), w=CT(r_))
                P.dve(lambda e: e.tensor_tensor(out=aT[:, fcc, :], in0=r_[:, :], in1=r_[:, :], op=ALU.mult), r=CT(r_), w=[aT.tok(fcc)])
            proj_feat(ws, wt, 16, 0, 4, hT, hT_all, NT, up_sink)
        for c4 in range(4):
            banks = [nxt("mm") for _ in tiles]
            for g4 in range(4):
                ws, wt = wload([(w_down[g4 * 2048:(g4 + 1) * 2048, c4 * 512:(c4 + 1) * 512], 16, 0, 512)])
                for t, (c0, PT) in enumerate(tiles):
                    for fc in range(16):
                        P.pe(lambda e, t=t, fc=fc, c0=c0, PT=PT, ws=ws, g4=g4: e.matmul(banks[t][:PT, :], lhsT=aT[:, g4 * 16 + fc, c0:c0 + PT], rhs=ws[:, fc, :], start=(g4 == 0 and fc == 0), stop=(g4 == 3 and fc == 15)),
                             r=wt + [aT.tok(g4 * 16 + fc)], w=CT(banks[t]))
            for t, (c0, PT) in enumerate(tiles):
                P.dve(lambda e, t=t, PT=PT, c4=c4: e.tensor_tensor(out=x2[:PT, t, c4 * 512:(c4 + 1) * 512], in0=banks[t][:PT, :], in1=x2[:PT, t, c4 * 512:(c4 + 1) * 512], op=ALU.add), r=CT(banks[t]) + [x2.tok(t)], w=[x2.tok(t)])
        G6.free()

        mark(f'b{bi} ph6 end')
        nxt_bi = BLOCKS[BLOCKS.index(bi) + 1] if BLOCKS.index(bi) + 1 < len(BLOCKS) else None
        if nxt_bi is not None:
            phase1(nxt_bi)
        G7 = Group(P, B0 + (24 if sample else 56) * 1024)
        fnb = G7.a("fnb", [128, D], F32)
        yo_ = G7.a("yo", [128, D], F32)
        P.dma(fnb[:], fnw.broadcast_to([128, D]), w=CT(fnb))
        for t, (c0, PT) in enumerate(tiles):
            rms_rstd(x2[:PT, t, :], [x2.tok(t)], PT, 2, 1.0 / D, yo_[:PT, :], CT(yo_))
            P.dve(lambda e, t=t, PT=PT: e.scalar_tensor_tensor(out=yo_[:PT, :], in0=x2[:PT, t, :], scalar=rstd[:PT, 2:3], in1=fnb[:PT, :], op0=ALU.mult, op1=ALU.mult), r=[x2.tok(t), rstd.tok(2)] + CT(fnb), w=CT(yo_))
            P.dma(ydst[c0:c0 + PT, :], yo_[:PT, :], r=CT(yo_))
        G7.free(); X.free()
        if XS is not None:
            del ring[NS:]
            XS.free()
        mark(f'b{bi} ph7 end')

    try:
        for bi in BLOCKS:
            if block(bi):
                break
    except StopBuild:
        pass
    P.emit()
    return nc, P


def _consts():
    import ml_dtypes
    bf = ml_dtypes.bfloat16
    c = {}
    c["c_ident"] = np.eye(128, dtype=np.float32).astype(bf)
    s_ = np.arange(128)[:, None]; q_ = np.arange(128)[None, :]
    mprev = np.where(s_ > q_, 0.0, NEG).astype(np.float32)
    mdiag = np.where(s_ <= q_, 0.0, NEG).astype(np.float32)
    c["c_mprev"] = np.tile(mprev, (1, 4)).astype(bf)
    c["c_mdiag"] = np.tile(mdiag, (1, 4)).astype(bf)
    tok = np.arange(64)
    msc = np.where(np.arange(128)[:, None] > (tok % 4)[None, :], 0.0, NEG).astype(np.float32)
    c["c_msc"] = np.tile(msc, (1, 4)).astype(bf)
    j_ = np.arange(64)[:, None]; i_ = tok[None, :]
    msn = np.where((j_ // 4 == i_ // 4) & (j_ % 4 <= i_ % 4), 0.0, NEG).astype(np.float32)
    msn_full = np.full((128, 256), NEG, np.float32); msn_full[:64] = np.tile(msn, (1, 4))
    c["c_msn"] = msn_full.astype(bf)
    tri = np.zeros((2, 3, 128, 128), np.float32)
    m_ = np.arange(128)[:, None]; l_ = np.arange(128)[None, :]
    tri[0, 0] = np.where(m_ <= l_, -1.0 / 16.0, 0.0); tri[0, 1] = np.where(m_ > l_, -1.0 / 16.0, 0.0); tri[0, 2] = np.where(m_ <= l_, 1.0, 0.0)
    same = (m_ // 4 == l_ // 4)
    tri[1, 0] = np.where(same & (m_ <= l_), -1.0 / 16.0, 0.0); tri[1, 1] = np.where(same & (m_ > l_), -1.0 / 16.0, 0.0); tri[1, 2] = np.where(same & (m_ <= l_), 1.0, 0.0)
    c["c_tri"] = tri
    inv = (np.float32(500000.0) ** (-np.arange(8, dtype=np.float32) * np.float32(2.0) / np.float32(16.0))).astype(np.float32)
    pos = (np.arange(16)[None, :] * 128 + np.arange(128)[:, None]).astype(np.float32)
    ang = (pos[:, :, None] * inv[None, None, :]).astype(np.float32)
    c["c_ropep"] = np.stack([np.cos(ang), np.sin(ang)], axis=1).astype(np.float32)
    poss = (PAST + (np.arange(128) % 4)).astype(np.float32)
    angs = (poss[:, None] * inv[None, :]).astype(np.float32)
    c["c_ropes"] = np.stack([np.cos(angs), np.sin(angs)], axis=1).astype(np.float32)
    cmk = (np.arange(64)[None, :] // 4 == np.arange(16)[:, None]).astype(np.float32)
    c["c_cm"] = np.broadcast_to(cmk[None], (128, 16, 64)).astype(bf)
    rmk = np.zeros((128, 16), np.float32); rmk[:64] = cmk.T
    c["c_rm"] = rmk
    ones = np.zeros((128, 2, 128), np.float32); ones[:, 0, :64] = 1.0; ones[:, 1, 64:] = 1.0
    c["c_ones"] = ones.astype(bf)
    return {k: np.ascontiguousarray(v) for k, v in c.items()}


_CACHE = {}


def kernel(x_prompt, x_sample, cache_swa_k, cache_swa_v, state_gla, norm1, w_in, w_a2, b_a,
           sink, gla_norm, p_swa, p_gla, w_o, norm2, w_up, w_down, final_norm):
    f = lambda a: np.ascontiguousarray(np.asarray(a, dtype=np.float32))
    x_prompt, x_sample = f(x_prompt), f(x_sample)
    ck, cv, stt = f(cache_swa_k)[0], f(cache_swa_v)[0], f(state_gla)[0]
    if "nc" not in _CACHE:
        _CACHE["nc"] = build_program()[0]
    nc = _CACHE["nc"]
    sk = f(sink)[0]
    sinkl = np.empty((128, 8), np.float32)
    sinkl[:64, :] = sk[0::2][None, :]; sinkl[64:, :] = sk[1::2][None, :]
    shared = dict(
        w_in=f(w_in)[0], w_a2=f(w_a2)[0], b_a=f(b_a), sinkl=sinkl, gnorm=f(gla_norm),
        p_swa=f(p_swa)[0], p_gla=f(p_gla)[0], w_o=f(w_o)[0],
        n1=np.ascontiguousarray(f(norm1)[0].reshape(16, 128).T), n2=np.ascontiguousarray(f(norm2)[0].reshape(16, 128).T),
        fnw=f(final_norm).reshape(1, D), w_up=f(w_up)[0], w_down=f(w_down)[0],
    )
    shared.update(_consts())
    in_maps = []
    for c in range(8):
        m = dict(shared)
        m["xp"] = x_prompt[c]
        m["xs"] = np.ascontiguousarray(x_sample[c * 16:(c + 1) * 16].reshape(64, D))
        m["ck"] = np.ascontiguousarray(ck[c * 16:(c + 1) * 16].reshape(16, 128, 256))
        m["cv"] = np.ascontiguousarray(cv[c * 16:(c + 1) * 16].reshape(16, 128, 256))
        m["st"] = np.ascontiguousarray(stt[c * 16:(c + 1) * 16])
        in_maps.append(m)
    res = run_bass_kernel_spmd(nc, in_maps, core_ids=list(range(8)))
    R = res.results
    y_prompt = np.stack([R[c]["yp"] for c in range(8)]).astype(np.float32)
    y_sample = np.concatenate([R[c]["ys"].reshape(16, 4, D) for c in range(8)]).astype(np.float32)
    kp = np.stack([R[c]["kpo"].reshape(128, 4, 64) for c in range(8)])[None].astype(np.float32)
    vp = np.stack([R[c]["vpo"].reshape(128, 4, 64) for c in range(8)])[None].astype(np.float32)
    sp_ = np.stack([R[c]["spo"] for c in range(8)])[None].astype(np.float32)
    ks = np.concatenate([R[c]["kso"].reshape(16, 128, 4, 64) for c in range(8)])[None].astype(np.float32)
    vs = np.concatenate([R[c]["vso"].reshape(16, 128, 4, 64) for c in range(8)])[None].astype(np.float32)
    ss = np.concatenate([R[c]["sso"] for c in range(8)])[None].astype(np.float32)
    return (y_prompt, y_sample, kp, vp, sp_, ks, vs, ss)
```

```python
import numpy as np
import concourse.bass as bass
import concourse.mybir as mybir
from concourse.bass_utils import run_bass_kernel_spmd

F32 = mybir.dt.float32
BF16 = mybir.dt.bfloat16
ALU = mybir.AluOpType
ACT = mybir.ActivationFunctionType
AX = mybir.AxisListType

ENGS = ("pe", "act", "dve", "pool", "sp")


class Buf:
    def __init__(self, name, t, off=0, nbytes=0):
        self.name = name
        self.t = t
        self.off = off
        self.nbytes = nbytes
        self.inherit = []
        self.tokens = set()

    def tok(self, *key):
        k = (self.name,) + key
        self.tokens.add(k)
        return k

    def __getitem__(self, key):
        return self.t[key]


class _Rec:
    def __init__(self):
        self.call = None

    def __getattr__(self, name):
        def f(*a, **k):
            self.call = (name, a, k)
        return f


class Prog:
    RING = {"sp": 8, "pool": 8, "act": 4}

    def __init__(self, nc):
        self.nc = nc
        self.ops = {e: [] for e in ENGS}
        self.last_w = {}
        self.readers = {}
        self.bufs = {}
        self.ndma = {e: 0 for e in self.RING}
        self.live = []
        self.dead = []
        self.sb_lo = None
        self.psum_names = set()
        self.nop = 0

    def sb_init(self, lo, hi):
        self.sb_lo, self.sb_hi = (lo + 63) // 64 * 64, hi

    def sb_alloc(self, name, shape, dtype, off):
        nb = int(np.prod(shape[1:])) * mybir.dt.size(dtype)
        assert off % 32 == 0, (name, off)
        assert self.sb_lo + off + nb <= self.sb_hi, (name, off, nb, self.sb_hi - self.sb_lo)
        for (o, e, b) in self.live:
            assert off + nb <= o or off >= e, f"{name} overlaps live {b.name}"
        t = self.nc.alloc_sbuf_tensor_at(name, list(shape), dtype, offset=self.sb_lo + off)
        b = Buf(name, t, off, nb)
        inh = []
        for (o, e, d) in self.dead:
            if not (off + nb <= o or off >= e):
                for tk in d.tokens:
                    if tk in self.last_w:
                        inh.append(self.last_w[tk])
                    inh.extend(self.readers.get(tk, []))
                inh.extend(d.inherit)
        b.inherit = list(set(inh))
        self.live.append((off, off + nb, b))
        self.bufs[name] = b
        return b

    def sb_free(self, *bufs):
        for b in bufs:
            ent = [x for x in self.live if x[2] is b]
            assert ent, b.name
            self.live.remove(ent[0])
            self.dead.append(ent[0])

    def psum(self, name, shape, dtype=F32):
        t = self.nc.alloc_psum_tensor(name, list(shape), dtype)
        b = Buf(name, t)
        self.bufs[name] = b
        self.psum_names.add(name)
        return b

    def _deps(self, eng, r, w):
        deps = set()
        for tk in r:
            self._touch(tk)
            if tk in self.last_w:
                deps.add(self.last_w[tk] + ("raw",))
        for tk in w:
            self._touch(tk)
            port = tk == ("PSUMPORT",)
            if tk in self.last_w:
                deps.add(self.last_w[tk] + ("port" if port else "waw",))
            for rd in self.readers.get(tk, []):
                deps.add(rd + ("war",))
        return deps

    def _touch(self, tk):
        if tk not in self.last_w and tk not in self.readers:
            b = self.bufs.get(tk[0])
            if b is not None and b.inherit:
                self.readers[tk] = list(b.inherit)

    def op(self, eng, fn, r=(), w=(), dma=False):
        r = [x for x in r if x is not None]
        w = [x for x in w if x is not None]
        if eng in ("act", "dve") and any(tk[0] in self.psum_names for tk in r):
            w = w + [("PSUMPORT",)]
        deps = self._deps(eng, r, w)
        idx = len(self.ops[eng])
        ref = (eng, idx)
        dmaj = None
        if dma:
            dmaj = self.ndma[eng]
            self.ndma[eng] += 1
        rec = _Rec()
        fn(rec)
        assert rec.call is not None
        self.ops[eng].append(dict(fn=rec.call, deps=deps, dma=dmaj, sig=False))
        for tk in w:
            self.last_w[tk] = ref
            self.readers[tk] = []
        for tk in r:
            self.readers.setdefault(tk, []).append(ref)
        return ref

    def pe(self, fn, r=(), w=()):
        return self.op("pe", fn, r, w)

    def act(self, fn, r=(), w=()):
        return self.op("act", fn, r, w)

    def dve(self, fn, r=(), w=()):
        return self.op("dve", fn, r, w)

    def pool(self, fn, r=(), w=()):
        return self.op("pool", fn, r, w)

    def dma(self, out, in_, r=(), w=(), q="sp", **kw):
        return self.op(q, lambda e: e.dma_start(out=out, in_=in_, **kw), r, w, dma=True)

    def _needs_wait(self, eng, idx, dep):
        deng, didx, kind = dep
        dop = self.ops[deng][didx]
        if dop["dma"] is not None:
            return True
        if deng != eng:
            return True
        if eng == "pe":
            return False
        if kind in ("war", "port"):
            return False
        return (idx - didx) <= 3

    def emit(self):
        nc = self.nc
        sem_eng = {e: nc.alloc_semaphore(f"s_{e}") for e in ("pe", "act", "dve", "pool")}
        ring = {q: [nc.alloc_semaphore(f"r_{q}{i}") for i in range(n)] for q, n in self.RING.items()}
        for eng in ENGS:
            for idx, o in enumerate(self.ops[eng]):
                keep = set()
                for dep in o["deps"]:
                    if self._needs_wait(eng, idx, dep):
                        keep.add(dep[:2])
                        d = self.ops[dep[0]][dep[1]]
                        if d["dma"] is None:
                            d["sig"] = True
                o["waits"] = keep
        for eng in ENGS:
            cnt = 0
            for o in self.ops[eng]:
                if o["dma"] is not None:
                    n = self.RING[eng]
                    j = o["dma"]
                    o["done"] = (ring[eng][j % n], 16 * (j // n + 1))
                    o["pre"] = (ring[eng][j % n], 16 * (j // n)) if j >= n else None
                elif o["sig"]:
                    cnt += 1
                    o["done"] = (sem_eng[eng], cnt)
        engobj = {"pe": "tensor", "act": "scalar", "dve": "vector", "pool": "gpsimd", "sp": "sync"}
        self.final_waits = {}

        def run(eng):
            def body(e):
                waited = {}
                ops = self.ops[eng]
                for o in ops:
                    ws = {}
                    for (deng, didx) in o["waits"]:
                        s, v = self.ops[deng][didx]["done"]
                        ws[s] = max(ws.get(s, 0), v)
                    if o["dma"] is not None and o["pre"] is not None:
                        s, v = o["pre"]
                        ws[s] = max(ws.get(s, 0), v)
                    for s, v in ws.items():
                        if waited.get(s, 0) < v:
                            e.wait_ge(s, v)
                            waited[s] = v
                    ins = getattr(e, o["fn"][0])(*o["fn"][1], **o["fn"][2])
                    if o["dma"] is not None:
                        s, v = o["done"]
                        ins.then_inc(s, 16)
                    elif o["sig"]:
                        ins.then_inc(sem_eng[eng], 1)
                if eng in self.RING:
                    last = {}
                    for o in ops:
                        if o["dma"] is not None:
                            s, v = o["done"]
                            last[s] = max(last.get(s, 0), v)
                    for s, v in last.items():
                        if waited.get(s, 0) < v:
                            e.wait_ge(s, v)
            return body

        with nc.Block() as block:
            for eng in ENGS:
                if not self.ops[eng]:
                    continue
                getattr(block, engobj[eng])(run(eng))

    def stats(self):
        return {e: len(self.ops[e]) for e in ENGS}

D = 2048
DIN = 11792
DFF = 8192
PAST = 8192
QS, KS, VS, QG, KG, VG, RG, AG, GS, GG = 0, 1024, 1280, 1536, 2560, 3584, 5632, 7680, 7696, 9744
EPS = 1e-6
NEG = -30000.0
NS = 3


class Group:
    cnt = [0]

    def __init__(s, P, off):
        s.P, s.off, s.bufs = P, (off + 63) // 64 * 64, []

    def a(s, name, shape, dt):
        Group.cnt[0] += 1
        b = s.P.sb_alloc(f"{name}_{Group.cnt[0]}", shape, dt, s.off)
        s.off += (b.nbytes + 63) // 64 * 64
        s.bufs.append(b)
        return b

    def free(s):
        for b in s.bufs:
            s.P.sb_free(b)
        s.bufs = []


def build_program(dbg=False):
    import os
    STOP = os.environ.get('KSTOP', '')
    BLOCKS = [int(c) for c in os.environ.get('KBLOCKS', '01234')]
    KSUB = int(os.environ.get('KSUB', '0'))
    ck_n = [0]

    class StopBuild(Exception):
        pass

    def ckpt(tag=''):
        ck_n[0] += 1
        if KSUB and ck_n[0] == KSUB:
            print('STOP at ckpt', ck_n[0], tag)
            raise StopBuild()
    nc = bass.Bass("TRN2", target_bir_lowering=False)
    P = Prog(nc)
    P.sb_init(nc.sbuf_base, nc.sbuf_top)
    P.marks = []

    def mark(lab):
        P.marks.append((lab, sum(1 for o in P.ops['pe'] if o['fn'][0] == 'matmul')))
    al = Group(P, 0)

    def din(name, shape, dt=F32):
        return nc.dram_tensor(name, list(shape), dt, kind="ExternalInput").ap()

    def dout(name, shape):
        return nc.dram_tensor(name, list(shape), F32, kind="ExternalOutput").ap()

    xp = din("xp", [2048, D]); xs = din("xs", [64, D])
    ck = din("ck", [16, 128, 256]); cv = din("cv", [16, 128, 256])
    st = din("st", [16, 4, 256, 512])
    w_in = din("w_in", [D, DIN]); w_a2 = din("w_a2", [16, 1024]); b_a = din("b_a", [1, 1024])
    sinkl = din("sinkl", [128, 8]); gnorm = din("gnorm", [1, 512])
    p_swa = din("p_swa", [1024, D]); p_gla = din("p_gla", [D, D]); w_o = din("w_o", [D, D])
    n1 = din("n1", [128, 16]); n2 = din("n2", [128, 16]); fnw = din("fnw", [1, D])
    w_up = din("w_up", [D, DFF]); w_down = din("w_down", [DFF, D])
    c_ident = din("c_ident", [128, 128], BF16)
    c_mprev = din("c_mprev", [128, 512], BF16); c_mdiag = din("c_mdiag", [128, 512], BF16)
    c_msc = din("c_msc", [128, 256], BF16); c_msn = din("c_msn", [128, 256], BF16)
    c_tri = din("c_tri", [2, 3, 128, 128])
    c_ropep = din("c_ropep", [128, 2, 16, 8]); c_ropes = din("c_ropes", [128, 2, 8])
    c_cm = din("c_cm", [128, 16, 64], BF16); c_rm = din("c_rm", [128, 16])
    c_ones = din("c_ones", [128, 2, 128], BF16)

    yp = dout("yp", [2048, D]); ys = dout("ys", [64, D])
    kpo = dout("kpo", [128, 256]); vpo = dout("vpo", [128, 256]); spo = dout("spo", [4, 256, 512])
    kso = dout("kso", [16, 128, 256]); vso = dout("vso", [16, 128, 256]); sso = dout("sso", [16, 4, 256, 512])

    pools = {
        "mm": [P.psum(f"mm{i}", [128, 512], F32) for i in range(4)],
        "tp": [P.psum(f"tp{i}", [128, 1024], BF16) for i in range(2)],
        "ax": [P.psum(f"ax{i}", [128, 512], F32) for i in range(2)],
    }
    pcnt = {k: 0 for k in pools}

    def nxt(pool):
        b = pools[pool][pcnt[pool] % len(pools[pool])]
        pcnt[pool] += 1
        return b

    ident = al.a("ident", [128, 128], BF16)
    mprev = al.a("mprev", [128, 512], BF16); mdiag = al.a("mdiag", [128, 512], BF16)
    msc = al.a("msc", [128, 256], BF16); msn = al.a("msn", [128, 256], BF16)
    tri = al.a("tri", [128, 6, 128], F32)
    ropep = al.a("ropep", [128, 2, 16, 8], F32); ropes = al.a("ropes", [128, 2, 8], F32)
    cm = al.a("cm", [128, 16, 64], BF16); rm = al.a("rm", [128, 16], F32)
    ones = al.a("ones", [128, 2, 128], BF16)
    esink = al.a("esink", [128, 8], F32)
    gnb = al.a("gnb", [128, 512], F32)
    n1t = al.a("n1t", [128, 16], F32); n2t = al.a("n2t", [128, 16], F32)
    wa2 = al.a("wa2", [32, 1024], BF16)
    ssq = al.a("ssq", [128, 4], F32); rstd = al.a("rstd", [128, 4], F32)
    S = al.a("S", [128, 8, 512], F32)
    hT = al.a("hT", [128, 16, 512], BF16)
    wslots = [al.a(f"ws{i}", [128, 16, 512], BF16) for i in range(NS)]
    kT = [al.a("kT", [64, 4, 128], BF16) for _ in range(2)]
    vAB = [al.a("vAB", [128, 4, 2, 128], BF16) for _ in range(2)]
    kTn, vnAB = kT[0], vAB[0]
    B0 = al.off
    print('B0', B0, 'arena', P.sb_hi - P.sb_lo)
    T0 = Group(P, B0)
    wa2f = T0.a("wa2f", [32, 1024], F32)
    CT = lambda b: [b.tok()]

    for dst, src in ((ident, c_ident), (mprev, c_mprev), (mdiag, c_mdiag), (msc, c_msc), (msn, c_msn),
                     (ropep, c_ropep), (ropes, c_ropes), (cm, c_cm), (rm, c_rm), (ones, c_ones),
                     (n1t, n1), (n2t, n2)):
        P.dma(dst[:], src, w=CT(dst))
    P.dma(tri[:], c_tri.rearrange("a b p l -> p (a b) l"), w=CT(tri))
    P.dma(esink[:], sinkl, w=CT(esink))
    P.act(lambda e: e.activation(out=esink[:], in_=esink[:], func=ACT.Exp), r=CT(esink), w=CT(esink))
    P.dma(gnb[:], gnorm.broadcast_to([128, 512]), w=CT(gnb))
    P.dve(lambda e: e.memset(wa2f[:], 0.0), w=CT(wa2f))
    P.dma(wa2f[0:16, :], w_a2, w=CT(wa2f))
    P.dma(wa2f[16:17, :], b_a, w=CT(wa2f))
    P.dve(lambda e: e.tensor_copy(out=wa2[:], in_=wa2f[:]), r=CT(wa2f), w=CT(wa2))
    T0.free()
    for v_ in vAB:
        P.dve(lambda e, v_=v_: e.memset(v_[:], 0.0), w=CT(v_))

    wuse = [0]
    ring = list(wslots)
    wblk = [0]
    wvis = {}
    wscr = {}

    def wload(parts):
        ws = ring[wuse[0] % len(ring)]
        wuse[0] += 1
        k = wblk[0]
        wblk[0] += 1
        visit = wvis.get(k, 0)
        wvis[k] = visit + 1
        store_visit = 0 if k % 3 == 0 else 1
        if k not in wscr:
            wscr[k] = nc.dram_tensor(f"wscr{k}", [128, 16 * 512], BF16).ap()
        scr = wscr[k]
        nk = parts[0][1]
        ctot = sum(p_[3] for p_ in parts)
        assert all(p_[1] == nk for p_ in parts) and parts[0][2] == 0
        toks = [ws.tok(j) for j in range(len(parts))]
        flat = ws[:].rearrange("p k c -> p (k c)")[:, 0:nk * ctot]
        view = flat.rearrange("p (k c) -> p k c", c=ctot)
        if visit <= store_visit:
            for j, (src, nk_, c0, C) in enumerate(parts):
                P.dma(view[:, :, c0:c0 + C], src.rearrange("(kc p) c -> p kc c", p=128), w=[ws.tok(j)], q="pool")
            if visit == store_visit:
                P.dma(scr[:, 0:nk * ctot], flat, r=toks, w=[("wscr", k)])
        else:
            P.dma(flat, scr[:, 0:nk * ctot], r=[("wscr", k)], w=toks, q="pool")
        return view, toks

    def rms_rstd(src_ap, src_tok, PT, col, scale, junk, junk_tok):
        P.act(lambda e: e.activation(out=junk, in_=src_ap, func=ACT.Square, accum_out=ssq[:PT, col:col + 1]),
              r=src_tok, w=[ssq.tok(col)] + junk_tok)
        P.act(lambda e: e.activation(out=rstd[:PT, col:col + 1], in_=ssq[:PT, col:col + 1], func=ACT.Ln, scale=scale, bias=EPS),
              r=[ssq.tok(col)], w=[rstd.tok(col)])
        P.act(lambda e: e.activation(out=rstd[:PT, col:col + 1], in_=rstd[:PT, col:col + 1], func=ACT.Exp, scale=-0.5),
              r=[rstd.tok(col)], w=[rstd.tok(col)])

    def norm_T(x_ap, x_tok, PT, nwt, dstT, c0, hb, junk):
        rms_rstd(x_ap, x_tok, PT, 0, 1.0 / D, junk[:PT, :], CT(junk))
        P.dve(lambda e: e.tensor_scalar(out=hb[:PT, :], in0=x_ap, scalar1=rstd[:PT, 0:1], scalar2=None, op0=ALU.mult),
              r=x_tok + [rstd.tok(0)], w=CT(hb))
        for half in range(2):
            bank = nxt("tp")
            for j in range(8):
                kc = half * 8 + j
                P.pe(lambda e, kc=kc, j=j, bank=bank: e.transpose(out=bank[:, j * PT:(j + 1) * PT], in_=hb[:PT, kc * 128:(kc + 1) * 128], identity=ident[:PT, :PT]),
                     r=CT(hb) + CT(ident), w=CT(bank))
            for j in range(8):
                kc = half * 8 + j
                if half == 0:
                    P.act(lambda e, kc=kc, j=j, bank=bank: e.activation(out=dstT[:, kc, c0:c0 + PT], in_=bank[:, j * PT:(j + 1) * PT], func=ACT.Copy, scale=nwt[:, kc:kc + 1]),
                          r=CT(bank) + CT(nwt), w=[dstT.tok(kc)])
                else:
                    P.dve(lambda e, kc=kc, j=j, bank=bank: e.tensor_scalar(out=dstT[:, kc, c0:c0 + PT], in0=bank[:, j * PT:(j + 1) * PT], scalar1=nwt[:, kc:kc + 1], scalar2=None, op0=ALU.mult),
                          r=CT(bank) + CT(nwt), w=[dstT.tok(kc)])

    hT_all = [hT.tok(kc) for kc in range(16)]

    def proj_tok(ws, wt, nk, wc0, wC, actT, act_tok, tiles, sink):
        for t, (c0, PT) in enumerate(tiles):
            bank = nxt("mm")
            for kc in range(nk):
                P.pe(lambda e, kc=kc, bank=bank, c0=c0, PT=PT: e.matmul(bank[:PT, 0:wC], lhsT=actT[:, kc, c0:c0 + PT], rhs=ws[:, kc, wc0:wc0 + wC], start=(kc == 0), stop=(kc == nk - 1)),
                     r=wt + act_tok, w=CT(bank))
            sink(t, bank)

    def proj_feat(ws, wt, nk, wc0, nchunk, actT, act_tok, NT, sink, pool="mm", M=128):
        for cc in range(nchunk):
            bank = nxt(pool)
            for kc in range(nk):
                P.pe(lambda e, kc=kc, bank=bank, cc=cc: e.matmul(bank[:M, 0:NT], lhsT=ws[:, kc, wc0 + cc * 128:wc0 + cc * 128 + M], rhs=actT[:, kc, 0:NT], start=(kc == 0), stop=(kc == nk - 1)),
                     r=wt + act_tok, w=CT(bank))
            sink(cc, bank)

    def phase1(bi):
        sample = bi == 4
        NT = 64 if sample else 512
        tiles = [(0, 64)] if sample else [(i * 128, 128) for i in range(4)]
        xsrc = xs if sample else xp[bi * 512:(bi + 1) * 512, :]
        r1off = B0 + (48 * NT + 63) // 64 * 64
        T = Group(P, r1off)
        xt = [T.a("xt", [128, D], F32) for _ in range(2)]
        hb = T.a("hb", [128, D], BF16)
        junk = T.a("junk", [128, D], F32)
        for t, (c0, PT) in enumerate(tiles):
            x_ = xt[t % 2]
            P.dma(x_[:PT, :], xsrc[c0:c0 + PT, :], w=CT(x_), q="pool")
            norm_T(x_[:PT, :], CT(x_), PT, n1t, hT, c0, hb, junk)
        T.free()

    def block(bi):
        sample = bi == 4
        wblk[0] = 0
        NT = 64 if sample else 512
        tiles = [(0, 64)] if sample else [(i * 128, 128) for i in range(4)]
        PTm = tiles[0][1]
        xsrc = xs if sample else xp[bi * 512:(bi + 1) * 512, :]
        ydst = ys if sample else yp[bi * 512:(bi + 1) * 512, :]
        ci = 1 if sample else 0
        trin, uu, caus = tri[:PTm, 3 * ci, :PTm], tri[:PTm, 3 * ci + 1, :PTm], tri[:PTm, 3 * ci + 2, :PTm]
        R1 = Group(P, B0)
        oTs = R1.a("oTs", [128, 8, NT], BF16)
        ogT = R1.a("ogT", [128, 16, NT], BF16)

        mark(f'b{bi} ph0 end')
        if bi == BLOCKS[0]:
            phase1(bi)

        mark(f'b{bi} ph1 end')
        T = Group(P, R1.off)
        z32 = [T.a("z32", [128, 512], F32) for _ in range(2)]
        rt = T.a("rt", [128, 4, 64], F32)
        qr = T.a("qr", [128, len(tiles), 1024], BF16)
        kr = T.a("kr", [128, len(tiles), 256], BF16)
        qT = [T.a("qT", [64, 16, 128], BF16) for _ in range(2)]
        PTb = [T.a("PTb", [128, 2, 512], BF16) for _ in range(2)]
        dtmp = T.a("dtmp", [128, 2, 128], F32)
        vst = T.a("vst", [128, len(tiles), 256], BF16)
        zc = [0]

        def rope_inplace(zb, PT, nh, t):
            v = zb[:PT, 0:nh * 64].rearrange("p (h d) -> p h d", d=64)
            x1, x2 = v[:, :, 0:8], v[:, :, 8:16]
            if sample:
                cos, sin = ropes[:PT, 0, :], ropes[:PT, 1, :]
            else:
                cos, sin = ropep[:PT, 0, bi * 4 + t, :], ropep[:PT, 1, bi * 4 + t, :]
            cb = cos.unsqueeze(1).to_broadcast([PT, nh, 8]); sb = sin.unsqueeze(1).to_broadcast([PT, nh, 8])
            T = [rt[:PT, i, 0:nh * 8].rearrange("p (h d) -> p h d", d=8) for i in range(4)]
            zt_, rtt = CT(zb), CT(rt)
            rr = zt_ + [ropes.tok() if sample else ropep.tok()]
            P.dve(lambda e: e.tensor_tensor(out=T[0], in0=x1, in1=cb, op=ALU.mult), r=rr, w=rtt)
            P.dve(lambda e: e.tensor_tensor(out=T[1], in0=x2, in1=sb, op=ALU.mult), r=rr, w=rtt)
            P.dve(lambda e: e.tensor_tensor(out=T[2], in0=x2, in1=cb, op=ALU.mult), r=rr, w=rtt)
            P.dve(lambda e: e.tensor_tensor(out=T[3], in0=x1, in1=sb, op=ALU.mult), r=rr, w=rtt)
            P.dve(lambda e: e.tensor_tensor(out=x1, in0=T[0], in1=T[1], op=ALU.subtract), r=rtt, w=zt_)
            P.dve(lambda e: e.tensor_tensor(out=x2, in0=T[2], in1=T[3], op=ALU.add), r=rtt, w=zt_)

        def q_sink(slot):
            def f(t, bank):
                PT = tiles[t][1]
                zb = z32[zc[0] % 2]; zc[0] += 1
                P.act(lambda e: e.activation(out=zb[:PT, :], in_=bank[:PT, :], func=ACT.Copy), r=CT(bank), w=CT(zb))
                rope_inplace(zb, PT, 8, t)
                P.dve(lambda e: e.tensor_copy(out=qr[:PT, t, slot * 512:(slot + 1) * 512], in_=zb[:PT, :]), r=CT(zb), w=[qr.tok(t, slot)])
            return f

        def kv_sink(t, bank):
            PT = tiles[t][1]
            gt = bi * 4 + t
            zb = z32[zc[0] % 2]; zc[0] += 1
            P.act(lambda e: e.activation(out=zb[:PT, :], in_=bank[:PT, :], func=ACT.Copy), r=CT(bank), w=CT(zb))
            rope_inplace(zb, PT, 4, t)
            P.dve(lambda e: e.tensor_copy(out=kr[:PT, t, :], in_=zb[:PT, 0:256]), r=CT(zb), w=[kr.tok(t)])
            P.dve(lambda e: e.tensor_copy(out=vst[:PT, t, :], in_=zb[:PT, 256:512]), r=CT(zb), w=[vst.tok(t)])
            if sample:
                for b in range(16):
                    P.dma(kso[b, 124:128, :], zb[4 * b:4 * b + 4, 0:256], r=CT(zb))
                    P.dma(vso[b, 124:128, :], zb[4 * b:4 * b + 4, 256:512], r=CT(zb))
            elif gt == 15:
                P.dma(kpo, zb[:, 0:256], r=CT(zb))
                P.dma(vpo, zb[:, 256:512], r=CT(zb))

        for slot in range(2):
            ws, wt = wload([(w_in[:, QS + slot * 512:QS + (slot + 1) * 512], 16, 0, 512)])
            proj_tok(ws, wt, 16, 0, 512, hT, hT_all, tiles, q_sink(slot))
        ckpt('q proj')
        ws, wt = wload([(w_in[:, KS:KS + 512], 16, 0, 512)])
        proj_tok(ws, wt, 16, 0, 512, hT, hT_all, tiles, kv_sink)
        ckpt('kv proj')

        if sample:
            ckb = T.a("ckb", [128, 16, 256], BF16); cvb = T.a("cvb", [128, 16, 256], BF16)
            kTc = T.a("kTc", [64, 16, 4, 128], BF16)
            vcAB = T.a("vcAB", [128, 64, 2, 128], BF16)
            P.dma(ckb[:], ck.rearrange("b c f -> c b f"), w=CT(ckb), q="pool")
            P.dma(cvb[:], cv.rearrange("b c f -> c b f"), w=CT(cvb), q="pool")
            for b in range(16):
                P.dma(kso[b, 0:124, :], ck[b, 4:128, :])
                P.dma(vso[b, 0:124, :], cv[b, 4:128, :])
            P.dve(lambda e: e.memset(vcAB[:], 0.0), w=CT(vcAB))
            cvv = cvb[:].rearrange("p b (g d) -> p (b g) d", d=64)
            P.dve(lambda e: e.tensor_copy(out=vcAB[:, :, 0, 0:64], in_=cvv), r=CT(cvb), w=CT(vcAB))
            P.dve(lambda e: e.tensor_copy(out=vcAB[:, :, 1, 64:128], in_=cvv), r=CT(cvb), w=CT(vcAB))
            for b2 in range(8):
                bank = nxt("tp")
                for j in range(8):
                    b, g = (b2 * 8 + j) // 4, (b2 * 8 + j) % 4
                    P.pe(lambda e, b=b, g=g, j=j, bank=bank: e.transpose(out=bank[:64, j * 128:(j + 1) * 128], in_=ckb[:, b, g * 64:(g + 1) * 64], identity=ident[:]),
                         r=CT(ckb) + CT(ident), w=CT(bank))
                src = bank[:64, :].rearrange("p (b g c) -> p b g c", g=4, c=128)
                if b2 % 2 == 0:
                    P.act(lambda e, b2=b2, src=src: e.activation(out=kTc[:, 2 * b2:2 * b2 + 2, :, :], in_=src, func=ACT.Copy), r=CT(bank), w=CT(kTc))
                else:
                    P.dve(lambda e, b2=b2, src=src: e.tensor_copy(out=kTc[:, 2 * b2:2 * b2 + 2, :, :], in_=src), r=CT(bank), w=CT(kTc))

        for t, (c0, PT) in enumerate(tiles):
            gt = bi * 4 + t
            qT_ = qT[t % 2]
            kT_ = kT[gt % 2] if not sample else kTn
            vs_ = vAB[gt % 2] if not sample else vnAB
            vv = vst[:PT, t, :].rearrange("p (g d) -> p g d", d=64)
            P.dve(lambda e: e.tensor_copy(out=vs_[:PT, :, 0, 0:64], in_=vv), r=[vst.tok(t)], w=CT(vs_))
            P.dve(lambda e: e.tensor_copy(out=vs_[:PT, :, 1, 64:128], in_=vv), r=[vst.tok(t)], w=CT(vs_))
            for half in range(2):
                bank = nxt("tp")
                for j in range(8):
                    hh = half * 8 + j
                    P.pe(lambda e, hh=hh, j=j, bank=bank: e.transpose(out=bank[:64, j * PT:(j + 1) * PT], in_=qr[:PT, t, hh * 64:(hh + 1) * 64], identity=ident[:PT, :PT]),
                         r=[qr.tok(t, hh // 8)] + CT(ident), w=CT(bank))
                src = bank[:64, 0:8 * PT].rearrange("p (h q) -> p h q", q=PT)
                if half == 0:
                    P.act(lambda e, src=src, half=half: e.activation(out=qT_[:, 0:8, :PT], in_=src, func=ACT.Copy), r=CT(bank), w=CT(qT_))
                else:
                    P.dve(lambda e, src=src, half=half: e.tensor_copy(out=qT_[:, 8:16, :PT], in_=src), r=CT(bank), w=CT(qT_))
            bank = nxt("tp")
            for g in range(4):
                P.pe(lambda e, g=g, bank=bank: e.transpose(out=bank[:64, g * PT:(g + 1) * PT], in_=kr[:PT, t, g * 64:(g + 1) * 64], identity=ident[:PT, :PT]),
                     r=[kr.tok(t)] + CT(ident), w=CT(bank))
            P.dve(lambda e, bank=bank: e.tensor_copy(out=kT_[:, :, :PT], in_=bank[:64, 0:4 * PT].rearrange("p (g s) -> p g s", s=PT)), r=CT(bank), w=CT(kT_))

            mmb2, axb2 = pools["mm"], pools["ax"]

            def S_(g):
                PT_ = PTb[g % 2]
                rq = qT_[:, 4 * g:4 * g + 4, :PT]
                sb = (mmb2[0], mmb2[1]) if g % 2 == 0 else (mmb2[2], mmb2[3])
                if not sample:
                    kinds = ([(0, kT[(gt - 1) % 2], mprev)] if gt > 0 else []) + [(1, kT_, mdiag)]
                    for (kd_, ksrc, msk) in kinds:
                        bank = sb[kd_]
                        P.pe(lambda e: e.matmul(bank[:, :], lhsT=ident[:], rhs=msk[:], start=True, stop=False), r=CT(ident) + CT(msk), w=CT(bank))
                        P.pe(lambda e: e.matmul(bank[:, :], lhsT=ksrc[:, g, :], rhs=rq, start=False, stop=True), r=CT(ksrc) + CT(qT_), w=CT(bank))
                        P.act(lambda e: e.activation(out=PT_[:, kd_, :], in_=bank[:, :], func=ACT.Exp, scale=0.125), r=CT(bank), w=[PT_.tok(kd_)])
                else:
                    bank = sb[0]
                    P.pe(lambda e: e.matmul(bank[:, 0:256], lhsT=ident[:], rhs=msc[:], start=True, stop=False), r=CT(ident) + CT(msc), w=CT(bank))
                    for b in range(16):
                        P.pe(lambda e, b=b: e.matmul(bank[:, 0:256].rearrange("p (h q) -> p h q", q=64)[:, :, 4 * b:4 * b + 4], lhsT=kTc[:, b, g, :], rhs=qT_[:, 4 * g:4 * g + 4, 4 * b:4 * b + 4], start=False, stop=(b == 15)),
                             r=CT(kTc) + CT(qT_), w=CT(bank))
                    P.act(lambda e: e.activation(out=PT_[:, 0, 0:256], in_=bank[:, 0:256], func=ACT.Exp, scale=0.125), r=CT(bank), w=[PT_.tok(0)])
                    bank2 = sb[1]
                    P.pe(lambda e: e.matmul(bank2[:64, 0:256], lhsT=ident[:64, :64], rhs=msn[:64, :], start=True, stop=False), r=CT(ident) + CT(msn), w=CT(bank2))
                    P.pe(lambda e: e.matmul(bank2[:64, 0:256], lhsT=kT_[:, g, :64], rhs=rq, start=False, stop=True), r=CT(kT_) + CT(qT_), w=CT(bank2))
                    P.act(lambda e: e.activation(out=PT_[:64, 1, 0:256], in_=bank2[:64, 0:256], func=ACT.Exp, scale=0.125), r=CT(bank2), w=[PT_.tok(1)])

            def PV_(g):
                PT_ = PTb[g % 2]
                pv = axb2[g % 2]
                pvv = pv[:, :].rearrange("p (r o q) -> p r o q", r=2, o=2)
                ptoks = [PT_.tok(0), PT_.tok(1)]
                for pr in range(2):
                    for od in range(2):
                        mms = []
                        if not sample:
                            for (kd_, vsrc) in ([(0, vAB[(gt - 1) % 2])] if gt > 0 else []) + [(1, vAB[gt % 2])]:
                                for ab in range(2):
                                    lh = vsrc[:, g, ab, :] if od == 0 else ones[:, ab, :]
                                    mms.append((pvv[:, pr, od, :], lh, PT_[:, kd_, (2 * pr + ab) * 128:(2 * pr + ab + 1) * 128], CT(vsrc)))
                        else:
                            for ab in range(2):
                                lh = vnAB[:64, g, ab, :] if od == 0 else ones[:64, ab, :]
                                mms.append((pvv[:, pr, od, 0:64], lh, PT_[:64, 1, (2 * pr + ab) * 64:(2 * pr + ab + 1) * 64], CT(vnAB)))
                            for b in range(16):
                                for ab in range(2):
                                    lh = vcAB[:, b * 4 + g, ab, :] if od == 0 else ones[:, ab, :]
                                    c_ = (2 * pr + ab) * 64 + 4 * b
                                    mms.append((pvv[:, pr, od, 4 * b:4 * b + 4], lh, PT_[:, 0, c_:c_ + 4], CT(vcAB)))
                        for i, (o_, l_, r_, tk) in enumerate(mms):
                            P.pe(lambda e, o_=o_, l_=l_, r_=r_, i=i, n=len(mms): e.matmul(o_, lhsT=l_, rhs=r_, start=(i == 0), stop=(i == n - 1)),
                                 r=ptoks + tk + CT(ones), w=CT(pv))

            def EV_(g):
                pv = axb2[g % 2]
                pvv = pv[:, :].rearrange("p (r o q) -> p r o q", r=2, o=2)
                for pr in range(2):
                    P.dve(lambda e, pr=pr: e.tensor_scalar(out=dtmp[:, pr, :PT], in0=pvv[:, pr, 1, :PT], scalar1=esink[:, 2 * g + pr:2 * g + pr + 1], scalar2=None, op0=ALU.add),
                          r=CT(pv) + CT(esink), w=CT(dtmp))
                P.dve(lambda e: e.reciprocal(out=dtmp[:, :, :PT], in_=dtmp[:, :, :PT]), r=CT(dtmp), w=CT(dtmp))
                P.dve(lambda e: e.tensor_tensor(out=oTs[:, 2 * g:2 * g + 2, c0:c0 + PT], in0=pvv[:, :, 0, :PT], in1=dtmp[:, :, :PT], op=ALU.mult),
                      r=CT(pv) + CT(dtmp), w=[oTs.tok(g)])

            S_(0)
            for g in range(4):
                if g < 3:
                    S_(g + 1)
                PV_(g)
                EV_(g)
        T.free()

        mark(f'b{bi} ph2 end')
        T = Group(P, R1.off)
        agT = T.a("agT", [32, NT], BF16)
        qgT = [T.a("qgT", [128, 2, NT], BF16) for _ in range(2)]; kgT = [T.a("kgT", [128, 2, NT], BF16) for _ in range(2)]
        kg = [T.a("kg", [128, len(tiles), 256], BF16) for _ in range(2)]
        vg = [T.a("vg", [128, len(tiles), 512], BF16) for _ in range(2)]
        srg = [T.a("srg", [128, len(tiles), 512], BF16) for _ in range(2)]
        e1 = T.a("e1", [128, 256], F32); sp = T.a("sp", [128, 256], F32)
        epos = [T.a("epos", [128, 2, 128], F32) for _ in range(2)]; eneg = T.a("eneg", [128, 2, 128], F32); ed = T.a("ed", [128, 256], F32)
        qiT = [T.a("qiT", [128, 2, 128], BF16) for _ in range(2)]; kiT = [T.a("kiT", [128, 2, 128], BF16) for _ in range(2)]; kd = [T.a("kd", [128, 256], BF16) for _ in range(2)]
        ATm = [T.a("ATm", [128, 128], BF16) for _ in range(2)]
        Sbf = [T.a("Sbf", [128, 2, 512], BF16) for _ in range(2)]
        ogf = T.a("ogf", [128, 512], F32); ogb = T.a("ogb", [128, 512], BF16)
        gjunk = T.a("gjunk", [128, 512], F32)
        if sample:
            qm = T.a("qm", [128, 2, 64], BF16); kdm = T.a("kdm", [64, 256], BF16)
            S0 = [T.a("S0", [128, 2, 512], F32) for _ in range(3)]
            S0b = [T.a("S0b", [128, 2, 512], BF16) for _ in range(2)]
            Sn = [T.a("Sn", [128, 2, 512], F32) for _ in range(2)]

        ws, wt = wload([(w_in[:, AG:AG + 16], 16, 0, 16)])
        P.dve(lambda e: e.memset(agT[:], 1.0), w=CT(agT))

        def ag_sink(cc, bank):
            P.act(lambda e: e.activation(out=agT[0:16, :], in_=bank[0:16, 0:NT], func=ACT.Copy), r=CT(bank), w=CT(agT))
        proj_feat(ws, wt, 16, 0, 1, hT, hT_all, NT, ag_sink, pool="ax", M=16)

        mmb = pools["mm"]
        pj_cnt = [0]

        def make_proj_items(h):
            hp = h % 2
            items = []
            ws1, wt1 = wload([(w_in[:, QG + h * 256:QG + (h + 1) * 256], 16, 0, 256), (w_in[:, KG + h * 256:KG + (h + 1) * 256], 16, 256, 256)])
            ws2, wt2 = wload([(w_in[:, VG + h * 512:VG + (h + 1) * 512], 16, 0, 512)])
            ws3, wt3 = wload([(w_in[:, RG + h * 512:RG + (h + 1) * 512], 16, 0, 512)])

            def pjbank():
                bk = mmb[2 + pj_cnt[0] % 2]
                pj_cnt[0] += 1
                return bk

            def feat_item(cc):
                def f():
                    bank = pjbank()
                    for kc in range(16):
                        P.pe(lambda e, kc=kc: e.matmul(bank[:, 0:NT], lhsT=ws1[:, kc, cc * 128:(cc + 1) * 128], rhs=hT[:, kc, 0:NT], start=(kc == 0), stop=(kc == 15)), r=wt1 + hT_all, w=CT(bank))
                    dst = qgT[hp] if cc < 2 else kgT[hp]
                    if cc % 2 == 0:
                        P.act(lambda e: e.activation(out=dst[:, cc % 2, :], in_=bank[:, 0:NT], func=ACT.Copy), r=CT(bank), w=CT(dst))
                    else:
                        P.dve(lambda e: e.tensor_copy(out=dst[:, cc % 2, :], in_=bank[:, 0:NT]), r=CT(bank), w=CT(dst))
                return f

            def tok_item(which, t):
                def f():
                    c0, PT = tiles[t]
                    bank = pjbank()
                    ws_, wt_, wc0, wC = {"k": (ws1, wt1, 256, 256), "v": (ws2, wt2, 0, 512), "r": (ws3, wt3, 0, 512)}[which]
                    for kc in range(16):
                        P.pe(lambda e, kc=kc: e.matmul(bank[:PT, 0:wC], lhsT=hT[:, kc, c0:c0 + PT], rhs=ws_[:, kc, wc0:wc0 + wC], start=(kc == 0), stop=(kc == 15)), r=wt_ + hT_all, w=CT(bank))
                    if which == "k":
                        P.dve(lambda e: e.tensor_copy(out=kg[hp][:PT, t, :], in_=bank[:PT, 0:256]), r=CT(bank), w=CT(kg[hp]))
                    elif which == "v":
                        P.act(lambda e: e.activation(out=vg[hp][:PT, t, :], in_=bank[:PT, :], func=ACT.Copy), r=CT(bank), w=CT(vg[hp]))
                    else:
                        P.act(lambda e: e.activation(out=srg[hp][:PT, t, :], in_=bank[:PT, :], func=ACT.Silu), r=CT(bank), w=CT(srg[hp]))
                return f
            def ktr_item(t):
                def f():
                    c0, PT = tiles[t]
                    bank = nxt("tp")
                    for kc in range(2):
                        P.pe(lambda e, kc=kc: e.transpose(out=bank[:PT, kc * 128:(kc + 1) * 128], in_=kgT[hp][:, kc, c0:c0 + PT], identity=ident[:]), r=CT(kgT[hp]) + CT(ident), w=CT(bank))
                    P.dve(lambda e: e.tensor_copy(out=kg[hp][:PT, t, :], in_=bank[:PT, 0:256]), r=CT(bank), w=CT(kg[hp]))
                return f
            for cc in range(4):
                items.append(feat_item(cc))
            for t in range(len(tiles)):
                items.append(ktr_item(t))
            for which in ("v", "r"):
                for t in range(len(tiles)):
                    items.append(tok_item(which, t))
            return items

        pending = make_proj_items(0)
        for h in range(4):
            hp = h % 2
            for it in pending:
                it()
            pending = make_proj_items(h + 1) if h < 3 else []

            def PJ():
                if pending:
                    pending.pop(0)()
            nt_ = len(tiles)
            st_ = {}

            def A1(i):
                c0, PT = tiles[i]
                p = i % 2
                bx = nxt("ax")
                P.pe(lambda e: e.matmul(bx[:PT, 0:256], lhsT=agT[:, c0:c0 + PT], rhs=wa2[:, h * 256:(h + 1) * 256], start=True, stop=True), r=CT(agT) + CT(wa2), w=CT(bx))
                P.act(lambda e: e.activation(out=e1[:PT, :], in_=bx[:PT, 0:256], func=ACT.Exp, scale=-1.0), r=CT(bx), w=CT(e1))
                P.act(lambda e: e.activation(out=sp[:PT, :], in_=e1[:PT, :], func=ACT.Ln, bias=1.0), r=CT(e1), w=CT(sp))

            def A2(i):
                c0, PT = tiles[i]
                p = i % 2
                bb = nxt("ax")
                for kc in range(2):
                    P.pe(lambda e, kc=kc: e.matmul(bb[:, kc * PT:(kc + 1) * PT], lhsT=sp[:PT, kc * 128:(kc + 1) * 128], rhs=trin, start=True, stop=True), r=CT(sp) + CT(tri), w=CT(bb))
                P.pe(lambda e: e.matmul(bb[:PT, 256:512], lhsT=uu, rhs=sp[:PT, :], start=True, stop=True), r=CT(sp) + CT(tri), w=CT(bb))
                bbT = bb[:, 0:2 * PT].rearrange("p (k l) -> p k l", l=PT)
                ep = epos[p]
                P.act(lambda e: e.activation(out=ep[:, :, :PT], in_=bbT, func=ACT.Exp), r=CT(bb), w=CT(ep))
                P.act(lambda e: e.activation(out=eneg[:, :, :PT], in_=bbT, func=ACT.Exp, scale=-1.0), r=CT(bb), w=CT(eneg))
                P.act(lambda e: e.activation(out=ed[:PT, :], in_=bb[:PT, 256:512], func=ACT.Exp), r=CT(bb), w=CT(ed))

            def A3(i):
                c0, PT = tiles[i]
                p = i % 2
                ep, qi, ki, kd_, AT_ = epos[p], qiT[p], kiT[p], kd[p], ATm[p]
                P.dve(lambda e: e.scalar_tensor_tensor(out=qi[:, :, :PT], in0=qgT[hp][:, :, c0:c0 + PT], scalar=1.0 / 16.0, in1=ep[:, :, :PT], op0=ALU.mult, op1=ALU.mult), r=CT(qgT[hp]) + CT(ep), w=CT(qi))
                P.dve(lambda e: e.tensor_tensor(out=ki[:, :, :PT], in0=kgT[hp][:, :, c0:c0 + PT], in1=eneg[:, :, :PT], op=ALU.mult), r=CT(kgT[hp]) + CT(eneg), w=CT(ki))
                P.dve(lambda e: e.tensor_tensor(out=kd_[:PT, :], in0=kg[hp][:PT, i, :], in1=ed[:PT, :], op=ALU.mult), r=CT(kg[hp]) + CT(ed), w=CT(kd_))
                ba = nxt("ax")
                for kc in range(2):
                    P.pe(lambda e, kc=kc: e.matmul(ba[:PT, 0:PT], lhsT=ki[:, kc, :PT], rhs=qi[:, kc, :PT], start=(kc == 0), stop=(kc == 1)), r=CT(ki) + CT(qi), w=CT(ba))
                P.dve(lambda e: e.tensor_tensor(out=AT_[:PT, :PT], in0=ba[:PT, 0:PT], in1=caus, op=ALU.mult), r=CT(ba) + CT(tri), w=CT(AT_))

            def B1(t):
                c0, PT = tiles[t]
                gt = bi * 4 + t
                p = t % 2
                ep, qi, kd_, AT_ = epos[p], qiT[p], kd[p], ATm[p]
                po = mmb[0]
                st_[t] = po
                if not sample:
                    has_state = gt > 0
                    Sb = Sbf[t % 2]
                    if has_state:
                        P.act(lambda e: e.activation(out=Sb[:, 0, :], in_=S[:, 2 * h, :], func=ACT.Copy), r=[S.tok(2 * h)], w=[Sb.tok(0)])
                        P.dve(lambda e: e.tensor_copy(out=Sb[:, 1, :], in_=S[:, 2 * h + 1, :]), r=[S.tok(2 * h + 1)], w=[Sb.tok(1)])
                    P.pe(lambda e: e.matmul(po[:PT, :], lhsT=AT_[:PT, :PT], rhs=vg[hp][:PT, t, :], start=True, stop=not has_state), r=CT(AT_) + CT(vg[hp]), w=CT(po))
                    if has_state:
                        for kc in range(2):
                            P.pe(lambda e, kc=kc: e.matmul(po[:PT, :], lhsT=qi[:, kc, :PT], rhs=Sb[:, kc, :], start=False, stop=(kc == 1)), r=CT(qi) + [Sb.tok(kc)], w=CT(po))
                else:
                    P.pe(lambda e: e.matmul(po[:PT, :], lhsT=AT_[:PT, :PT], rhs=vg[hp][:PT, t, :], start=True, stop=False), r=CT(AT_) + CT(vg[hp]), w=CT(po))
                    for b in range(16):
                        P.dve(lambda e, b=b: e.tensor_tensor(out=qm[:, :, :], in0=qi[:, :, :64], in1=cm[:, b, :].unsqueeze(1).to_broadcast([128, 2, 64]), op=ALU.mult), r=CT(qi) + CT(cm), w=CT(qm))
                        P.dve(lambda e, b=b: e.tensor_scalar(out=kdm[:, :], in0=kd_[:64, :], scalar1=rm[:64, b:b + 1], scalar2=None, op0=ALU.mult), r=CT(kd_) + CT(rm), w=CT(kdm))
                        s0, s0b, sn = S0[b % 3], S0b[b % 2], Sn[b % 2]
                        P.dma(s0[:], st[b, h].rearrange("(kc p) v -> p kc v", p=128), w=CT(s0), q="pool")
                        P.act(lambda e, s0=s0, s0b=s0b: e.activation(out=s0b[:, 0, :], in_=s0[:, 0, :], func=ACT.Copy), r=CT(s0), w=[s0b.tok(0)])
                        P.dve(lambda e, s0=s0, s0b=s0b: e.tensor_copy(out=s0b[:, 1, :], in_=s0[:, 1, :]), r=CT(s0), w=[s0b.tok(1)])
                        for kc in range(2):
                            i = b * 2 + kc
                            P.pe(lambda e, kc=kc, s0b=s0b, i=i: e.matmul(po[:64, :], lhsT=qm[:, kc, :], rhs=s0b[:, kc, :], start=False, stop=(i == 31)), r=CT(qm) + [s0b.tok(kc)], w=CT(po))
                            pd_ = nxt("ax")
                            P.pe(lambda e, pd_=pd_, kc=kc: e.matmul(pd_[:, :], lhsT=kdm[:, kc * 128:(kc + 1) * 128], rhs=vg[hp][:64, t, :], start=True, stop=True), r=CT(kdm) + CT(vg[hp]), w=CT(pd_))
                            P.dve(lambda e, pd_=pd_, kc=kc, s0=s0, sn=sn, b=b: e.scalar_tensor_tensor(out=sn[:, kc, :], in0=s0[:, kc, :], scalar=ep[:, kc, 4 * b + 3:4 * b + 4], in1=pd_[:, :], op0=ALU.mult, op1=ALU.add),
                                  r=CT(pd_) + CT(ep) + CT(s0), w=[sn.tok(kc)])
                        P.dma(sso[b, h].rearrange("(kc p) v -> p kc v", p=128), sn[:], r=[sn.tok(0), sn.tok(1)])

            def B2(t, kcs=(0, 1), norm=True):
                c0, PT = tiles[t]
                gt = bi * 4 + t
                p = t % 2
                ep, kd_ = epos[p], kd[p]
                po = st_[t]
                if not sample:
                    has_state = gt > 0
                    for kc in kcs:
                        pd_ = mmb[1]
                        P.pe(lambda e, pd_=pd_, kc=kc: e.matmul(pd_[:, :], lhsT=kd_[:PT, kc * 128:(kc + 1) * 128], rhs=vg[hp][:PT, t, :], start=True, stop=True), r=CT(kd_) + CT(vg[hp]), w=CT(pd_))
                        if has_state:
                            P.dve(lambda e, pd_=pd_, kc=kc: e.scalar_tensor_tensor(out=S[:, 2 * h + kc, :], in0=S[:, 2 * h + kc, :], scalar=ep[:, kc, PT - 1:PT], in1=pd_[:, :], op0=ALU.mult, op1=ALU.add),
                                  r=CT(pd_) + CT(ep) + [S.tok(2 * h + kc)], w=[S.tok(2 * h + kc)])
                        else:
                            P.dve(lambda e, pd_=pd_, kc=kc: e.tensor_copy(out=S[:, 2 * h + kc, :], in_=pd_[:, :]), r=CT(pd_), w=[S.tok(2 * h + kc)])
                        if gt == 15:
                            P.dma(spo[h, kc * 128:(kc + 1) * 128, :], S[:, 2 * h + kc, :], r=[S.tok(2 * h + kc)])
                if norm:
                    rms_rstd(po[:PT, :], CT(po), PT, 1, 1.0 / 512.0, gjunk[:PT, :], CT(gjunk))

            def B3(t):
                c0, PT = tiles[t]
                po = st_[t]
                P.dve(lambda e: e.scalar_tensor_tensor(out=ogf[:PT, :], in0=po[:PT, :], scalar=rstd[:PT, 1:2], in1=gnb[:PT, :], op0=ALU.mult, op1=ALU.mult), r=CT(po) + [rstd.tok(1)] + CT(gnb), w=CT(ogf))
                P.dve(lambda e: e.tensor_tensor(out=ogb[:PT, :], in0=ogf[:PT, :], in1=srg[hp][:PT, t, :], op=ALU.mult), r=CT(ogf) + CT(srg[hp]), w=CT(ogb))
                bank = nxt("tp")
                for j in range(4):
                    P.pe(lambda e, j=j: e.transpose(out=bank[:, j * PT:(j + 1) * PT], in_=ogb[:PT, j * 128:(j + 1) * 128], identity=ident[:PT, :PT]), r=CT(ogb) + CT(ident), w=CT(bank))
                P.act(lambda e: e.activation(out=ogT[:, 4 * h:4 * h + 4, c0:c0 + PT], in_=bank[:, 0:4 * PT].rearrange("p (j q) -> p j q", q=PT), func=ACT.Copy), r=CT(bank), w=[ogT.tok(h)])

            for step in range(nt_ + 1):
                i, j = step, step - 1
                if i < nt_:
                    A1(i)
                PJ()
                if j >= 0:
                    B1(j)
                if i < nt_:
                    A2(i)
                PJ()
                if j >= 0:
                    B2(j, kcs=(0,), norm=True)
                if i < nt_:
                    A3(i)
                if j >= 0:
                    B2(j, kcs=(1,), norm=False)
                PJ()
                if j >= 0:
                    B3(j)
                PJ()
        T.free()

        mark(f'b{bi} ph3 end')
        XS = None
        if sample:
            XS = Group(P, B0 + 56 * 1024)
            ring.extend(XS.a("wsx", [128, 16, 512], BF16) for _ in range(3))
        G4 = Group(P, R1.off)
        yT = G4.a("yT", [128, 16, NT], BF16)
        sg = G4.a("sg", [128, 4, 2, NT], F32)
        y1 = G4.a("y1", [128, NT], F32)
        oTs_all = [oTs.tok(g) for g in range(4)]; ogT_all = [ogT.tok(h) for h in range(4)]
        for c4 in range(4):
            for c2 in range(2):
                fc = c4 * 4 + c2 * 2
                wsG, wtG = wload([(w_in[:, GS + fc * 128:GS + fc * 128 + 256], 16, 0, 256), (w_in[:, GG + fc * 128:GG + fc * 128 + 256], 16, 256, 256)])
                for j in range(2):
                    for gi in range(2):
                        bg = nxt("ax")
                        cc = c2 * 2 + j
                        for kc in range(16):
                            P.pe(lambda e, kc=kc, bg=bg, gi=gi, wsG=wsG, j=j: e.matmul(bg[:, 0:NT], lhsT=wsG[:, kc, gi * 256 + j * 128:gi * 256 + j * 128 + 128], rhs=hT[:, kc, 0:NT], start=(kc == 0), stop=(kc == 15)),
                                 r=wtG + hT_all, w=CT(bg))
                        P.act(lambda e, bg=bg, gi=gi, cc=cc: e.activation(out=sg[:, cc, gi, :], in_=bg[:, 0:NT], func=ACT.Sigmoid), r=CT(bg), w=[sg.tok(cc, gi)])
            wsA, wtA = wload([(p_swa[:, c4 * 512:(c4 + 1) * 512], 8, 0, 512)])
            wsB, wtB = wload([(p_gla[:, c4 * 512:(c4 + 1) * 512], 16, 0, 512)])
            for cc in range(4):
                fc = c4 * 4 + cc
                pa = nxt("mm")
                for kc in range(8):
                    P.pe(lambda e, kc=kc, pa=pa, wsA=wsA, cc=cc: e.matmul(pa[:, 0:NT], lhsT=wsA[:, kc, cc * 128:(cc + 1) * 128], rhs=oTs[:, kc, :], start=(kc == 0), stop=(kc == 7)), r=wtA + oTs_all, w=CT(pa))
                pb = nxt("mm")
                for kc in range(16):
                    P.pe(lambda e, kc=kc, pb=pb, wsB=wsB, cc=cc: e.matmul(pb[:, 0:NT], lhsT=wsB[:, kc, cc * 128:(cc + 1) * 128], rhs=ogT[:, kc, :], start=(kc == 0), stop=(kc == 15)), r=wtB + ogT_all, w=CT(pb))
                P.dve(lambda e, pa=pa, cc=cc: e.tensor_tensor(out=y1[:, :], in0=pa[:, 0:NT], in1=sg[:, cc, 0, :], op=ALU.mult), r=CT(pa) + [sg.tok(cc, 0)], w=CT(y1))
                P.dve(lambda e, pb=pb, cc=cc: e.tensor_tensor(out=sg[:, cc, 1, :], in0=pb[:, 0:NT], in1=sg[:, cc, 1, :], op=ALU.mult), r=CT(pb) + [sg.tok(cc, 1)], w=[sg.tok(cc, 1)])
                P.dve(lambda e, fc=fc, cc=cc: e.tensor_tensor(out=yT[:, fc, :], in0=y1[:, :], in1=sg[:, cc, 1, :], op=ALU.add), r=CT(y1) + [sg.tok(cc, 1)], w=[yT.tok(fc)])
        yT_all = [yT.tok(fc) for fc in range(16)]
        R1.free()

        mark(f'b{bi} ph4 end')
        X = Group(P, B0 + (40 if sample else 72) * 1024)
        x2 = X.a("x2", [128, len(tiles), D], F32)
        G5 = Group(P, G4.off)
        hb = G5.a("hb2", [128, D], BF16); junk = G5.a("junk2", [128, D], F32)
        for t, (c0, PT) in enumerate(tiles):
            P.dma(x2[:PT, t, :], xsrc[c0:c0 + PT, :], w=[x2.tok(t)])
        for c4 in range(4):
            ws, wt = wload([(w_o[:, c4 * 512:(c4 + 1) * 512], 16, 0, 512)])

            def wo_sink(t, bank, c4=c4):
                PT = tiles[t][1]
                P.dve(lambda e: e.tensor_tensor(out=x2[:PT, t, c4 * 512:(c4 + 1) * 512], in0=bank[:PT, :], in1=x2[:PT, t, c4 * 512:(c4 + 1) * 512], op=ALU.add), r=CT(bank) + [x2.tok(t)], w=[x2.tok(t)])
            proj_tok(ws, wt, 16, 0, 512, yT, yT_all, tiles, wo_sink)
        for t, (c0, PT) in enumerate(tiles):
            norm_T(x2[:PT, t, :], [x2.tok(t)], PT, n2t, hT, c0, hb, junk)
        G4.free(); G5.free()

        mark(f'b{bi} ph5 end')
        G6 = Group(P, B0)
        aT = G6.a("aT", [128, 64, NT], BF16)
        rl = [G6.a("rl", [128, NT], F32) for _ in range(2)]
        for g16 in range(16):
            ws, wt = wload([(w_up[:, g16 * 512:(g16 + 1) * 512], 16, 0, 512)])

            def up_sink(cc, bank, g16=g16):
                fcc = g16 * 4 + cc
                r_ = rl[fcc % 2]
                P.act(lambda e: e.activation(out=r_[:, :], in_=bank[:, 0:NT], func=ACT.Relu), r=CT(bank), w=CT(r_))
                P.dve(lambda e: e.tensor_tensor(out=aT[:, fcc, :], in0=r_[:, :], in1=r_[:, :], op=ALU.mult), r=CT(r_), w=[aT.tok(fcc)])
            proj_feat(ws, wt, 16, 0, 4, hT, hT_all, NT, up_sink)
        for c4 in range(4):
            banks = [nxt("mm") for _ in tiles]
            for g4 in range(4):
                ws, wt = wload([(w_down[g4 * 2048:(g4 + 1) * 2048, c4 * 512:(c4 + 1) * 512], 16, 0, 512)])
                for t, (c0, PT) in enumerate(tiles):
                    for fc in range(16):
                        P.pe(lambda e, t=t, fc=fc, c0=c0, PT=PT, ws=ws, g4=g4: e.matmul(banks[t][:PT, :], lhsT=aT[:, g4 * 16 + fc, c0:c0 + PT], rhs=ws[:, fc, :], start=(g4 == 0 and fc == 0), stop=(g4 == 3 and fc == 15)),
                             r=wt + [aT.tok(g4 * 16 + fc)], w=CT(banks[t]))
            for t, (c0, PT) in enumerate(tiles):
                P.dve(lambda e, t=t, PT=PT, c4=c4: e.tensor_tensor(out=x2[:PT, t, c4 * 512:(c4 + 1) * 512], in0=banks[t][:PT, :], in1=x2[:PT, t, c4 * 512:(c4 + 1) * 512], op=ALU.add), r=CT(banks[t]) + [x2.tok(t)], w=[x2.tok(t)])
        G6.free()

        mark(f'b{bi} ph6 end')
        nxt_bi = BLOCKS[BLOCKS.index(bi) + 1] if BLOCKS.index(bi) + 1 < len(BLOCKS) else None
        if nxt_bi is not None:
            phase1(nxt_bi)
        G7 = Group(P, B0 + (24 if sample else 56) * 1024)
        fnb = G7.a("fnb", [128, D], F32)
        yo_ = G7.a("yo", [128, D], F32)
        P.dma(fnb[:], fnw.broadcast_to([128, D]), w=CT(fnb))
        for t, (c0, PT) in enumerate(tiles):
            rms_rstd(x2[:PT, t, :], [x2.tok(t)], PT, 2, 1.0 / D, yo_[:PT, :], CT(yo_))
            P.dve(lambda e, t=t, PT=PT: e.scalar_tensor_tensor(out=yo_[:PT, :], in0=x2[:PT, t, :], scalar=rstd[:PT, 2:3], in1=fnb[:PT, :], op0=ALU.mult, op1=ALU.mult), r=[x2.tok(t), rstd.tok(2)] + CT(fnb), w=CT(yo_))
            P.dma(ydst[c0:c0 + PT, :], yo_[:PT, :], r=CT(yo_))
        G7.free(); X.free()
        if XS is not None:
            del ring[NS:]
            XS.free()
        mark(f'b{bi} ph7 end')

    try:
        for bi in BLOCKS:
            if block(bi):
                break
    except StopBuild:
        pass
    P.emit()
    return nc, P


def _consts():
    import ml_dtypes
    bf = ml_dtypes.bfloat16
    c = {}
    c["c_ident"] = np.eye(128, dtype=np.float32).astype(bf)
    s_ = np.arange(128)[:, None]; q_ = np.arange(128)[None, :]
    mprev = np.where(s_ > q_, 0.0, NEG).astype(np.float32)
    mdiag = np.where(s_ <= q_, 0.0, NEG).astype(np.float32)
    c["c_mprev"] = np.tile(mprev, (1, 4)).astype(bf)
    c["c_mdiag"] = np.tile(mdiag, (1, 4)).astype(bf)
    tok = np.arange(64)
    msc = np.where(np.arange(128)[:, None] > (tok % 4)[None, :], 0.0, NEG).astype(np.float32)
    c["c_msc"] = np.tile(msc, (1, 4)).astype(bf)
    j_ = np.arange(64)[:, None]; i_ = tok[None, :]
    msn = np.where((j_ // 4 == i_ // 4) & (j_ % 4 <= i_ % 4), 0.0, NEG).astype(np.float32)
    msn_full = np.full((128, 256), NEG, np.float32); msn_full[:64] = np.tile(msn, (1, 4))
    c["c_msn"] = msn_full.astype(bf)
    tri = np.zeros((2, 3, 128, 128), np.float32)
    m_ = np.arange(128)[:, None]; l_ = np.arange(128)[None, :]
    tri[0, 0] = np.where(m_ <= l_, -1.0 / 16.0, 0.0); tri[0, 1] = np.where(m_ > l_, -1.0 / 16.0, 0.0); tri[0, 2] = np.where(m_ <= l_, 1.0, 0.0)
    same = (m_ // 4 == l_ // 4)
    tri[1, 0] = np.where(same & (m_ <= l_), -1.0 / 16.0, 0.0); tri[1, 1] = np.where(same & (m_ > l_), -1.0 / 16.0, 0.0); tri[1, 2] = np.where(same & (m_ <= l_), 1.0, 0.0)
    c["c_tri"] = tri
    inv = (np.float32(500000.0) ** (-np.arange(8, dtype=np.float32) * np.float32(2.0) / np.float32(16.0))).astype(np.float32)
    pos = (np.arange(16)[None, :] * 128 + np.arange(128)[:, None]).astype(np.float32)
    ang = (pos[:, :, None] * inv[None, None, :]).astype(np.float32)
    c["c_ropep"] = np.stack([np.cos(ang), np.sin(ang)], axis=1).astype(np.float32)
    poss = (PAST + (np.arange(128) % 4)).astype(np.float32)
    angs = (poss[:, None] * inv[None, :]).astype(np.float32)
    c["c_ropes"] = np.stack([np.cos(angs), np.sin(angs)], axis=1).astype(np.float32)
    cmk = (np.arange(64)[None, :] // 4 == np.arange(16)[:, None]).astype(np.float32)
    c["c_cm"] = np.broadcast_to(cmk[None], (128, 16, 64)).astype(bf)
    rmk = np.zeros((128, 16), np.float32); rmk[:64] = cmk.T
    c["c_rm"] = rmk
    ones = np.zeros((128, 2, 128), np.float32); ones[:, 0, :64] = 1.0; ones[:, 1, 64:] = 1.0
    c["c_ones"] = ones.astype(bf)
    return {k: np.ascontiguousarray(v) for k, v in c.items()}


_CACHE = {}


def kernel(x_prompt, x_sample, cache_swa_k, cache_swa_v, state_gla, norm1, w_in, w_a2, b_a,
           sink, gla_norm, p_swa, p_gla, w_o, norm2, w_up, w_down, final_norm):
    f = lambda a: np.ascontiguousarray(np.asarray(a, dtype=np.float32))
    x_prompt, x_sample = f(x_prompt), f(x_sample)
    ck, cv, stt = f(cache_swa_k)[0], f(cache_swa_v)[0], f(state_gla)[0]
    if "nc" not in _CACHE:
        _CACHE["nc"] = build_program()[0]
    nc = _CACHE["nc"]
    sk = f(sink)[0]
    sinkl = np.empty((128, 8), np.float32)
    sinkl[:64, :] = sk[0::2][None, :]; sinkl[64:, :] = sk[1::2][None, :]
    shared = dict(
        w_in=f(w_in)[0], w_a2=f(w_a2)[0], b_a=f(b_a), sinkl=sinkl, gnorm=f(gla_norm),
        p_swa=f(p_swa)[0], p_gla=f(p_gla)[0], w_o=f(w_o)[0],
        n1=np.ascontiguousarray(f(norm1)[0].reshape(16, 128).T), n2=np.ascontiguousarray(f(norm2)[0].reshape(16, 128).T),
        fnw=f(final_norm).reshape(1, D), w_up=f(w_up)[0], w_down=f(w_down)[0],
    )
    shared.update(_consts())
    in_maps = []
    for c in range(8):
        m = dict(shared)
        m["xp"] = x_prompt[c]
        m["xs"] = np.ascontiguousarray(x_sample[c * 16:(c + 1) * 16].reshape(64, D))
        m["ck"] = np.ascontiguousarray(ck[c * 16:(c + 1) * 16].reshape(16, 128, 256))
        m["cv"] = np.ascontiguousarray(cv[c * 16:(c + 1) * 16].reshape(16, 128, 256))
        m["st"] = np.ascontiguousarray(stt[c * 16:(c + 1) * 16])
        in_maps.append(m)
    res = run_bass_kernel_spmd(nc, in_maps, core_ids=list(range(8)))
    R = res.results
    y_prompt = np.stack([R[c]["yp"] for c in range(8)]).astype(np.float32)
    y_sample = np.concatenate([R[c]["ys"].reshape(16, 4, D) for c in range(8)]).astype(np.float32)
    kp = np.stack([R[c]["kpo"].reshape(128, 4, 64) for c in range(8)])[None].astype(np.float32)
    vp = np.stack([R[c]["vpo"].reshape(128, 4, 64) for c in range(8)])[None].astype(np.float32)
    sp_ = np.stack([R[c]["spo"] for c in range(8)])[None].astype(np.float32)
    ks = np.concatenate([R[c]["kso"].reshape(16, 128, 4, 64) for c in range(8)])[None].astype(np.float32)
    vs = np.concatenate([R[c]["vso"].reshape(16, 128, 4, 64) for c in range(8)])[None].astype(np.float32)
    ss = np.concatenate([R[c]["sso"] for c in range(8)])[None].astype(np.float32)
    return (y_prompt, y_sample, kp, vp, sp_, ks, vs, ss)
```

```python
import numpy as np
import concourse.bass as bass
import concourse.mybir as mybir
from concourse.bass_utils import run_bass_kernel_spmd

F32 = mybir.dt.float32
BF16 = mybir.dt.bfloat16
ALU = mybir.AluOpType
ACT = mybir.ActivationFunctionType
AX = mybir.AxisListType

ENGS = ("pe", "act", "dve", "pool", "sp")


class Buf:
    def __init__(self, name, t, off=0, nbytes=0):
        self.name = name
        self.t = t
        self.off = off
        self.nbytes = nbytes
        self.inherit = []
        self.tokens = set()

    def tok(self, *key):
        k = (self.name,) + key
        self.tokens.add(k)
        return k

    def __getitem__(self, key):
        return self.t[key]


class _Rec:
    def __init__(self):
        self.call = None

    def __getattr__(self, name):
        def f(*a, **k):
            self.call = (name, a, k)
        return f


class Prog:
    RING = {"sp": 8, "pool": 8, "act": 4}

    def __init__(self, nc):
        self.nc = nc
        self.ops = {e: [] for e in ENGS}
        self.last_w = {}
        self.readers = {}
        self.bufs = {}
        self.ndma = {e: 0 for e in self.RING}
        self.live = []
        self.dead = []
        self.sb_lo = None
        self.psum_names = set()
        self.nop = 0

    def sb_init(self, lo, hi):
        self.sb_lo, self.sb_hi = (lo + 63) // 64 * 64, hi

    def sb_alloc(self, name, shape, dtype, off):
        nb = int(np.prod(shape[1:])) * mybir.dt.size(dtype)
        assert off % 32 == 0, (name, off)
        assert self.sb_lo + off + nb <= self.sb_hi, (name, off, nb, self.sb_hi - self.sb_lo)
        for (o, e, b) in self.live:
            assert off + nb <= o or off >= e, f"{name} overlaps live {b.name}"
        t = self.nc.alloc_sbuf_tensor_at(name, list(shape), dtype, offset=self.sb_lo + off)
        b = Buf(name, t, off, nb)
        inh = []
        for (o, e, d) in self.dead:
            if not (off + nb <= o or off >= e):
                for tk in d.tokens:
                    if tk in self.last_w:
                        inh.append(self.last_w[tk])
                    inh.extend(self.readers.get(tk, []))
                inh.extend(d.inherit)
        b.inherit = list(set(inh))
        self.live.append((off, off + nb, b))
        self.bufs[name] = b
        return b

    def sb_free(self, *bufs):
        for b in bufs:
            ent = [x for x in self.live if x[2] is b]
            assert ent, b.name
            self.live.remove(ent[0])
            self.dead.append(ent[0])

    def psum(self, name, shape, dtype=F32):
        t = self.nc.alloc_psum_tensor(name, list(shape), dtype)
        b = Buf(name, t)
        self.bufs[name] = b
        self.psum_names.add(name)
        return b

    def _deps(self, eng, r, w):
        deps = set()
        for tk in r:
            self._touch(tk)
            if tk in self.last_w:
                deps.add(self.last_w[tk] + ("raw",))
        for tk in w:
            self._touch(tk)
            port = tk == ("PSUMPORT",)
            if tk in self.last_w:
                deps.add(self.last_w[tk] + ("port" if port else "waw",))
            for rd in self.readers.get(tk, []):
                deps.add(rd + ("war",))
        return deps

    def _touch(self, tk):
        if tk not in self.last_w and tk not in self.readers:
            b = self.bufs.get(tk[0])
            if b is not None and b.inherit:
                self.readers[tk] = list(b.inherit)

    def op(self, eng, fn, r=(), w=(), dma=False):
        r = [x for x in r if x is not None]
        w = [x for x in w if x is not None]
        if eng in ("act", "dve") and any(tk[0] in self.psum_names for tk in r):
            w = w + [("PSUMPORT",)]
        deps = self._deps(eng, r, w)
        idx = len(self.ops[eng])
        ref = (eng, idx)
        dmaj = None
        if dma:
            dmaj = self.ndma[eng]
            self.ndma[eng] += 1
        rec = _Rec()
        fn(rec)
        assert rec.call is not None
        self.ops[eng].append(dict(fn=rec.call, deps=deps, dma=dmaj, sig=False))
        for tk in w:
            self.last_w[tk] = ref
            self.readers[tk] = []
        for tk in r:
            self.readers.setdefault(tk, []).append(ref)
        return ref

    def pe(self, fn, r=(), w=()):
        return self.op("pe", fn, r, w)

    def act(self, fn, r=(), w=()):
        return self.op("act", fn, r, w)

    def dve(self, fn, r=(), w=()):
        return self.op("dve", fn, r, w)

    def pool(self, fn, r=(), w=()):
        return self.op("pool", fn, r, w)

    def dma(self, out, in_, r=(), w=(), q="sp", **kw):
        return self.op(q, lambda e: e.dma_start(out=out, in_=in_, **kw), r, w, dma=True)

    def _needs_wait(self, eng, idx, dep):
        deng, didx, kind = dep
        dop = self.ops[deng][didx]
        if dop["dma"] is not None:
            return True
        if deng != eng:
            return True
        if eng == "pe":
            return False
        if kind in ("war", "port"):
            return False
        return (idx - didx) <= 1

    def emit(self):
        nc = self.nc
        sem_eng = {e: nc.alloc_semaphore(f"s_{e}") for e in ("pe", "act", "dve", "pool")}
        ring = {q: [nc.alloc_semaphore(f"r_{q}{i}") for i in range(n)] for q, n in self.RING.items()}
        for eng in ENGS:
            for idx, o in enumerate(self.ops[eng]):
                keep = set()
                for dep in o["deps"]:
                    if self._needs_wait(eng, idx, dep):
                        keep.add(dep[:2])
                        d = self.ops[dep[0]][dep[1]]
                        if d["dma"] is None:
                            d["sig"] = True
                o["waits"] = keep
        for eng in ENGS:
            cnt = 0
            for o in self.ops[eng]:
                if o["dma"] is not None:
                    n = self.RING[eng]
                    j = o["dma"]
                    o["done"] = (ring[eng][j % n], 16 * (j // n + 1))
                    o["pre"] = (ring[eng][j % n], 16 * (j // n)) if j >= n else None
                elif o["sig"]:
                    cnt += 1
                    o["done"] = (sem_eng[eng], cnt)
        engobj = {"pe": "tensor", "act": "scalar", "dve": "vector", "pool": "gpsimd", "sp": "sync"}
        self.final_waits = {}

        def run(eng):
            def body(e):
                waited = {}
                ops = self.ops[eng]
                for o in ops:
                    ws = {}
                    for (deng, didx) in o["waits"]:
                        s, v = self.ops[deng][didx]["done"]
                        ws[s] = max(ws.get(s, 0), v)
                    if o["dma"] is not None and o["pre"] is not None:
                        s, v = o["pre"]
                        ws[s] = max(ws.get(s, 0), v)
                    for s, v in ws.items():
                        if waited.get(s, 0) < v:
                            e.wait_ge(s, v)
                            waited[s] = v
                    ins = getattr(e, o["fn"][0])(*o["fn"][1], **o["fn"][2])
                    if o["dma"] is not None:
                        s, v = o["done"]
                        ins.then_inc(s, 16)
                    elif o["sig"]:
                        ins.then_inc(sem_eng[eng], 1)
                if eng in self.RING:
                    last = {}
                    for o in ops:
                        if o["dma"] is not None:
                            s, v = o["done"]
                            last[s] = max(last.get(s, 0), v)
                    for s, v in last.items():
                        if waited.get(s, 0) < v:
                            e.wait_ge(s, v)
            return body

        with nc.Block() as block:
            for eng in ENGS:
                if not self.ops[eng]:
                    continue
                getattr(block, engobj[eng])(run(eng))

    def stats(self):
        return {e: len(self.ops[e]) for e in ENGS}

D = 2048
DIN = 11792
DFF = 8192
PAST = 8192
QS, KS, VS, QG, KG, VG, RG, AG, GS, GG = 0, 1024, 1280, 1536, 2560, 3584, 5632, 7680, 7696, 9744
EPS = 1e-6
NEG = -30000.0
NS = 3


class Group:
    cnt = [0]

    def __init__(s, P, off):
        s.P, s.off, s.bufs = P, (off + 63) // 64 * 64, []

    def a(s, name, shape, dt):
        Group.cnt[0] += 1
        b = s.P.sb_alloc(f"{name}_{Group.cnt[0]}", shape, dt, s.off)
        s.off += (b.nbytes + 63) // 64 * 64
        s.bufs.append(b)
        return b

    def free(s):
        for b in s.bufs:
            s.P.sb_free(b)
        s.bufs = []


def build_program(dbg=False):
    import os
    STOP = os.environ.get('KSTOP', '')
    BLOCKS = [int(c) for c in os.environ.get('KBLOCKS', '01234')]
    KSUB = int(os.environ.get('KSUB', '0'))
    ck_n = [0]

    class StopBuild(Exception):
        pass

    def ckpt(tag=''):
        ck_n[0] += 1
        if KSUB and ck_n[0] == KSUB:
            print('STOP at ckpt', ck_n[0], tag)
            raise StopBuild()
    nc = bass.Bass("TRN2", target_bir_lowering=False)
    P = Prog(nc)
    P.sb_init(nc.sbuf_base, nc.sbuf_top)
    P.marks = []

    def mark(lab):
        P.marks.append((lab, sum(1 for o in P.ops['pe'] if o['fn'][0] == 'matmul')))
    al = Group(P, 0)

    def din(name, shape, dt=F32):
        return nc.dram_tensor(name, list(shape), dt, kind="ExternalInput").ap()

    def dout(name, shape):
        return nc.dram_tensor(name, list(shape), F32, kind="ExternalOutput").ap()

    xp = din("xp", [2048, D]); xs = din("xs", [64, D])
    ck = din("ck", [16, 128, 256]); cv = din("cv", [16, 128, 256])
    st = din("st", [16, 4, 256, 512])
    w_in = din("w_in", [D, DIN]); w_a2 = din("w_a2", [16, 1024]); b_a = din("b_a", [1, 1024])
    sinkl = din("sinkl", [128, 8]); gnorm = din("gnorm", [1, 512])
    p_swa = din("p_swa", [1024, D]); p_gla = din("p_gla", [D, D]); w_o = din("w_o", [D, D])
    n1 = din("n1", [128, 16]); n2 = din("n2", [128, 16]); fnw = din("fnw", [1, D])
    w_up = din("w_up", [D, DFF]); w_down = din("w_down", [DFF, D])
    c_ident = din("c_ident", [128, 128], BF16)
    c_mprev = din("c_mprev", [128, 512], BF16); c_mdiag = din("c_mdiag", [128, 512], BF16)
    c_msc = din("c_msc", [128, 256], BF16); c_msn = din("c_msn", [128, 256], BF16)
    c_tri = din("c_tri", [2, 3, 128, 128])
    c_ropep = din("c_ropep", [128, 2, 16, 8]); c_ropes = din("c_ropes", [128, 2, 8])
    c_cm = din("c_cm", [128, 16, 64], BF16); c_rm = din("c_rm", [128, 16])
    c_ones = din("c_ones", [128, 2, 128], BF16)

    yp = dout("yp", [2048, D]); ys = dout("ys", [64, D])
    kpo = dout("kpo", [128, 256]); vpo = dout("vpo", [128, 256]); spo = dout("spo", [4, 256, 512])
    kso = dout("kso", [16, 128, 256]); vso = dout("vso", [16, 128, 256]); sso = dout("sso", [16, 4, 256, 512])

    pools = {
        "mm": [P.psum(f"mm{i}", [128, 512], F32) for i in range(4)],
        "tp": [P.psum(f"tp{i}", [128, 1024], BF16) for i in range(2)],
        "ax": [P.psum(f"ax{i}", [128, 512], F32) for i in range(2)],
    }
    pcnt = {k: 0 for k in pools}

    def nxt(pool):
        b = pools[pool][pcnt[pool] % len(pools[pool])]
        pcnt[pool] += 1
        return b

    ident = al.a("ident", [128, 128], BF16)
    mprev = al.a("mprev", [128, 512], BF16); mdiag = al.a("mdiag", [128, 512], BF16)
    msc = al.a("msc", [128, 256], BF16); msn = al.a("msn", [128, 256], BF16)
    tri = al.a("tri", [128, 6, 128], F32)
    ropep = al.a("ropep", [128, 2, 16, 8], F32); ropes = al.a("ropes", [128, 2, 8], F32)
    cm = al.a("cm", [128, 16, 64], BF16); rm = al.a("rm", [128, 16], F32)
    ones = al.a("ones", [128, 2, 128], BF16)
    esink = al.a("esink", [128, 8], F32)
    gnb = al.a("gnb", [128, 512], F32)
    n1t = al.a("n1t", [128, 16], F32); n2t = al.a("n2t", [128, 16], F32)
    wa2 = al.a("wa2", [32, 1024], BF16)
    ssq = al.a("ssq", [128, 4], F32); rstd = al.a("rstd", [128, 4], F32)
    S = al.a("S", [128, 8, 512], F32)
    hT = al.a("hT", [128, 16, 512], BF16)
    wslots = [al.a(f"ws{i}", [128, 16, 512], BF16) for i in range(NS)]
    kT = [al.a("kT", [64, 4, 128], BF16) for _ in range(2)]
    vAB = [al.a("vAB", [128, 4, 2, 128], BF16) for _ in range(2)]
    kTn, vnAB = kT[0], vAB[0]
    B0 = al.off
    print('B0', B0, 'arena', P.sb_hi - P.sb_lo)
    T0 = Group(P, B0)
    wa2f = T0.a("wa2f", [32, 1024], F32)
    CT = lambda b: [b.tok()]

    for dst, src in ((ident, c_ident), (mprev, c_mprev), (mdiag, c_mdiag), (msc, c_msc), (msn, c_msn),
                     (ropep, c_ropep), (ropes, c_ropes), (cm, c_cm), (rm, c_rm), (ones, c_ones),
                     (n1t, n1), (n2t, n2)):
        P.dma(dst[:], src, w=CT(dst))
    P.dma(tri[:], c_tri.rearrange("a b p l -> p (a b) l"), w=CT(tri))
    P.dma(esink[:], sinkl, w=CT(esink))
    P.act(lambda e: e.activation(out=esink[:], in_=esink[:], func=ACT.Exp), r=CT(esink), w=CT(esink))
    P.dma(gnb[:], gnorm.broadcast_to([128, 512]), w=CT(gnb))
    P.dve(lambda e: e.memset(wa2f[:], 0.0), w=CT(wa2f))
    P.dma(wa2f[0:16, :], w_a2, w=CT(wa2f))
    P.dma(wa2f[16:17, :], b_a, w=CT(wa2f))
    P.dve(lambda e: e.tensor_copy(out=wa2[:], in_=wa2f[:]), r=CT(wa2f), w=CT(wa2))
    T0.free()
    for v_ in vAB:
        P.dve(lambda e, v_=v_: e.memset(v_[:], 0.0), w=CT(v_))

    wuse = [0]
    ring = list(wslots)
    wblk = [0]
    wvis = {}
    wscr = {}

    def wload(parts):
        ws = ring[wuse[0] % len(ring)]
        wuse[0] += 1
        k = wblk[0]
        wblk[0] += 1
        visit = wvis.get(k, 0)
        wvis[k] = visit + 1
        store_visit = 0 if k % 4 == 0 else 1
        if k not in wscr:
            wscr[k] = nc.dram_tensor(f"wscr{k}", [128, 16 * 512], BF16).ap()
        scr = wscr[k]
        nk = parts[0][1]
        ctot = sum(p_[3] for p_ in parts)
        assert all(p_[1] == nk for p_ in parts) and parts[0][2] == 0
        toks = [ws.tok(j) for j in range(len(parts))]
        flat = ws[:].rearrange("p k c -> p (k c)")[:, 0:nk * ctot]
        view = flat.rearrange("p (k c) -> p k c", c=ctot)
        if visit <= store_visit:
            for j, (src, nk_, c0, C) in enumerate(parts):
                P.dma(view[:, :, c0:c0 + C], src.rearrange("(kc p) c -> p kc c", p=128), w=[ws.tok(j)], q="pool")
            if visit == store_visit:
                P.dma(scr[:, 0:nk * ctot], flat, r=toks, w=[("wscr", k)])
        else:
            P.dma(flat, scr[:, 0:nk * ctot], r=[("wscr", k)], w=toks, q="pool")
        return view, toks

    def rms_rstd(src_ap, src_tok, PT, col, scale, junk, junk_tok):
        P.act(lambda e: e.activation(out=junk, in_=src_ap, func=ACT.Square, accum_out=ssq[:PT, col:col + 1]),
              r=src_tok, w=[ssq.tok(col)] + junk_tok)
        P.act(lambda e: e.activation(out=rstd[:PT, col:col + 1], in_=ssq[:PT, col:col + 1], func=ACT.Ln, scale=scale, bias=EPS),
              r=[ssq.tok(col)], w=[rstd.tok(col)])
        P.act(lambda e: e.activation(out=rstd[:PT, col:col + 1], in_=rstd[:PT, col:col + 1], func=ACT.Exp, scale=-0.5),
              r=[rstd.tok(col)], w=[rstd.tok(col)])

    def norm_T(x_ap, x_tok, PT, nwt, dstT, c0, hb, junk):
        rms_rstd(x_ap, x_tok, PT, 0, 1.0 / D, junk[:PT, :], CT(junk))
        P.dve(lambda e: e.tensor_scalar(out=hb[:PT, :], in0=x_ap, scalar1=rstd[:PT, 0:1], scalar2=None, op0=ALU.mult),
              r=x_tok + [rstd.tok(0)], w=CT(hb))
        for half in range(2):
            bank = nxt("tp")
            for j in range(8):
                kc = half * 8 + j
                P.pe(lambda e, kc=kc, j=j, bank=bank: e.transpose(out=bank[:, j * PT:(j + 1) * PT], in_=hb[:PT, kc * 128:(kc + 1) * 128], identity=ident[:PT, :PT]),
                     r=CT(hb) + CT(ident), w=CT(bank))
            for j in range(8):
                kc = half * 8 + j
                if half == 0:
                    P.act(lambda e, kc=kc, j=j, bank=bank: e.activation(out=dstT[:, kc, c0:c0 + PT], in_=bank[:, j * PT:(j + 1) * PT], func=ACT.Copy, scale=nwt[:, kc:kc + 1]),
                          r=CT(bank) + CT(nwt), w=[dstT.tok(kc)])
                else:
                    P.dve(lambda e, kc=kc, j=j, bank=bank: e.tensor_scalar(out=dstT[:, kc, c0:c0 + PT], in0=bank[:, j * PT:(j + 1) * PT], scalar1=nwt[:, kc:kc + 1], scalar2=None, op0=ALU.mult),
                          r=CT(bank) + CT(nwt), w=[dstT.tok(kc)])

    hT_all = [hT.tok(kc) for kc in range(16)]

    def proj_tok(ws, wt, nk, wc0, wC, actT, act_tok, tiles, sink):
        for t, (c0, PT) in enumerate(tiles):
            bank = nxt("mm")
            for kc in range(nk):
                P.pe(lambda e, kc=kc, bank=bank, c0=c0, PT=PT: e.matmul(bank[:PT, 0:wC], lhsT=actT[:, kc, c0:c0 + PT], rhs=ws[:, kc, wc0:wc0 + wC], start=(kc == 0), stop=(kc == nk - 1)),
                     r=wt + act_tok, w=CT(bank))
            sink(t, bank)

    def proj_feat(ws, wt, nk, wc0, nchunk, actT, act_tok, NT, sink, pool="mm", M=128):
        for cc in range(nchunk):
            bank = nxt(pool)
            for kc in range(nk):
                P.pe(lambda e, kc=kc, bank=bank, cc=cc: e.matmul(bank[:M, 0:NT], lhsT=ws[:, kc, wc0 + cc * 128:wc0 + cc * 128 + M], rhs=actT[:, kc, 0:NT], start=(kc == 0), stop=(kc == nk - 1)),
                     r=wt + act_tok, w=CT(bank))
            sink(cc, bank)

    def phase1(bi):
        sample = bi == 4
        NT = 64 if sample else 512
        tiles = [(0, 64)] if sample else [(i * 128, 128) for i in range(4)]
        xsrc = xs if sample else xp[bi * 512:(bi + 1) * 512, :]
        r1off = B0 + (48 * NT + 63) // 64 * 64
        T = Group(P, r1off)
        xt = [T.a("xt", [128, D], F32) for _ in range(2)]
        hb = T.a("hb", [128, D], BF16)
        junk = T.a("junk", [128, D], F32)
        for t, (c0, PT) in enumerate(tiles):
            x_ = xt[t % 2]
            P.dma(x_[:PT, :], xsrc[c0:c0 + PT, :], w=CT(x_), q="pool")
            norm_T(x_[:PT, :], CT(x_), PT, n1t, hT, c0, hb, junk)
        T.free()

    def block(bi):
        sample = bi == 4
        wblk[0] = 0
        NT = 64 if sample else 512
        tiles = [(0, 64)] if sample else [(i * 128, 128) for i in range(4)]
        PTm = tiles[0][1]
        xsrc = xs if sample else xp[bi * 512:(bi + 1) * 512, :]
        ydst = ys if sample else yp[bi * 512:(bi + 1) * 512, :]
        ci = 1 if sample else 0
        trin, uu, caus = tri[:PTm, 3 * ci, :PTm], tri[:PTm, 3 * ci + 1, :PTm], tri[:PTm, 3 * ci + 2, :PTm]
        R1 = Group(P, B0)
        oTs = R1.a("oTs", [128, 8, NT], BF16)
        ogT = R1.a("ogT", [128, 16, NT], BF16)

        mark(f'b{bi} ph0 end')
        if bi == BLOCKS[0]:
            phase1(bi)

        mark(f'b{bi} ph1 end')
        T = Group(P, R1.off)
        z32 = [T.a("z32", [128, 512], F32) for _ in range(2)]
        rt = T.a("rt", [128, 4, 64], F32)
        qr = T.a("qr", [128, len(tiles), 1024], BF16)
        kr = T.a("kr", [128, len(tiles), 256], BF16)
        qT = [T.a("qT", [64, 16, 128], BF16) for _ in range(2)]
        PTb = [T.a("PTb", [128, 2, 512], BF16) for _ in range(2)]
        dtmp = T.a("dtmp", [128, 2, 128], F32)
        vst = T.a("vst", [128, len(tiles), 256], BF16)
        zc = [0]

        def rope_inplace(zb, PT, nh, t):
            v = zb[:PT, 0:nh * 64].rearrange("p (h d) -> p h d", d=64)
            x1, x2 = v[:, :, 0:8], v[:, :, 8:16]
            if sample:
                cos, sin = ropes[:PT, 0, :], ropes[:PT, 1, :]
            else:
                cos, sin = ropep[:PT, 0, bi * 4 + t, :], ropep[:PT, 1, bi * 4 + t, :]
            cb = cos.unsqueeze(1).to_broadcast([PT, nh, 8]); sb = sin.unsqueeze(1).to_broadcast([PT, nh, 8])
            T = [rt[:PT, i, 0:nh * 8].rearrange("p (h d) -> p h d", d=8) for i in range(4)]
            zt_, rtt = CT(zb), CT(rt)
            rr = zt_ + [ropes.tok() if sample else ropep.tok()]
            P.dve(lambda e: e.tensor_tensor(out=T[0], in0=x1, in1=cb, op=ALU.mult), r=rr, w=rtt)
            P.dve(lambda e: e.tensor_tensor(out=T[1], in0=x2, in1=sb, op=ALU.mult), r=rr, w=rtt)
            P.dve(lambda e: e.tensor_tensor(out=T[2], in0=x2, in1=cb, op=ALU.mult), r=rr, w=rtt)
            P.dve(lambda e: e.tensor_tensor(out=T[3], in0=x1, in1=sb, op=ALU.mult), r=rr, w=rtt)
            P.dve(lambda e: e.tensor_tensor(out=x1, in0=T[0], in1=T[1], op=ALU.subtract), r=rtt, w=zt_)
            P.dve(lambda e: e.tensor_tensor(out=x2, in0=T[2], in1=T[3], op=ALU.add), r=rtt, w=zt_)

        def q_sink(slot):
            def f(t, bank):
                PT = tiles[t][1]
                zb = z32[zc[0] % 2]; zc[0] += 1
                P.act(lambda e: e.activation(out=zb[:PT, :], in_=bank[:PT, :], func=ACT.Copy), r=CT(bank), w=CT(zb))
                rope_inplace(zb, PT, 8, t)
                P.dve(lambda e: e.tensor_copy(out=qr[:PT, t, slot * 512:(slot + 1) * 512], in_=zb[:PT, :]), r=CT(zb), w=[qr.tok(t, slot)])
            return f

        def kv_sink(t, bank):
            PT = tiles[t][1]
            gt = bi * 4 + t
            zb = z32[zc[0] % 2]; zc[0] += 1
            P.act(lambda e: e.activation(out=zb[:PT, :], in_=bank[:PT, :], func=ACT.Copy), r=CT(bank), w=CT(zb))
            rope_inplace(zb, PT, 4, t)
            P.dve(lambda e: e.tensor_copy(out=kr[:PT, t, :], in_=zb[:PT, 0:256]), r=CT(zb), w=[kr.tok(t)])
            P.dve(lambda e: e.tensor_copy(out=vst[:PT, t, :], in_=zb[:PT, 256:512]), r=CT(zb), w=[vst.tok(t)])
            if sample:
                for b in range(16):
                    P.dma(kso[b, 124:128, :], zb[4 * b:4 * b + 4, 0:256], r=CT(zb))
                    P.dma(vso[b, 124:128, :], zb[4 * b:4 * b + 4, 256:512], r=CT(zb))
            elif gt == 15:
                P.dma(kpo, zb[:, 0:256], r=CT(zb))
                P.dma(vpo, zb[:, 256:512], r=CT(zb))

        for slot in range(2):
            ws, wt = wload([(w_in[:, QS + slot * 512:QS + (slot + 1) * 512], 16, 0, 512)])
            proj_tok(ws, wt, 16, 0, 512, hT, hT_all, tiles, q_sink(slot))
        ckpt('q proj')
        ws, wt = wload([(w_in[:, KS:KS + 512], 16, 0, 512)])
        proj_tok(ws, wt, 16, 0, 512, hT, hT_all, tiles, kv_sink)
        ckpt('kv proj')

        if sample:
            ckb = T.a("ckb", [128, 16, 256], BF16); cvb = T.a("cvb", [128, 16, 256], BF16)
            kTc = T.a("kTc", [64, 16, 4, 128], BF16)
            vcAB = T.a("vcAB", [128, 64, 2, 128], BF16)
            P.dma(ckb[:], ck.rearrange("b c f -> c b f"), w=CT(ckb), q="pool")
            P.dma(cvb[:], cv.rearrange("b c f -> c b f"), w=CT(cvb), q="pool")
            for b in range(16):
                P.dma(kso[b, 0:124, :], ck[b, 4:128, :])
                P.dma(vso[b, 0:124, :], cv[b, 4:128, :])
            P.dve(lambda e: e.memset(vcAB[:], 0.0), w=CT(vcAB))
            cvv = cvb[:].rearrange("p b (g d) -> p (b g) d", d=64)
            P.dve(lambda e: e.tensor_copy(out=vcAB[:, :, 0, 0:64], in_=cvv), r=CT(cvb), w=CT(vcAB))
            P.dve(lambda e: e.tensor_copy(out=vcAB[:, :, 1, 64:128], in_=cvv), r=CT(cvb), w=CT(vcAB))
            for b2 in range(8):
                bank = nxt("tp")
                for j in range(8):
                    b, g = (b2 * 8 + j) // 4, (b2 * 8 + j) % 4
                    P.pe(lambda e, b=b, g=g, j=j, bank=bank: e.transpose(out=bank[:64, j * 128:(j + 1) * 128], in_=ckb[:, b, g * 64:(g + 1) * 64], identity=ident[:]),
                         r=CT(ckb) + CT(ident), w=CT(bank))
                src = bank[:64, :].rearrange("p (b g c) -> p b g c", g=4, c=128)
                if b2 % 2 == 0:
                    P.act(lambda e, b2=b2, src=src: e.activation(out=kTc[:, 2 * b2:2 * b2 + 2, :, :], in_=src, func=ACT.Copy), r=CT(bank), w=CT(kTc))
                else:
                    P.dve(lambda e, b2=b2, src=src: e.tensor_copy(out=kTc[:, 2 * b2:2 * b2 + 2, :, :], in_=src), r=CT(bank), w=CT(kTc))

        for t, (c0, PT) in enumerate(tiles):
            gt = bi * 4 + t
            qT_ = qT[t % 2]
            kT_ = kT[gt % 2] if not sample else kTn
            vs_ = vAB[gt % 2] if not sample else vnAB
            vv = vst[:PT, t, :].rearrange("p (g d) -> p g d", d=64)
            P.dve(lambda e: e.tensor_copy(out=vs_[:PT, :, 0, 0:64], in_=vv), r=[vst.tok(t)], w=CT(vs_))
            P.dve(lambda e: e.tensor_copy(out=vs_[:PT, :, 1, 64:128], in_=vv), r=[vst.tok(t)], w=CT(vs_))
            for half in range(2):
                bank = nxt("tp")
                for j in range(8):
                    hh = half * 8 + j
                    P.pe(lambda e, hh=hh, j=j, bank=bank: e.transpose(out=bank[:64, j * PT:(j + 1) * PT], in_=qr[:PT, t, hh * 64:(hh + 1) * 64], identity=ident[:PT, :PT]),
                         r=[qr.tok(t, hh // 8)] + CT(ident), w=CT(bank))
                src = bank[:64, 0:8 * PT].rearrange("p (h q) -> p h q", q=PT)
                if half == 0:
                    P.act(lambda e, src=src, half=half: e.activation(out=qT_[:, 0:8, :PT], in_=src, func=ACT.Copy), r=CT(bank), w=CT(qT_))
                else:
                    P.dve(lambda e, src=src, half=half: e.tensor_copy(out=qT_[:, 8:16, :PT], in_=src), r=CT(bank), w=CT(qT_))
            bank = nxt("tp")
            for g in range(4):
                P.pe(lambda e, g=g, bank=bank: e.transpose(out=bank[:64, g * PT:(g + 1) * PT], in_=kr[:PT, t, g * 64:(g + 1) * 64], identity=ident[:PT, :PT]),
                     r=[kr.tok(t)] + CT(ident), w=CT(bank))
            P.dve(lambda e, bank=bank: e.tensor_copy(out=kT_[:, :, :PT], in_=bank[:64, 0:4 * PT].rearrange("p (g s) -> p g s", s=PT)), r=CT(bank), w=CT(kT_))

            mmb2, axb2 = pools["mm"], pools["ax"]

            def S_(g):
                PT_ = PTb[g % 2]
                rq = qT_[:, 4 * g:4 * g + 4, :PT]
                sb = (mmb2[0], mmb2[1]) if g % 2 == 0 else (mmb2[2], mmb2[3])
                if not sample:
                    kinds = ([(0, kT[(gt - 1) % 2], mprev)] if gt > 0 else []) + [(1, kT_, mdiag)]
                    for (kd_, ksrc, msk) in kinds:
                        bank = sb[kd_]
                        P.pe(lambda e: e.matmul(bank[:, :], lhsT=ident[:], rhs=msk[:], start=True, stop=False), r=CT(ident) + CT(msk), w=CT(bank))
                        P.pe(lambda e: e.matmul(bank[:, :], lhsT=ksrc[:, g, :], rhs=rq, start=False, stop=True), r=CT(ksrc) + CT(qT_), w=CT(bank))
                        P.act(lambda e: e.activation(out=PT_[:, kd_, :], in_=bank[:, :], func=ACT.Exp, scale=0.125), r=CT(bank), w=[PT_.tok(kd_)])
                else:
                    bank = sb[0]
                    P.pe(lambda e: e.matmul(bank[:, 0:256], lhsT=ident[:], rhs=msc[:], start=True, stop=False), r=CT(ident) + CT(msc), w=CT(bank))
                    for b in range(16):
                        P.pe(lambda e, b=b: e.matmul(bank[:, 0:256].rearrange("p (h q) -> p h q", q=64)[:, :, 4 * b:4 * b + 4], lhsT=kTc[:, b, g, :], rhs=qT_[:, 4 * g:4 * g + 4, 4 * b:4 * b + 4], start=False, stop=(b == 15)),
                             r=CT(kTc) + CT(qT_), w=CT(bank))
                    P.act(lambda e: e.activation(out=PT_[:, 0, 0:256], in_=bank[:, 0:256], func=ACT.Exp, scale=0.125), r=CT(bank), w=[PT_.tok(0)])
                    bank2 = sb[1]
                    P.pe(lambda e: e.matmul(bank2[:64, 0:256], lhsT=ident[:64, :64], rhs=msn[:64, :], start=True, stop=False), r=CT(ident) + CT(msn), w=CT(bank2))
                    P.pe(lambda e: e.matmul(bank2[:64, 0:256], lhsT=kT_[:, g, :64], rhs=rq, start=False, stop=True), r=CT(kT_) + CT(qT_), w=CT(bank2))
                    P.act(lambda e: e.activation(out=PT_[:64, 1, 0:256], in_=bank2[:64, 0:256], func=ACT.Exp, scale=0.125), r=CT(bank2), w=[PT_.tok(1)])

            def PV_(g):
                PT_ = PTb[g % 2]
                pv = axb2[g % 2]
                pvv = pv[:, :].rearrange("p (r o q) -> p r o q", r=2, o=2)
                ptoks = [PT_.tok(0), PT_.tok(1)]
                for pr in range(2):
                    for od in range(2):
                        mms = []
                        if not sample:
                            for (kd_, vsrc) in ([(0, vAB[(gt - 1) % 2])] if gt > 0 else []) + [(1, vAB[gt % 2])]:
                                for ab in range(2):
                                    lh = vsrc[:, g, ab, :] if od == 0 else ones[:, ab, :]
                                    mms.append((pvv[:, pr, od, :], lh, PT_[:, kd_, (2 * pr + ab) * 128:(2 * pr + ab + 1) * 128], CT(vsrc)))
                        else:
                            for ab in range(2):
                                lh = vnAB[:64, g, ab, :] if od == 0 else ones[:64, ab, :]
                                mms.append((pvv[:, pr, od, 0:64], lh, PT_[:64, 1, (2 * pr + ab) * 64:(2 * pr + ab + 1) * 64], CT(vnAB)))
                            for b in range(16):
                                for ab in range(2):
                                    lh = vcAB[:, b * 4 + g, ab, :] if od == 0 else ones[:, ab, :]
                                    c_ = (2 * pr + ab) * 64 + 4 * b
                                    mms.append((pvv[:, pr, od, 4 * b:4 * b + 4], lh, PT_[:, 0, c_:c_ + 4], CT(vcAB)))
                        for i, (o_, l_, r_, tk) in enumerate(mms):
                            P.pe(lambda e, o_=o_, l_=l_, r_=r_, i=i, n=len(mms): e.matmul(o_, lhsT=l_, rhs=r_, start=(i == 0), stop=(i == n - 1)),
                                 r=ptoks + tk + CT(ones), w=CT(pv))

            def EV_(g):
                pv = axb2[g % 2]
                pvv = pv[:, :].rearrange("p (r o q) -> p r o q", r=2, o=2)
                for pr in range(2):
                    P.dve(lambda e, pr=pr: e.tensor_scalar(out=dtmp[:, pr, :PT], in0=pvv[:, pr, 1, :PT], scalar1=esink[:, 2 * g + pr:2 * g + pr + 1], scalar2=None, op0=ALU.add),
                          r=CT(pv) + CT(esink), w=CT(dtmp))
                P.dve(lambda e: e.reciprocal(out=dtmp[:, :, :PT], in_=dtmp[:, :, :PT]), r=CT(dtmp), w=CT(dtmp))
                P.dve(lambda e: e.tensor_tensor(out=oTs[:, 2 * g:2 * g + 2, c0:c0 + PT], in0=pvv[:, :, 0, :PT], in1=dtmp[:, :, :PT], op=ALU.mult),
                      r=CT(pv) + CT(dtmp), w=[oTs.tok(g)])

            S_(0)
            for g in range(4):
                if g < 3:
                    S_(g + 1)
                PV_(g)
                EV_(g)
        T.free()

        mark(f'b{bi} ph2 end')
        T = Group(P, R1.off)
        agT = T.a("agT", [32, NT], BF16)
        qgT = [T.a("qgT", [128, 2, NT], BF16) for _ in range(2)]; kgT = [T.a("kgT", [128, 2, NT], BF16) for _ in range(2)]
        kg = [T.a("kg", [128, len(tiles), 256], BF16) for _ in range(2)]
        vg = [T.a("vg", [128, len(tiles), 512], BF16) for _ in range(2)]
        srg = [T.a("srg", [128, len(tiles), 512], BF16) for _ in range(2)]
        e1 = T.a("e1", [128, 256], F32); sp = T.a("sp", [128, 256], F32)
        epos = [T.a("epos", [128, 2, 128], F32) for _ in range(2)]; eneg = T.a("eneg", [128, 2, 128], F32); ed = T.a("ed", [128, 256], F32)
        qiT = [T.a("qiT", [128, 2, 128], BF16) for _ in range(2)]; kiT = [T.a("kiT", [128, 2, 128], BF16) for _ in range(2)]; kd = [T.a("kd", [128, 256], BF16) for _ in range(2)]
        ATm = [T.a("ATm", [128, 128], BF16) for _ in range(2)]
        Sbf = [T.a("Sbf", [128, 2, 512], BF16) for _ in range(2)]
        ogf = T.a("ogf", [128, 512], F32); ogb = T.a("ogb", [128, 512], BF16)
        gjunk = T.a("gjunk", [128, 512], F32)
        if sample:
            qm = T.a("qm", [128, 2, 64], BF16); kdm = T.a("kdm", [64, 256], BF16)
            S0 = [T.a("S0", [128, 2, 512], F32) for _ in range(3)]
            S0b = [T.a("S0b", [128, 2, 512], BF16) for _ in range(2)]
            Sn = [T.a("Sn", [128, 2, 512], F32) for _ in range(2)]

        ws, wt = wload([(w_in[:, AG:AG + 16], 16, 0, 16)])
        P.dve(lambda e: e.memset(agT[:], 1.0), w=CT(agT))

        def ag_sink(cc, bank):
            P.act(lambda e: e.activation(out=agT[0:16, :], in_=bank[0:16, 0:NT], func=ACT.Copy), r=CT(bank), w=CT(agT))
        proj_feat(ws, wt, 16, 0, 1, hT, hT_all, NT, ag_sink, pool="ax", M=16)

        mmb = pools["mm"]
        pj_cnt = [0]

        def make_proj_items(h):
            hp = h % 2
            items = []
            ws1, wt1 = wload([(w_in[:, QG + h * 256:QG + (h + 1) * 256], 16, 0, 256), (w_in[:, KG + h * 256:KG + (h + 1) * 256], 16, 256, 256)])
            ws2, wt2 = wload([(w_in[:, VG + h * 512:VG + (h + 1) * 512], 16, 0, 512)])
            ws3, wt3 = wload([(w_in[:, RG + h * 512:RG + (h + 1) * 512], 16, 0, 512)])

            def pjbank():
                bk = mmb[2 + pj_cnt[0] % 2]
                pj_cnt[0] += 1
                return bk

            def feat_item(cc):
                def f():
                    bank = pjbank()
                    for kc in range(16):
                        P.pe(lambda e, kc=kc: e.matmul(bank[:, 0:NT], lhsT=ws1[:, kc, cc * 128:(cc + 1) * 128], rhs=hT[:, kc, 0:NT], start=(kc == 0), stop=(kc == 15)), r=wt1 + hT_all, w=CT(bank))
                    dst = qgT[hp] if cc < 2 else kgT[hp]
                    if cc % 2 == 0:
                        P.act(lambda e: e.activation(out=dst[:, cc % 2, :], in_=bank[:, 0:NT], func=ACT.Copy), r=CT(bank), w=CT(dst))
                    else:
                        P.dve(lambda e: e.tensor_copy(out=dst[:, cc % 2, :], in_=bank[:, 0:NT]), r=CT(bank), w=CT(dst))
                return f

            def tok_item(which, t):
                def f():
                    c0, PT = tiles[t]
                    bank = pjbank()
                    ws_, wt_, wc0, wC = {"k": (ws1, wt1, 256, 256), "v": (ws2, wt2, 0, 512), "r": (ws3, wt3, 0, 512)}[which]
                    for kc in range(16):
                        P.pe(lambda e, kc=kc: e.matmul(bank[:PT, 0:wC], lhsT=hT[:, kc, c0:c0 + PT], rhs=ws_[:, kc, wc0:wc0 + wC], start=(kc == 0), stop=(kc == 15)), r=wt_ + hT_all, w=CT(bank))
                    if which == "k":
                        P.dve(lambda e: e.tensor_copy(out=kg[hp][:PT, t, :], in_=bank[:PT, 0:256]), r=CT(bank), w=CT(kg[hp]))
                    elif which == "v":
                        P.act(lambda e: e.activation(out=vg[hp][:PT, t, :], in_=bank[:PT, :], func=ACT.Copy), r=CT(bank), w=CT(vg[hp]))
                    else:
                        P.act(lambda e: e.activation(out=srg[hp][:PT, t, :], in_=bank[:PT, :], func=ACT.Silu), r=CT(bank), w=CT(srg[hp]))
                return f
            for cc in range(4):
                items.append(feat_item(cc))
            for which in ("k", "v", "r"):
                for t in range(len(tiles)):
                    items.append(tok_item(which, t))
            return items

        pending = make_proj_items(0)
        for h in range(4):
            hp = h % 2
            for it in pending:
                it()
            pending = make_proj_items(h + 1) if h < 3 else []

            def PJ():
                if pending:
                    pending.pop(0)()
            nt_ = len(tiles)
            st_ = {}

            def A1(i):
                c0, PT = tiles[i]
                p = i % 2
                bx = nxt("ax")
                P.pe(lambda e: e.matmul(bx[:PT, 0:256], lhsT=agT[:, c0:c0 + PT], rhs=wa2[:, h * 256:(h + 1) * 256], start=True, stop=True), r=CT(agT) + CT(wa2), w=CT(bx))
                P.act(lambda e: e.activation(out=e1[:PT, :], in_=bx[:PT, 0:256], func=ACT.Exp, scale=-1.0), r=CT(bx), w=CT(e1))
                P.act(lambda e: e.activation(out=sp[:PT, :], in_=e1[:PT, :], func=ACT.Ln, bias=1.0), r=CT(e1), w=CT(sp))

            def A2(i):
                c0, PT = tiles[i]
                p = i % 2
                bb = nxt("ax")
                for kc in range(2):
                    P.pe(lambda e, kc=kc: e.matmul(bb[:, kc * PT:(kc + 1) * PT], lhsT=sp[:PT, kc * 128:(kc + 1) * 128], rhs=trin, start=True, stop=True), r=CT(sp) + CT(tri), w=CT(bb))
                P.pe(lambda e: e.matmul(bb[:PT, 256:512], lhsT=uu, rhs=sp[:PT, :], start=True, stop=True), r=CT(sp) + CT(tri), w=CT(bb))
                bbT = bb[:, 0:2 * PT].rearrange("p (k l) -> p k l", l=PT)
                ep = epos[p]
                P.act(lambda e: e.activation(out=ep[:, :, :PT], in_=bbT, func=ACT.Exp), r=CT(bb), w=CT(ep))
                P.act(lambda e: e.activation(out=eneg[:, :, :PT], in_=bbT, func=ACT.Exp, scale=-1.0), r=CT(bb), w=CT(eneg))
                P.act(lambda e: e.activation(out=ed[:PT, :], in_=bb[:PT, 256:512], func=ACT.Exp), r=CT(bb), w=CT(ed))

            def A3(i):
                c0, PT = tiles[i]
                p = i % 2
                ep, qi, ki, kd_, AT_ = epos[p], qiT[p], kiT[p], kd[p], ATm[p]
                P.dve(lambda e: e.scalar_tensor_tensor(out=qi[:, :, :PT], in0=qgT[hp][:, :, c0:c0 + PT], scalar=1.0 / 16.0, in1=ep[:, :, :PT], op0=ALU.mult, op1=ALU.mult), r=CT(qgT[hp]) + CT(ep), w=CT(qi))
                P.dve(lambda e: e.tensor_tensor(out=ki[:, :, :PT], in0=kgT[hp][:, :, c0:c0 + PT], in1=eneg[:, :, :PT], op=ALU.mult), r=CT(kgT[hp]) + CT(eneg), w=CT(ki))
                P.dve(lambda e: e.tensor_tensor(out=kd_[:PT, :], in0=kg[hp][:PT, i, :], in1=ed[:PT, :], op=ALU.mult), r=CT(kg[hp]) + CT(ed), w=CT(kd_))
                ba = nxt("ax")
                for kc in range(2):
                    P.pe(lambda e, kc=kc: e.matmul(ba[:PT, 0:PT], lhsT=ki[:, kc, :PT], rhs=qi[:, kc, :PT], start=(kc == 0), stop=(kc == 1)), r=CT(ki) + CT(qi), w=CT(ba))
                P.dve(lambda e: e.tensor_tensor(out=AT_[:PT, :PT], in0=ba[:PT, 0:PT], in1=caus, op=ALU.mult), r=CT(ba) + CT(tri), w=CT(AT_))

            def B1(t):
                c0, PT = tiles[t]
                gt = bi * 4 + t
                p = t % 2
                ep, qi, kd_, AT_ = epos[p], qiT[p], kd[p], ATm[p]
                po = mmb[0]
                st_[t] = po
                if not sample:
                    has_state = gt > 0
                    Sb = Sbf[t % 2]
                    if has_state:
                        P.act(lambda e: e.activation(out=Sb[:, 0, :], in_=S[:, 2 * h, :], func=ACT.Copy), r=[S.tok(2 * h)], w=[Sb.tok(0)])
                        P.dve(lambda e: e.tensor_copy(out=Sb[:, 1, :], in_=S[:, 2 * h + 1, :]), r=[S.tok(2 * h + 1)], w=[Sb.tok(1)])
                    P.pe(lambda e: e.matmul(po[:PT, :], lhsT=AT_[:PT, :PT], rhs=vg[hp][:PT, t, :], start=True, stop=not has_state), r=CT(AT_) + CT(vg[hp]), w=CT(po))
                    if has_state:
                        for kc in range(2):
                            P.pe(lambda e, kc=kc: e.matmul(po[:PT, :], lhsT=qi[:, kc, :PT], rhs=Sb[:, kc, :], start=False, stop=(kc == 1)), r=CT(qi) + [Sb.tok(kc)], w=CT(po))
                else:
                    P.pe(lambda e: e.matmul(po[:PT, :], lhsT=AT_[:PT, :PT], rhs=vg[hp][:PT, t, :], start=True, stop=False), r=CT(AT_) + CT(vg[hp]), w=CT(po))
                    for b in range(16):
                        P.dve(lambda e, b=b: e.tensor_tensor(out=qm[:, :, :], in0=qi[:, :, :64], in1=cm[:, b, :].unsqueeze(1).to_broadcast([128, 2, 64]), op=ALU.mult), r=CT(qi) + CT(cm), w=CT(qm))
                        P.dve(lambda e, b=b: e.tensor_scalar(out=kdm[:, :], in0=kd_[:64, :], scalar1=rm[:64, b:b + 1], scalar2=None, op0=ALU.mult), r=CT(kd_) + CT(rm), w=CT(kdm))
                        s0, s0b, sn = S0[b % 3], S0b[b % 2], Sn[b % 2]
                        P.dma(s0[:], st[b, h].rearrange("(kc p) v -> p kc v", p=128), w=CT(s0), q="pool")
                        P.act(lambda e, s0=s0, s0b=s0b: e.activation(out=s0b[:, 0, :], in_=s0[:, 0, :], func=ACT.Copy), r=CT(s0), w=[s0b.tok(0)])
                        P.dve(lambda e, s0=s0, s0b=s0b: e.tensor_copy(out=s0b[:, 1, :], in_=s0[:, 1, :]), r=CT(s0), w=[s0b.tok(1)])
                        for kc in range(2):
                            i = b * 2 + kc
                            P.pe(lambda e, kc=kc, s0b=s0b, i=i: e.matmul(po[:64, :], lhsT=qm[:, kc, :], rhs=s0b[:, kc, :], start=False, stop=(i == 31)), r=CT(qm) + [s0b.tok(kc)], w=CT(po))
                            pd_ = nxt("ax")
                            P.pe(lambda e, pd_=pd_, kc=kc: e.matmul(pd_[:, :], lhsT=kdm[:, kc * 128:(kc + 1) * 128], rhs=vg[hp][:64, t, :], start=True, stop=True), r=CT(kdm) + CT(vg[hp]), w=CT(pd_))
                            P.dve(lambda e, pd_=pd_, kc=kc, s0=s0, sn=sn, b=b: e.scalar_tensor_tensor(out=sn[:, kc, :], in0=s0[:, kc, :], scalar=ep[:, kc, 4 * b + 3:4 * b + 4], in1=pd_[:, :], op0=ALU.mult, op1=ALU.add),
                                  r=CT(pd_) + CT(ep) + CT(s0), w=[sn.tok(kc)])
                        P.dma(sso[b, h].rearrange("(kc p) v -> p kc v", p=128), sn[:], r=[sn.tok(0), sn.tok(1)])

            def B2(t, kcs=(0, 1), norm=True):
                c0, PT = tiles[t]
                gt = bi * 4 + t
                p = t % 2
                ep, kd_ = epos[p], kd[p]
                po = st_[t]
                if not sample:
                    has_state = gt > 0
                    for kc in kcs:
                        pd_ = mmb[1]
                        P.pe(lambda e, pd_=pd_, kc=kc: e.matmul(pd_[:, :], lhsT=kd_[:PT, kc * 128:(kc + 1) * 128], rhs=vg[hp][:PT, t, :], start=True, stop=True), r=CT(kd_) + CT(vg[hp]), w=CT(pd_))
                        if has_state:
                            P.dve(lambda e, pd_=pd_, kc=kc: e.scalar_tensor_tensor(out=S[:, 2 * h + kc, :], in0=S[:, 2 * h + kc, :], scalar=ep[:, kc, PT - 1:PT], in1=pd_[:, :], op0=ALU.mult, op1=ALU.add),
                                  r=CT(pd_) + CT(ep) + [S.tok(2 * h + kc)], w=[S.tok(2 * h + kc)])
                        else:
                            P.dve(lambda e, pd_=pd_, kc=kc: e.tensor_copy(out=S[:, 2 * h + kc, :], in_=pd_[:, :]), r=CT(pd_), w=[S.tok(2 * h + kc)])
                        if gt == 15:
                            P.dma(spo[h, kc * 128:(kc + 1) * 128, :], S[:, 2 * h + kc, :], r=[S.tok(2 * h + kc)])
                if norm:
                    rms_rstd(po[:PT, :], CT(po), PT, 1, 1.0 / 512.0, gjunk[:PT, :], CT(gjunk))

            def B3(t):
                c0, PT = tiles[t]
                po = st_[t]
                P.dve(lambda e: e.scalar_tensor_tensor(out=ogf[:PT, :], in0=po[:PT, :], scalar=rstd[:PT, 1:2], in1=gnb[:PT, :], op0=ALU.mult, op1=ALU.mult), r=CT(po) + [rstd.tok(1)] + CT(gnb), w=CT(ogf))
                P.dve(lambda e: e.tensor_tensor(out=ogb[:PT, :], in0=ogf[:PT, :], in1=srg[hp][:PT, t, :], op=ALU.mult), r=CT(ogf) + CT(srg[hp]), w=CT(ogb))
                bank = nxt("tp")
                for j in range(4):
                    P.pe(lambda e, j=j: e.transpose(out=bank[:, j * PT:(j + 1) * PT], in_=ogb[:PT, j * 128:(j + 1) * 128], identity=ident[:PT, :PT]), r=CT(ogb) + CT(ident), w=CT(bank))
                P.act(lambda e: e.activation(out=ogT[:, 4 * h:4 * h + 4, c0:c0 + PT], in_=bank[:, 0:4 * PT].rearrange("p (j q) -> p j q", q=PT), func=ACT.Copy), r=CT(bank), w=[ogT.tok(h)])

            for step in range(nt_ + 1):
                i, j = step, step - 1
                if i < nt_:
                    A1(i)
                PJ()
                if j >= 0:
                    B1(j)
                if i < nt_:
                    A2(i)
                PJ()
                if j >= 0:
                    B2(j, kcs=(0,), norm=True)
                if i < nt_:
                    A3(i)
                if j >= 0:
                    B2(j, kcs=(1,), norm=False)
                PJ()
                if j >= 0:
                    B3(j)
                PJ()
        T.free()

        mark(f'b{bi} ph3 end')
        XS = None
        if sample:
            XS = Group(P, B0 + 56 * 1024)
            ring.extend(XS.a("wsx", [128, 16, 512], BF16) for _ in range(3))
        G4 = Group(P, R1.off)
        yT = G4.a("yT", [128, 16, NT], BF16)
        sg = G4.a("sg", [128, 4, 2, NT], F32)
        y1 = G4.a("y1", [128, NT], F32)
        oTs_all = [oTs.tok(g) for g in range(4)]; ogT_all = [ogT.tok(h) for h in range(4)]
        for c4 in range(4):
            for c2 in range(2):
                fc = c4 * 4 + c2 * 2
                wsG, wtG = wload([(w_in[:, GS + fc * 128:GS + fc * 128 + 256], 16, 0, 256), (w_in[:, GG + fc * 128:GG + fc * 128 + 256], 16, 256, 256)])
                for j in range(2):
                    for gi in range(2):
                        bg = nxt("ax")
                        cc = c2 * 2 + j
                        for kc in range(16):
                            P.pe(lambda e, kc=kc, bg=bg, gi=gi, wsG=wsG, j=j: e.matmul(bg[:, 0:NT], lhsT=wsG[:, kc, gi * 256 + j * 128:gi * 256 + j * 128 + 128], rhs=hT[:, kc, 0:NT], start=(kc == 0), stop=(kc == 15)),
                                 r=wtG + hT_all, w=CT(bg))
                        P.act(lambda e, bg=bg, gi=gi, cc=cc: e.activation(out=sg[:, cc, gi, :], in_=bg[:, 0:NT], func=ACT.Sigmoid), r=CT(bg), w=[sg.tok(cc, gi)])
            wsA, wtA = wload([(p_swa[:, c4 * 512:(c4 + 1) * 512], 8, 0, 512)])
            wsB, wtB = wload([(p_gla[:, c4 * 512:(c4 + 1) * 512], 16, 0, 512)])
            for cc in range(4):
                fc = c4 * 4 + cc
                pa = nxt("mm")
                for kc in range(8):
                    P.pe(lambda e, kc=kc, pa=pa, wsA=wsA, cc=cc: e.matmul(pa[:, 0:NT], lhsT=wsA[:, kc, cc * 128:(cc + 1) * 128], rhs=oTs[:, kc, :], start=(kc == 0), stop=(kc == 7)), r=wtA + oTs_all, w=CT(pa))
                pb = nxt("mm")
                for kc in range(16):
                    P.pe(lambda e, kc=kc, pb=pb, wsB=wsB, cc=cc: e.matmul(pb[:, 0:NT], lhsT=wsB[:, kc, cc * 128:(cc + 1) * 128], rhs=ogT[:, kc, :], start=(kc == 0), stop=(kc == 15)), r=wtB + ogT_all, w=CT(pb))
                P.dve(lambda e, pa=pa, cc=cc: e.tensor_tensor(out=y1[:, :], in0=pa[:, 0:NT], in1=sg[:, cc, 0, :], op=ALU.mult), r=CT(pa) + [sg.tok(cc, 0)], w=CT(y1))
                P.dve(lambda e, pb=pb, cc=cc: e.tensor_tensor(out=sg[:, cc, 1, :], in0=pb[:, 0:NT], in1=sg[:, cc, 1, :], op=ALU.mult), r=CT(pb) + [sg.tok(cc, 1)], w=[sg.tok(cc, 1)])
                P.dve(lambda e, fc=fc, cc=cc: e.tensor_tensor(out=yT[:, fc, :], in0=y1[:, :], in1=sg[:, cc, 1, :], op=ALU.add), r=CT(y1) + [sg.tok(cc, 1)], w=[yT.tok(fc)])
        yT_all = [yT.tok(fc) for fc in range(16)]
        R1.free()

        mark(f'b{bi} ph4 end')
        X = Group(P, B0 + (40 if sample else 72) * 1024)
        x2 = X.a("x2", [128, len(tiles), D], F32)
        G5 = Group(P, G4.off)
        hb = G5.a("hb2", [128, D], BF16); junk = G5.a("junk2", [128, D], F32)
        for t, (c0, PT) in enumerate(tiles):
            P.dma(x2[:PT, t, :], xsrc[c0:c0 + PT, :], w=[x2.tok(t)])
        for c4 in range(4):
            ws, wt = wload([(w_o[:, c4 * 512:(c4 + 1) * 512], 16, 0, 512)])

            def wo_sink(t, bank, c4=c4):
                PT = tiles[t][1]
                P.dve(lambda e: e.tensor_tensor(out=x2[:PT, t, c4 * 512:(c4 + 1) * 512], in0=bank[:PT, :], in1=x2[:PT, t, c4 * 512:(c4 + 1) * 512], op=ALU.add), r=CT(bank) + [x2.tok(t)], w=[x2.tok(t)])
            proj_tok(ws, wt, 16, 0, 512, yT, yT_all, tiles, wo_sink)
        for t, (c0, PT) in enumerate(tiles):
            norm_T(x2[:PT, t, :], [x2.tok(t)], PT, n2t, hT, c0, hb, junk)
        G4.free(); G5.free()

        mark(f'b{bi} ph5 end')
        G6 = Group(P, B0)
        aT = G6.a("aT", [128, 64, NT], BF16)
        rl = [G6.a("rl", [128, NT], F32) for _ in range(2)]
        for g16 in range(16):
            ws, wt = wload([(w_up[:, g16 * 512:(g16 + 1) * 512], 16, 0, 512)])

            def up_sink(cc, bank, g16=g16):
                fcc = g16 * 4 + cc
                r_ = rl[fcc % 2]
                P.act(lambda e: e.activation(out=r_[:, :], in_=bank[:, 0:NT], func=ACT.Relu), r=CT(bank), w=CT(r_))
                P.dve(lambda e: e.tensor_tensor(out=aT[:, fcc, :], in0=r_[:, :], in1=r_[:, :], op=ALU.mult), r=CT(r_), w=[aT.tok(fcc)])
            proj_feat(ws, wt, 16, 0, 4, hT, hT_all, NT, up_sink)
        for c4 in range(4):
            banks = [nxt("mm") for _ in tiles]
            for g4 in range(4):
                ws, wt = wload([(w_down[g4 * 2048:(g4 + 1) * 2048, c4 * 512:(c4 + 1) * 512], 16, 0, 512)])
                for t, (c0, PT) in enumerate(tiles):
                    for fc in range(16):
                        P.pe(lambda e, t=t, fc=fc, c0=c0, PT=PT, ws=ws, g4=g4: e.matmul(banks[t][:PT, :], lhsT=aT[:, g4 * 16 + fc, c0:c0 + PT], rhs=ws[:, fc, :], start=(g4 == 0 and fc == 0), stop=(g4 == 3 and fc == 15)),
                             r=wt + [aT.tok(g4 * 16 + fc)], w=CT(banks[t]))
            for t, (c0, PT) in enumerate(tiles):
                P.dve(lambda e, t=t, PT=PT, c4=c4: e.tensor_tensor(out=x2[:PT, t, c4 * 512:(c4 + 1) * 512], in0=banks[t][:PT, :], in1=x2[:PT, t, c4 * 512:(c4 + 1) * 512], op=ALU.add), r=CT(banks[t]) + [x2.tok(t)], w=[x2.tok(t)])
        G6.free()

        mark(f'b{bi} ph6 end')
        nxt_bi = BLOCKS[BLOCKS.index(bi) + 1] if BLOCKS.index(bi) + 1 < len(BLOCKS) else None
        if nxt_bi is not None:
            phase1(nxt_bi)
        G7 = Group(P, B0 + (24 if sample else 56) * 1024)
        fnb = G7.a("fnb", [128, D], F32)
        yo_ = G7.a("yo", [128, D], F32)
        P.dma(fnb[:], fnw.broadcast_to([128, D]), w=CT(fnb))
        for t, (c0, PT) in enumerate(tiles):
            rms_rstd(x2[:PT, t, :], [x2.tok(t)], PT, 2, 1.0 / D, yo_[:PT, :], CT(yo_))
            P.dve(lambda e, t=t, PT=PT: e.scalar_tensor_tensor(out=yo_[:PT, :], in0=x2[:PT, t, :], scalar=rstd[:PT, 2:3], in1=fnb[:PT, :], op0=ALU.mult, op1=ALU.mult), r=[x2.tok(t), rstd.tok(2)] + CT(fnb), w=CT(yo_))
            P.dma(ydst[c0:c0 + PT, :], yo_[:PT, :], r=CT(yo_))
        G7.free(); X.free()
        if XS is not None:
            del ring[NS:]
            XS.free()
        mark(f'b{bi} ph7 end')

    try:
        for bi in BLOCKS:
            if block(bi):
                break
    except StopBuild:
        pass
    P.emit()
    return nc, P


def _consts():
    import ml_dtypes
    bf = ml_dtypes.bfloat16
    c = {}
    c["c_ident"] = np.eye(128, dtype=np.float32).astype(bf)
    s_ = np.arange(128)[:, None]; q_ = np.arange(128)[None, :]
    mprev = np.where(s_ > q_, 0.0, NEG).astype(np.float32)
    mdiag = np.where(s_ <= q_, 0.0, NEG).astype(np.float32)
    c["c_mprev"] = np.tile(mprev, (1, 4)).astype(bf)
    c["c_mdiag"] = np.tile(mdiag, (1, 4)).astype(bf)
    tok = np.arange(64)
    msc = np.where(np.arange(128)[:, None] > (tok % 4)[None, :], 0.0, NEG).astype(np.float32)
    c["c_msc"] = np.tile(msc, (1, 4)).astype(bf)
    j_ = np.arange(64)[:, None]; i_ = tok[None, :]
    msn = np.where((j_ // 4 == i_ // 4) & (j_ % 4 <= i_ % 4), 0.0, NEG).astype(np.float32)
    msn_full = np.full((128, 256), NEG, np.float32); msn_full[:64] = np.tile(msn, (1, 4))
    c["c_msn"] = msn_full.astype(bf)
    tri = np.zeros((2, 3, 128, 128), np.float32)
    m_ = np.arange(128)[:, None]; l_ = np.arange(128)[None, :]
    tri[0, 0] = np.where(m_ <= l_, -1.0 / 16.0, 0.0); tri[0, 1] = np.where(m_ > l_, -1.0 / 16.0, 0.0); tri[0, 2] = np.where(m_ <= l_, 1.0, 0.0)
    same = (m_ // 4 == l_ // 4)
    tri[1, 0] = np.where(same & (m_ <= l_), -1.0 / 16.0, 0.0); tri[1, 1] = np.where(same & (m_ > l_), -1.0 / 16.0, 0.0); tri[1, 2] = np.where(same & (m_ <= l_), 1.0, 0.0)
    c["c_tri"] = tri
    inv = (np.float32(500000.0) ** (-np.arange(8, dtype=np.float32) * np.float32(2.0) / np.float32(16.0))).astype(np.float32)
    pos = (np.arange(16)[None, :] * 128 + np.arange(128)[:, None]).astype(np.float32)
    ang = (pos[:, :, None] * inv[None, None, :]).astype(np.float32)
    c["c_ropep"] = np.stack([np.cos(ang), np.sin(ang)], axis=1).astype(np.float32)
    poss = (PAST + (np.arange(128) % 4)).astype(np.float32)
    angs = (poss[:, None] * inv[None, :]).astype(np.float32)
    c["c_ropes"] = np.stack([np.cos(angs), np.sin(angs)], axis=1).astype(np.float32)
    cmk = (np.arange(64)[None, :] // 4 == np.arange(16)[:, None]).astype(np.float32)
    c["c_cm"] = np.broadcast_to(cmk[None], (128, 16, 64)).astype(bf)
    rmk = np.zeros((128, 16), np.float32); rmk[:64] = cmk.T
    c["c_rm"] = rmk
    ones = np.zeros((128, 2, 128), np.float32); ones[:, 0, :64] = 1.0; ones[:, 1, 64:] = 1.0
    c["c_ones"] = ones.astype(bf)
    return {k: np.ascontiguousarray(v) for k, v in c.items()}


_CACHE = {}


def kernel(x_prompt, x_sample, cache_swa_k, cache_swa_v, state_gla, norm1, w_in, w_a2, b_a,
           sink, gla_norm, p_swa, p_gla, w_o, norm2, w_up, w_down, final_norm):
    f = lambda a: np.ascontiguousarray(np.asarray(a, dtype=np.float32))
    x_prompt, x_sample = f(x_prompt), f(x_sample)
    ck, cv, stt = f(cache_swa_k)[0], f(cache_swa_v)[0], f(state_gla)[0]
    if "nc" not in _CACHE:
        _CACHE["nc"] = build_program()[0]
    nc = _CACHE["nc"]
    sk = f(sink)[0]
    sinkl = np.empty((128, 8), np.float32)
    sinkl[:64, :] = sk[0::2][None, :]; sinkl[64:, :] = sk[1::2][None, :]
    shared = dict(
        w_in=f(w_in)[0], w_a2=f(w_a2)[0], b_a=f(b_a), sinkl=sinkl, gnorm=f(gla_norm),
        p_swa=f(p_swa)[0], p_gla=f(p_gla)[0], w_o=f(w_o)[0],
        n1=np.ascontiguousarray(f(norm1)[0].reshape(16, 128).T), n2=np.ascontiguousarray(f(norm2)[0].reshape(16, 128).T),
        fnw=f(final_norm).reshape(1, D), w_up=f(w_up)[0], w_down=f(w_down)[0],
    )
    shared.update(_consts())
    in_maps = []
    for c in range(8):
        m = dict(shared)
        m["xp"] = x_prompt[c]
        m["xs"] = np.ascontiguousarray(x_sample[c * 16:(c + 1) * 16].reshape(64, D))
        m["ck"] = np.ascontiguousarray(ck[c * 16:(c + 1) * 16].reshape(16, 128, 256))
        m["cv"] = np.ascontiguousarray(cv[c * 16:(c + 1) * 16].reshape(16, 128, 256))
        m["st"] = np.ascontiguousarray(stt[c * 16:(c + 1) * 16])
        in_maps.append(m)
    res = run_bass_kernel_spmd(nc, in_maps, core_ids=list(range(8)))
    R = res.results
    y_prompt = np.stack([R[c]["yp"] for c in range(8)]).astype(np.float32)
    y_sample = np.concatenate([R[c]["ys"].reshape(16, 4, D) for c in range(8)]).astype(np.float32)
    kp = np.stack([R[c]["kpo"].reshape(128, 4, 64) for c in range(8)])[None].astype(np.float32)
    vp = np.stack([R[c]["vpo"].reshape(128, 4, 64) for c in range(8)])[None].astype(np.float32)
    sp_ = np.stack([R[c]["spo"] for c in range(8)])[None].astype(np.float32)
    ks = np.concatenate([R[c]["kso"].reshape(16, 128, 4, 64) for c in range(8)])[None].astype(np.float32)
    vs = np.concatenate([R[c]["vso"].reshape(16, 128, 4, 64) for c in range(8)])[None].astype(np.float32)
    ss = np.concatenate([R[c]["sso"] for c in range(8)])[None].astype(np.float32)
    return (y_prompt, y_sample, kp, vp, sp_, ks, vs, ss)
```
